# Optimizing a Trainium2 kernel written in Bass

```python
import jax, jax.numpy as jnp
from jax import lax
import numpy as np

D_MODEL = 1024
BATCH = 2
SEQ = 8192
DEPTH = 1

MEM_LEN = 256
EPS = 1e-6

M_HEADS = 4
M_DV = D_MODEL // 8
M_DQK = M_DV // 2
M_CONV = 4
M_CHUNK = 64

A_HEADS = 8
A_KV_HEADS = 2
A_DH = 64
WINDOW = 128
A_BLOCK = 128

X_HEADS = 4
X_DH = D_MODEL // X_HEADS

P_HEADS = 8
P_NKEYS = 128
P_EXPERTS = P_NKEYS * P_NKEYS
P_KEY_DIM = 128
P_TOPK = 16
P_TOKEN_BLOCK = 128

IN_SIZES = (M_HEADS * M_DQK, M_HEADS * M_DQK, M_HEADS * M_DV, M_HEADS * M_DV, M_HEADS, M_HEADS,
            A_HEADS * A_DH, A_KV_HEADS * A_DH, A_KV_HEADS * A_DH)
P_IN = sum(IN_SIZES)
MIX_WIDTH = M_HEADS * M_DV + A_HEADS * A_DH

kernel_name = "hybrid_mlstm_swa_sinks_peer_layer"


def rmsnorm(x, g):
    xf = x.astype(jnp.float32)
    y = xf * lax.rsqrt(jnp.mean(xf * xf, axis=-1, keepdims=True) + EPS)
    return (y * g.astype(jnp.float32)).astype(x.dtype)


def causal_depthwise_conv(x, w, b):
    y = lax.conv_general_dilated(
        x, w[:, None, :].astype(x.dtype), window_strides=(1,),
        padding=[(w.shape[0] - 1, 0)], dimension_numbers=('NWC', 'WIO', 'NWC'),
        feature_group_count=x.shape[-1])
    return y + b.astype(x.dtype)


def mlstm_chunkwise(q, k, v, i_pre, f_pre):
    B, S, H, _ = q.shape
    L = M_CHUNK
    nc = S // L
    f32 = jnp.float32

    def chunks(t):
        t = t.astype(f32).reshape((B, nc, L, H) + t.shape[3:])
        return jnp.moveaxis(t, (1, 3), (0, 2))

    qc = chunks(q) * (M_DQK ** -0.5)
    kc = chunks(k)
    vc = chunks(v)
    ic = chunks(i_pre)
    lfc = chunks(jax.nn.log_sigmoid(f_pre.astype(f32)))
    causal = jnp.tril(jnp.ones((L, L), dtype=bool))

    def step(carry, inp):
        C, n, m = carry
        qb, kb, vb, ib, lfb = inp
        b = jnp.cumsum(lfb, axis=-1)
        log_d = jnp.where(causal, b[..., :, None] - b[..., None, :] + ib[..., None, :], -jnp.inf)
        inter = b + m[..., None]
        m_t = jnp.maximum(inter, jnp.max(log_d, axis=-1))
        d = jnp.exp(log_d - m_t[..., None])
        s = jnp.einsum('bhtk,bhsk->bhts', qb, kb) * d
        w_inter = jnp.exp(inter - m_t)
        num = jnp.einsum('bhts,bhsv->bhtv', s, vb) + w_inter[..., None] * jnp.einsum('bhvk,bhtk->bhtv', C, qb)
        den = jnp.sum(s, axis=-1) + w_inter * jnp.einsum('bhk,bhtk->bht', n, qb)
        h = num / jnp.maximum(jnp.abs(den), jnp.exp(-m_t))[..., None]
        m_new = m_t[..., -1]
        w = jnp.exp(b[..., -1:] - b + ib - m_new[..., None])
        decay = jnp.exp(b[..., -1] + m - m_new)
        C_new = decay[..., None, None] * C + jnp.einsum('bhs,bhsv,bhsk->bhvk', w, vb, kb)
        n_new = decay[..., None] * n + jnp.einsum('bhs,bhsk->bhk', w, kb)
        return (C_new, n_new, m_new), h

    init = (jnp.zeros((B, H, M_DV, M_DQK), f32), jnp.zeros((B, H, M_DQK), f32), jnp.zeros((B, H), f32))
    _, h = lax.scan(step, init, (qc, kc, vc, ic, lfc))
    h = jnp.moveaxis(h, (0, 2), (1, 3)).reshape(B, S, H, M_DV)
    return h.astype(v.dtype)


def sliding_window_gqa_sinks(q, k, v, sinks):
    B, S, Hq, dh = q.shape
    Hkv = k.shape[2]
    G = Hq // Hkv
    nb = S // A_BLOCK
    f32 = jnp.float32
    qb = q.astype(f32).reshape(B, nb, A_BLOCK, Hkv, G, dh)

    def band_blocks(t):
        t = t.astype(f32)
        prev = jnp.concatenate([jnp.zeros_like(t[:, :A_BLOCK]), t[:, :S - A_BLOCK]], axis=1)
        return jnp.concatenate([prev.reshape(B, nb, A_BLOCK, Hkv, dh),
                                t.reshape(B, nb, A_BLOCK, Hkv, dh)], axis=2)

    kb = band_blocks(k)
    vb = band_blocks(v)
    qi = jnp.arange(A_BLOCK)[:, None]
    kj = jnp.arange(2 * A_BLOCK)[None, :]
    diff = qi - kj + A_BLOCK
    band = (diff >= 0) & (diff < WINDOW)
    exists = (jnp.arange(nb)[:, None, None] > 0) | (kj >= A_BLOCK)[None]
    mask = band[None] & exists

    scores = jnp.einsum('bnqhgd,bnkhd->bnhgqk', qb, kb) * (dh ** -0.5)
    scores = jnp.where(mask[None, :, None, None], scores, -jnp.inf)
    sink = sinks.astype(f32).reshape(Hkv, G)[None, None, :, :, None, None]
    mx = jnp.maximum(jnp.max(scores, axis=-1, keepdims=True), sink)
    p = jnp.exp(scores - mx)
    probs = p / (jnp.sum(p, axis=-1, keepdims=True) + jnp.exp(sink - mx))
    out = jnp.einsum('bnhgqk,bnkhd->bnqhgd', probs, vb)
    return out.reshape(B, S, Hq * dh).astype(q.dtype)


def memory_cross_attention(xn, memn, w_q, w_k, w_v, w_o):
    B, S, D = xn.shape
    M = memn.shape[1]
    f32 = jnp.float32
    q = (xn @ w_q).reshape(B, S, X_HEADS, X_DH).astype(f32)
    k = (memn @ w_k).reshape(B, M, X_HEADS, X_DH).astype(f32)
    v = (memn @ w_v).reshape(B, M, X_HEADS, X_DH).astype(f32)
    p = jax.nn.softmax(jnp.einsum('bshd,bmhd->bhsm', q, k) * (X_DH ** -0.5), axis=-1)
    o = jnp.einsum('bhsm,bmhd->bshd', p, v).reshape(B, S, X_HEADS * X_DH).astype(xn.dtype)
    return o @ w_o


def peer_ffn(xn, w_pq, sub_keys1, sub_keys2, expert_down, expert_up):
    B, S, D = xn.shape
    T = P_TOKEN_BLOCK
    xt = xn.reshape(-1, T, D)

    def block(xb):
        qh = (xb @ w_pq).reshape(T, P_HEADS, 2, P_KEY_DIM).astype(jnp.float32)
        s1 = jnp.einsum('thd,nd->thn', qh[:, :, 0], sub_keys1.astype(jnp.float32))
        s2 = jnp.einsum('thd,nd->thn', qh[:, :, 1], sub_keys2.astype(jnp.float32))
        v1, i1 = lax.top_k(s1, P_TOPK)
        v2, i2 = lax.top_k(s2, P_TOPK)
        cand_s = (v1[..., :, None] + v2[..., None, :]).reshape(T, P_HEADS, P_TOPK * P_TOPK)
        cand_i = (i1[..., :, None] * P_NKEYS + i2[..., None, :]).reshape(T, P_HEADS, P_TOPK * P_TOPK)
        top_s, pos = lax.top_k(cand_s, P_TOPK)
        eidx = jnp.take_along_axis(cand_i, pos, axis=-1)
        g = jax.nn.softmax(top_s, axis=-1)
        u = expert_down[eidx]
        a = jax.nn.gelu(jnp.einsum('thkd,td->thk', u, xb), approximate=False)
        coef = (g * a.astype(jnp.float32)).astype(xb.dtype)
        return jnp.einsum('thk,thkd->td', coef, expert_up[eidx])

    return lax.map(block, xt).reshape(B, S, D)


def setup_inputs(seed: int = 0) -> dict:
    key = jax.random.key(seed)
    ks = jax.random.split(key, 24)
    f32 = jnp.float32

    def nrm(k, shape, scale):
        return jax.random.normal(k, shape, f32) * scale

    Dn = D_MODEL ** -0.5
    qk_ch = 2 * M_HEADS * M_DQK
    return {
        "x": nrm(ks[0], (BATCH, SEQ, D_MODEL), 1.0),
        "mem": nrm(ks[1], (BATCH, MEM_LEN, D_MODEL), 1.0),
        "g_mix": 1.0 + nrm(ks[2], (DEPTH, D_MODEL), 0.02),
        "w_in": nrm(ks[3], (DEPTH, D_MODEL, P_IN), Dn),
        "conv_w": nrm(ks[4], (DEPTH, M_CONV, qk_ch), M_CONV ** -0.5),
        "conv_b": nrm(ks[5], (DEPTH, qk_ch), 0.02),
        "b_igate": nrm(ks[6], (DEPTH, M_HEADS), 0.1),
        "b_fgate": jnp.linspace(3.0, 6.0, M_HEADS, dtype=f32)[None] + nrm(ks[7], (DEPTH, M_HEADS), 0.1),
        "g_mhead": 1.0 + nrm(ks[8], (DEPTH, M_HEADS * M_DV), 0.02),
        "sinks": nrm(ks[9], (DEPTH, A_HEADS), 0.5),
        "w_out": nrm(ks[10], (DEPTH, MIX_WIDTH, D_MODEL), MIX_WIDTH ** -0.5),
        "g_cross": 1.0 + nrm(ks[11], (DEPTH, D_MODEL), 0.02),
        "g_mem": 1.0 + nrm(ks[12], (DEPTH, D_MODEL), 0.02),
        "w_xq": nrm(ks[13], (DEPTH, D_MODEL, X_HEADS * X_DH), Dn),
        "w_xk": nrm(ks[14], (DEPTH, D_MODEL, X_HEADS * X_DH), Dn),
        "w_xv": nrm(ks[15], (DEPTH, D_MODEL, X_HEADS * X_DH), Dn),
        "w_xo": nrm(ks[16], (DEPTH, X_HEADS * X_DH, D_MODEL), (X_HEADS * X_DH) ** -0.5),
        "g_ffn": 1.0 + nrm(ks[17], (DEPTH, D_MODEL), 0.02),
        "w_pq": nrm(ks[18], (DEPTH, D_MODEL, P_HEADS * 2 * P_KEY_DIM), Dn),
        "sub_keys1": nrm(ks[19], (DEPTH, P_NKEYS, P_KEY_DIM), P_KEY_DIM ** -0.5),
        "sub_keys2": nrm(ks[20], (DEPTH, P_NKEYS, P_KEY_DIM), P_KEY_DIM ** -0.5),
        "expert_down": nrm(ks[21], (DEPTH, P_EXPERTS, D_MODEL), Dn),
        "expert_up": nrm(ks[22], (DEPTH, P_EXPERTS, D_MODEL), Dn),
        "g_final": 1.0 + nrm(ks[23], (D_MODEL,), 0.02),
    }


def reference(x, mem, g_mix, w_in, conv_w, conv_b, b_igate, b_fgate, g_mhead, sinks, w_out,
              g_cross, g_mem, w_xq, w_xk, w_xv, w_xo, g_ffn, w_pq, sub_keys1, sub_keys2,
              expert_down, expert_up, g_final):
    B, S, _ = x.shape
    split_points = [int(c) for c in np.cumsum(IN_SIZES)[:-1]]
    h = x
    for l in range(DEPTH):
        xn = rmsnorm(h, g_mix[l])
        proj = xn @ w_in[l]
        mq, mk, mv, mo, mi, mf, aq, ak, av = jnp.split(proj, split_points, axis=-1)
        qk = jax.nn.silu(causal_depthwise_conv(jnp.concatenate([mq, mk], axis=-1), conv_w[l], conv_b[l]))
        mq, mk = jnp.split(qk, 2, axis=-1)
        hm = mlstm_chunkwise(mq.reshape(B, S, M_HEADS, M_DQK), mk.reshape(B, S, M_HEADS, M_DQK),
                             mv.reshape(B, S, M_HEADS, M_DV),
                             mi + b_igate[l].astype(mi.dtype), mf + b_fgate[l].astype(mf.dtype))
        hm = rmsnorm(hm, g_mhead[l].reshape(M_HEADS, M_DV)).reshape(B, S, M_HEADS * M_DV)
        hm = (jax.nn.sigmoid(mo) * hm).astype(x.dtype)
        ha = sliding_window_gqa_sinks(aq.reshape(B, S, A_HEADS, A_DH), ak.reshape(B, S, A_KV_HEADS, A_DH),
                                      av.reshape(B, S, A_KV_HEADS, A_DH), sinks[l])
        h = h + jnp.concatenate([hm, ha], axis=-1) @ w_out[l]
        h = h + memory_cross_attention(rmsnorm(h, g_cross[l]), rmsnorm(mem, g_mem[l]),
                                       w_xq[l], w_xk[l], w_xv[l], w_xo[l])
        h = h + peer_ffn(rmsnorm(h, g_ffn[l]), w_pq[l], sub_keys1[l], sub_keys2[l],
                         expert_down[l], expert_up[l])
    return rmsnorm(h, g_final)
```

```python
import numpy as np
from contextlib import ExitStack
import concourse.bass as bass
import concourse.mybir as mybir
from concourse.bass_utils import run_bass_kernel_spmd

F32 = mybir.dt.float32
BF16 = mybir.dt.bfloat16
XB_PASSES = 3
I32 = mybir.dt.int32
U32 = mybir.dt.uint32
AF = mybir.ActivationFunctionType
ALU = mybir.AluOpType
AX = mybir.AxisListType

D = 1024
P_IN = 2312
EPS = 1e-6
NEG = -30000.0
NSEG = 4
import os
SAME_ENGINE_SYNC = bool(int(os.environ.get("KSES", "1")))
DBG = int(os.environ.get('KDBG', '0'))


class Buf:
    __slots__ = ("name", "last_write", "readers", "exclusive")

    def __init__(self, name, exclusive=False):
        self.name = name
        self.last_write = None
        self.readers = {}
        self.exclusive = exclusive


class TB:
    __slots__ = ("t", "b")

    def __init__(self, t, name):
        self.t = t
        self.b = Buf(name)


class _Rec:
    def __getattr__(self, name):
        def call(*a, **k):
            return (name, a, k)
        return call


_REC = _Rec()


class Sync:
    ENGS = ("pe", "act", "dve", "pool", "sp")

    def __init__(self, nc, n_dma_sems=32):
        self.nc = nc
        self.sems = {}
        self.counts = {}
        self.seen = {e: {} for e in self.ENGS}
        self.prog = {e: [] for e in self.ENGS}
        self._cms = []
        for name in self.ENGS:
            self._new_sem(name)
        self.dma_keys = []
        for i in range(n_dma_sems):
            k = "dma%d" % i
            self._new_sem(k)
            self.dma_keys.append(k)
        self.pdma_keys = []
        for i in range(8):
            k = "pdma%d" % i
            self._new_sem(k)
            self.pdma_keys.append(k)
        self.dma_rr = 0
        self.pdma_rr = 0
        self.dma_inflight = {k: None for k in self.dma_keys + self.pdma_keys}

    def _new_sem(self, key):
        cm = self.nc.semaphore("s_" + key)
        h = cm.__enter__()
        self._cms.append(cm)
        self.sems[key] = h
        self.counts[key] = 0

    def close(self):
        for cm in reversed(self._cms):
            cm.__exit__(None, None, None)

    def _need(self, ename, ev):
        if ev is None:
            return
        key, val = ev
        if key == ename and (ename == "pe" or not SAME_ENGINE_SYNC):
            return
        if self.seen[ename].get(key, 0) >= val:
            return
        self.prog[ename].append(("wait", key, val))
        self.seen[ename][key] = val

    def _deps(self, ename, reads, writes):
        for b in reads:
            self._need(ename, b.last_write)
        for b in writes:
            self._need(ename, b.last_write)
            for k, v in list(b.readers.items()):
                self._need(ename, (k, v))

    def op(self, ename, fn, reads=(), writes=()):
        writes = list(writes) + [b for b in reads if b.exclusive and b not in writes]
        self._deps(ename, reads, writes)
        self.counts[ename] += 1
        self.prog[ename].append(("ins", fn(_REC), ename, 1))
        ev = (ename, self.counts[ename])
        for b in reads:
            b.readers[ename] = ev[1]
        for b in writes:
            b.last_write = ev
            b.readers = {}
        return ev

    def dma(self, qname, fn, reads=(), writes=()):
        self._deps(qname, reads, writes)
        if qname == "pool":
            k = self.pdma_keys[self.pdma_rr]
            self.pdma_rr = (self.pdma_rr + 1) % len(self.pdma_keys)
        else:
            k = self.dma_keys[self.dma_rr]
            self.dma_rr = (self.dma_rr + 1) % len(self.dma_keys)
        prev = self.dma_inflight[k]
        if prev is not None:
            self._need(qname, prev)
        self.counts[k] += 16
        self.prog[qname].append(("ins", fn(_REC), k, 16))
        ev = (k, self.counts[k])
        self.dma_inflight[k] = ev
        for b in reads:
            b.readers[k] = ev[1]
        for b in writes:
            b.last_write = ev
            b.readers = {}
        return ev

    def wait_all_dma(self, ename):
        for k in self.dma_keys + self.pdma_keys:
            if self.dma_inflight[k] is not None:
                self._need(ename, self.dma_inflight[k])

    def barrier(self):
        for e in self.ENGS:
            for k in self.sems:
                if self.counts[k] > 0:
                    self._need(e, (k, self.counts[k]))

    def emit(self):
        nc = self.nc
        sems = self.sems
        prog = self.prog

        def replay(eng, items):
            for it in items:
                if it[0] == "wait":
                    eng.wait_ge(sems[it[1]], it[2])
                else:
                    name, a, k = it[1]
                    ins = getattr(eng, name)(*a, **k)
                    ins.then_inc(sems[it[2]], it[3])

        with nc.allow_low_precision("exact multi-term bf16 split of an fp32 operand (fp32-emulating)"), nc.Block() as block:
            @block.tensor
            def _(e):
                replay(e, prog["pe"])

            @block.scalar
            def _(e):
                replay(e, prog["act"])

            @block.vector
            def _(e):
                replay(e, prog["dve"])

            @block.gpsimd
            def _(e):
                replay(e, prog["pool"])

            @block.sync
            def _(e):
                replay(e, prog["sp"])
        self.prog = {e: [] for e in self.ENGS}


def build(NB, stop_after=3, n_tok_peer=128):
    NPRE = (NSEG - 1) * NB
    nc = bass.Bass("TRN2", target_bir_lowering=False)

    def din(name, shape, dt=F32):
        return nc.dram_tensor(name, list(shape), dt, kind="ExternalInput").ap()

    xs_d = din("xs", [NB * 128, D])
    xp_d = din("xp", [NPRE * 128, D])
    pm_d = din("pm", [1, NPRE])
    mem_d = din("mem", [256, D])
    g_mix_d = din("g_mix", [8, 128])
    w_in_d = din("w_in", [D, P_IN])
    conv_w_d = din("conv_w", [4, 512])
    conv_b_d = din("conv_b", [4, 128])
    gb_d = din("gateb", [1, 8])
    g_mhead_d = din("g_mhead", [1, 512])
    sinks_d = din("sinks", [1, 8])
    w_out_d = din("w_out", [D, D])
    g_cross_d = din("g_cross", [8, 128])
    g_mem_d = din("g_mem", [8, 128])
    w_xq_d = din("w_xq", [D, D])
    w_xk_d = din("w_xk", [D, D])
    w_xv_d = din("w_xv", [D, D])
    w_xo_d = din("w_xo", [D, D])
    g_ffn_d = din("g_ffn", [8, 128])
    g_ffn_row_d = din("g_ffn_row", [1, D])
    w_pq_d = din("w_pq", [D, 2048])
    sk1_d = din("sub_keys1", [128, 128])
    sk2_d = din("sub_keys2", [128, 128])
    ed_d = din("expert_down", [16384, D])
    eu_d = din("expert_up", [16384, D])
    g_final_d = din("g_final", [1, D])
    out_d = nc.dram_tensor("out", [NB * 128, D], F32, kind="ExternalOutput").ap()

    S = Sync(nc)
    es_all = ExitStack()

    def alloc(es, name, shape, dt=F32):
        return TB(es.enter_context(nc.sbuf_tensor("sb_" + name, list(shape), dt)), name)

    H = es_all.enter_context(nc.sbuf_tensor("sb_H", [128, NB, D], F32))
    Hb = [Buf("H%d" % i) for i in range(NB)]
    ident = alloc(es_all, "ident", [128, 128])
    Umat = alloc(es_all, "Umat", [128, 128])
    ones = alloc(es_all, "ones", [128, 128])
    mcur = alloc(es_all, "mcur", [128, 128])
    mprev = alloc(es_all, "mprev", [128, 128])
    mprev0 = alloc(es_all, "mprev0", [128, 128])
    ss = alloc(es_all, "ss", [128, 4])
    PSD = [es_all.enter_context(nc.psum_tensor("psd%d" % i, [128, 1024], F32)) for i in range(4)]
    PSb = [Buf("ps%d" % i, exclusive=True) for i in range(8)]
    bank_ptr = [0]
    last_db = [0]
    reserved = [None]

    def bank():
        k = bank_ptr[0] % 8
        if reserved[0] is not None and k // 2 == reserved[0]:
            bank_ptr[0] += 2 - (k % 2)
            k = bank_ptr[0] % 8
        bank_ptr[0] += 1
        return PSD[k // 2][:, (k % 2) * 512:(k % 2) * 512 + 512], PSb[k]

    def dbank():
        if bank_ptr[0] % 2:
            bank_ptr[0] += 1
        k = bank_ptr[0] % 8
        if reserved[0] is not None and k // 2 == reserved[0]:
            bank_ptr[0] += 2
            k = bank_ptr[0] % 8
        bank_ptr[0] += 2
        last_db[0] = k // 2
        return PSD[k // 2], [PSb[k], PSb[k + 1]]

    def dmaq():
        return "sp"

    S.op("pool", lambda e: e.memset(ident.t[:], 0.0), writes=[ident.b])
    S.op("pool", lambda e: e.affine_select(out=ident.t[:], in_=ident.t[:], pattern=[[-1, 128]],
                                           compare_op=ALU.not_equal, fill=1.0, base=0, channel_multiplier=1),
         reads=[ident.b], writes=[ident.b])
    S.op("pool", lambda e: e.memset(ones.t[:], 1.0), writes=[ones.b])
    S.op("pool", lambda e: e.memset(Umat.t[:], 1.0), writes=[Umat.b])
    S.op("pool", lambda e: e.affine_select(out=Umat.t[:], in_=Umat.t[:], pattern=[[1, 128]],
                                           compare_op=ALU.is_ge, fill=0.0, base=0, channel_multiplier=-1),
         reads=[Umat.b], writes=[Umat.b])
    S.op("pool", lambda e: e.memset(mcur.t[:], 0.0), writes=[mcur.b])
    S.op("pool", lambda e: e.affine_select(out=mcur.t[:], in_=mcur.t[:], pattern=[[1, 128]],
                                           compare_op=ALU.is_ge, fill=NEG, base=0, channel_multiplier=-1),
         reads=[mcur.b], writes=[mcur.b])
    S.op("pool", lambda e: e.memset(mprev.t[:], 0.0), writes=[mprev.b])
    S.op("pool", lambda e: e.affine_select(out=mprev.t[:], in_=mprev.t[:], pattern=[[-1, 128]],
                                           compare_op=ALU.is_gt, fill=NEG, base=0, channel_multiplier=1),
         reads=[mprev.b], writes=[mprev.b])

    def zero(tb, ap):
        S.op("pool", lambda e: e.memset(ap, 0.0), writes=[tb.b])

    def load_colvec(es, name, src_d, ncol):
        raw = alloc(es, name + "_raw", [ncol, 128])
        dst = alloc(es, name, [128, ncol])
        S.dma(dmaq(), lambda e: e.dma_start(out=raw.t[:], in_=src_d), writes=[raw.b])
        ps, pb = bank()
        S.op("pe", lambda e: e.transpose(out=ps[:, 0:ncol], in_=raw.t[:], identity=ident.t[0:ncol, 0:ncol]),
             reads=[raw.b, ident.b], writes=[pb])
        S.op("dve", lambda e: e.tensor_copy(out=dst.t[:], in_=ps[:, 0:ncol]), reads=[pb], writes=[dst.b])
        return dst

    def load_bcast(es, name, src_d, n):
        t = alloc(es, name, [128, n])
        S.dma(dmaq(), lambda e: e.dma_start(out=t.t[:], in_=src_d.partition_broadcast(128)), writes=[t.b])
        return t

    def norm_T(src_ap, src_b, gT, S1, xT, tok_out=None, g_row=None):
        zero(ss, ss.t[:, 0:1])
        S.op("act", lambda e: e.activation(out=S1.t[:], in_=src_ap, func=AF.Square, accum_out=ss.t[:, 0:1]),
             reads=[src_b, ss.b], writes=[S1.b, ss.b])
        S.op("act", lambda e: e.activation(out=ss.t[:, 1:2], in_=ss.t[:, 0:1], func=AF.Sqrt, bias=EPS, scale=1.0 / D),
             reads=[ss.b], writes=[ss.b])
        S.op("dve", lambda e: e.reciprocal(out=ss.t[:, 2:3], in_=ss.t[:, 1:2]), reads=[ss.b], writes=[ss.b])
        S.op("act", lambda e: e.activation(out=S1.t[:], in_=src_ap, func=AF.Copy, scale=ss.t[:, 2:3]),
             reads=[src_b, ss.b], writes=[S1.b])
        if tok_out is not None:
            S.op("dve", lambda e: e.tensor_tensor(out=tok_out.t[:], in0=S1.t[:], in1=g_row.t[:], op=ALU.mult),
                 reads=[S1.b, g_row.b], writes=[tok_out.b])
        pd, pbs = dbank()
        for c in range(8):
            S.op("pe", lambda e, c=c: e.transpose(out=pd[:, c * 128:(c + 1) * 128], in_=S1.t[:, c * 128:(c + 1) * 128],
                                                  identity=ident.t[:]),
                 reads=[S1.b, ident.b], writes=[pbs[c // 4]])
        for k in range(2):
            S.op("dve", lambda e, k=k: e.tensor_tensor(
                out=xT.t[:, 4 * k:4 * k + 4, :],
                in0=pd[:, k * 512:(k + 1) * 512].rearrange("p (c t) -> p c t", c=4),
                in1=gT.t[:, 4 * k:4 * k + 4].unsqueeze(2).to_broadcast([128, 4, 128]), op=ALU.mult),
                 reads=[pbs[k], gT.b], writes=[xT.b])

    def transpose8(src, dstT, eng="dve"):
        pd, pbs = dbank()
        for c in range(8):
            S.op("pe", lambda e, c=c: e.transpose(out=pd[:, c * 128:(c + 1) * 128], in_=src.t[:, c * 128:(c + 1) * 128],
                                                  identity=ident.t[:]),
                 reads=[src.b, ident.b], writes=[pbs[c // 4]])
        S.op("dve", lambda e: e.tensor_copy(out=dstT.t[:, 0:4, :], in_=pd[:, 0:512].rearrange("p (c t) -> p c t", c=4)),
             reads=[pbs[0]], writes=[dstT.b])
        S.op("act", lambda e: e.activation(func=AF.Copy, out=dstT.t[:, 4:8, :], in_=pd[:, 512:1024].rearrange("p (c t) -> p c t", c=4)),
             reads=[pbs[1]], writes=[dstT.b])

    def out_proj_add(srcT, W, n):
        for half in range(2):
            ps, pb = bank()
            for cc in range(8):
                S.op("pe", lambda e, cc=cc, half=half, ps=ps: e.matmul(
                    ps[:, :], lhsT=srcT.t[:, cc, :], rhs=W.t[:, cc, half * 512:(half + 1) * 512],
                    start=(cc == 0), stop=(cc == 7)), reads=[srcT.b, W.b], writes=[pb])
            S.op("dve", lambda e, half=half, ps=ps: e.tensor_tensor(
                out=H[:, n, half * 512:(half + 1) * 512], in0=H[:, n, half * 512:(half + 1) * 512], in1=ps[:, :],
                op=ALU.add), reads=[pb, Hb[n]], writes=[Hb[n]])

    joinT = alloc(es_all, "joinT", [128, 1])

    def load_w(W, src_d, ncols, col0=0):
        S._deps("sp", [], [W.b])
        cbs = [Buf("wchunk") for _ in range(8)]
        for dc in range(8):
            S.dma(dmaq(), lambda e, dc=dc: e.dma_start(out=W.t[:, dc, col0:col0 + ncols],
                                                        in_=src_d[dc * 128:(dc + 1) * 128, 0:ncols]),
                  writes=[cbs[dc]])
        S.op("pool", lambda e: e.memset(joinT.t[:], 0.0), reads=cbs, writes=[W.b, joinT.b])

    with ExitStack() as es:
        Win = alloc(es, "Win", [128, 8, P_IN])
        Wout = alloc(es, "Wout", [128, 8, D])
        load_w(Win, w_in_d, P_IN)
        load_w(Wout, w_out_d, D)
        gmixT = load_colvec(es, "gmixT", g_mix_d, 8)
        cwraw = alloc(es, "cwraw", [4, 512])
        S.dma(dmaq(), lambda e: e.dma_start(out=cwraw.t[:], in_=conv_w_d), writes=[cwraw.b])
        convw = alloc(es, "convw", [128, 4, 4])
        ps, pb = bank()
        for ch in range(4):
            S.op("pe", lambda e, ch=ch: e.transpose(out=ps[:, ch * 4:ch * 4 + 4], in_=cwraw.t[:, ch * 128:(ch + 1) * 128],
                                                    identity=ident.t[0:4, 0:4]), reads=[cwraw.b, ident.b], writes=[pb])
        S.op("dve", lambda e: e.tensor_copy(out=convw.t[:], in_=ps[:, 0:16].rearrange("p (c j) -> p c j", c=4)),
             reads=[pb], writes=[convw.b])
        convb = load_colvec(es, "convb", conv_b_d, 4)
        gateb = load_bcast(es, "gateb", gb_d, 8)
        gmh = load_bcast(es, "gmh", g_mhead_d, 512)
        esink = load_bcast(es, "esink", sinks_d, 8)
        S.op("act", lambda e: e.activation(out=esink.t[:], in_=esink.t[:], func=AF.Exp), reads=[esink.b], writes=[esink.b])
        pmask = load_bcast(es, "pmask", pm_d, NPRE)
        pmm = alloc(es, "pmm", [128, 1])
        S.op("dve", lambda e: e.tensor_scalar(out=pmm.t[:], in0=pmask.t[:, NPRE - 1:NPRE], scalar1=-1.0, scalar2=-NEG,
                                              op0=ALU.add, op1=ALU.mult), reads=[pmask.b], writes=[pmm.b])
        S.op("dve", lambda e: e.tensor_scalar(out=mprev0.t[:], in0=mprev.t[:], scalar1=pmm.t[:, 0:1], scalar2=None,
                                              op0=ALU.add), reads=[mprev.b, pmm.b], writes=[mprev0.b])

        S1 = alloc(es, "S1", [128, D])
        xT = alloc(es, "xT", [128, 8, 128])
        pre = alloc(es, "pre", [128, 4, 131])
        qkT = alloc(es, "qkT", [128, 4, 128])
        vext0 = alloc(es, "vext", [128, 4, 129])
        aqT = alloc(es, "aqT", [128, 8, 128])
        kTs = alloc(es, "kTs", [128, 2, 2, 128])
        vsw = alloc(es, "vsw", [128, 2, 2, 65])
        PT = alloc(es, "PT", [128, 512])
        PT2 = alloc(es, "PT2", [128, 512])
        cacc = TB(PT.t[:, :].rearrange("p (c t) -> p c t", c=4), "cacc_alias")
        cacc.b = PT.b
        ctmp = TB(PT2.t[:, :].rearrange("p (c t) -> p c t", c=4), "ctmp_alias")
        ctmp.b = PT2.b
        gsig = PT2
        LFB = cacc
        DT = alloc(es, "DT", [128, 128])
        ST = alloc(es, "ST", [128, 128])
        num = alloc(es, "num", [128, 129])
        its = alloc(es, "its", [128, 129])
        kw = alloc(es, "kw", [128, 128])
        CT = alloc(es, "CT", [128, 2, 129])
        gs0 = alloc(es, "gs", [128, 48])
        gs1 = alloc(es, "gs1", [128, 48])
        gss = [gs0, gs1]
        sm = alloc(es, "sm", [128, 16])
        S.op("pool", lambda e: e.memset(pre.t[:], 0.0), writes=[pre.b])
        S.op("pool", lambda e: e.memset(CT.t[:], 0.0), writes=[CT.b])
        vext1 = TB(aqT.t[:, :, :].rearrange("p a b -> p (a b)")[:, 0:516].rearrange("p (h v) -> p h v", h=4), "vext1_alias")
        vext1.b = aqT.b
        vexts = [vext0, vext1]
        for vx in vexts:
            S.op("pool", lambda e: e.memset(vx.t[:], 1.0), writes=[vx.b])
        S.op("pool", lambda e: e.memset(vsw.t[:], 1.0), writes=[vsw.b])
        LN8 = float(np.log(0.125))

        def mixer_block(src_d, row0, n, light, slot_mask_col, with_swa_kv, part="ALL"):
            slot = n % 2
            vext = vexts[n % 2] if light else vexts[0]
            gs = gss[n % 2] if light else gss[0]
            if part in ("ALL", "F"):
                xap, xb = H[:, n % NB, :], Hb[n % NB]
                S.dma(dmaq(), lambda e: e.dma_start(out=xap, in_=src_d[row0:row0 + 128, :]), writes=[xb])
                norm_T(xap, xb, gmixT, S1, xT)
                ps_g, pb_g = bank()
                for dc in range(8):
                    S.op("pe", lambda e, dc=dc: e.matmul(ps_g[:, 0:8], lhsT=xT.t[:, dc, :], rhs=Win.t[:, dc, 1536:1544],
                                                         start=(dc == 0), stop=(dc == 7)), reads=[Win.b, xT.b], writes=[pb_g])
                if with_swa_kv:
                    for dc in range(8):
                        S.op("pe", lambda e, dc=dc: e.matmul(ps_g[:, 128:256], lhsT=xT.t[:, dc, :], rhs=Win.t[:, dc, 2184:2312],
                                                             start=(dc == 0), stop=(dc == 7)), reads=[Win.b, xT.b], writes=[pb_g])
                    for kv in range(2):
                        for dc in range(8):
                            S.op("pe", lambda e, dc=dc, kv=kv: e.matmul(
                                ps_g[0:64, 256 + kv * 128:256 + (kv + 1) * 128],
                                lhsT=Win.t[:, dc, 2056 + kv * 64:2056 + (kv + 1) * 64], rhs=xT.t[:, dc, :],
                                start=(dc == 0), stop=(dc == 7)), reads=[Win.b, xT.b], writes=[pb_g])
                    S.op("dve", lambda e: e.tensor_copy(out=vsw.t[:, slot, :, 0:64],
                                                        in_=ps_g[:, 128:256].rearrange("p (k d) -> p k d", k=2)),
                         reads=[pb_g], writes=[vsw.b])
                    S.op("dve", lambda e: e.tensor_copy(out=kTs.t[0:64, slot, :, :],
                                                        in_=ps_g[0:64, 256:512].rearrange("p (k t) -> p k t", k=2)),
                         reads=[pb_g], writes=[kTs.b])
                S.op("dve", lambda e: e.tensor_tensor(out=gs.t[:, 0:8], in0=ps_g[:, 0:8], in1=gateb.t[:, 0:8], op=ALU.add),
                     reads=[pb_g, gateb.b], writes=[gs.b])
                ps_qk, pb_qk = bank()
                for ch in (range(4) if (not light or with_swa_kv) else (2, 3)):
                    for dc in range(8):
                        S.op("pe", lambda e, ch=ch, dc=dc: e.matmul(
                            ps_qk[:, ch * 128:(ch + 1) * 128], lhsT=Win.t[:, dc, ch * 128:(ch + 1) * 128], rhs=xT.t[:, dc, :],
                            start=(dc == 0), stop=(dc == 7)), reads=[Win.b, xT.b], writes=[pb_qk])
                c0 = 0 if (not light or with_swa_kv) else 2
                S.op("act", lambda e: e.activation(func=AF.Copy, out=pre.t[:, c0:4, 3:131],
                                                   in_=ps_qk[:, c0 * 128:512].rearrange("p (c t) -> p c t", c=4 - c0)),
                     reads=[pb_qk], writes=[pre.b])
                ps_v, pb_v = bank()
                for dc in range(8):
                    S.op("pe", lambda e, dc=dc: e.matmul(ps_v[:, :], lhsT=xT.t[:, dc, :], rhs=Win.t[:, dc, 512:1024],
                                                         start=(dc == 0), stop=(dc == 7)), reads=[Win.b, xT.b], writes=[pb_v])
                S.op("dve", lambda e: e.tensor_copy(out=vext.t[:, :, 0:128], in_=ps_v[:, :].rearrange("p (h v) -> p h v", h=4)),
                     reads=[pb_v], writes=[vext.b])
            if part in ("ALL", "G"):
                S.op("act", lambda e: e.activation(out=gs.t[:, 24:28], in_=gs.t[:, 4:8], func=AF.Exp, scale=-1.0),
                     reads=[gs.b], writes=[gs.b])
                S.op("act", lambda e: e.activation(out=gs.t[:, 24:28], in_=gs.t[:, 24:28], func=AF.Ln, bias=1.0),
                     reads=[gs.b], writes=[gs.b])
                S.op("dve", lambda e: e.tensor_scalar(out=gs.t[:, 4:8], in0=gs.t[:, 24:28], scalar1=-1.0, scalar2=None,
                                                      op0=ALU.mult), reads=[gs.b], writes=[gs.b])
                ps_b, pb_b = bank()
                S.op("pe", lambda e: e.matmul(ps_b[:, 0:4], lhsT=Umat.t[:], rhs=gs.t[:, 4:8], start=True, stop=True),
                     reads=[Umat.b, gs.b], writes=[pb_b])
                S.op("pe", lambda e: e.matmul(ps_b[:, 4:8], lhsT=ones.t[:], rhs=gs.t[:, 4:8], start=True, stop=True),
                     reads=[ones.b, gs.b], writes=[pb_b])
                S.op("dve", lambda e: e.tensor_tensor(out=gs.t[:, 12:16], in0=gs.t[:, 0:4], in1=ps_b[:, 0:4], op=ALU.subtract),
                     reads=[gs.b, pb_b], writes=[gs.b])
                S.op("dve", lambda e: e.tensor_tensor(out=gs.t[:, 24:28], in0=gs.t[:, 12:16], in1=ps_b[:, 4:8], op=ALU.add),
                     reads=[gs.b, pb_b], writes=[gs.b])
                S.op("act", lambda e: e.activation(out=gs.t[:, 16:20], in_=gs.t[:, 24:28], func=AF.Exp),
                     reads=[gs.b], writes=[gs.b])
                if slot_mask_col is not None:
                    S.op("dve", lambda e: e.tensor_scalar(out=gs.t[:, 16:20], in0=gs.t[:, 16:20],
                                                          scalar1=pmask.t[:, slot_mask_col:slot_mask_col + 1], scalar2=None,
                                                          op0=ALU.mult), reads=[gs.b, pmask.b], writes=[gs.b])
                S.op("act", lambda e: e.activation(out=gs.t[:, 20:24], in_=ps_b[:, 4:8], func=AF.Exp),
                     reads=[pb_b], writes=[gs.b])
                if not light:
                    S.op("act", lambda e: e.activation(out=gs.t[:, 8:12], in_=ps_b[:, 0:4], func=AF.Exp, bias=LN8),
                         reads=[pb_b], writes=[gs.b])
                    S.op("dve", lambda e: e.tensor_scalar(out=gs.t[:, 12:16], in0=gs.t[:, 12:16], scalar1=LN8, scalar2=None,
                                                          op0=ALU.add), reads=[gs.b], writes=[gs.b])
                for j in range(4):
                    wj = convw.t[:, :, j:j + 1].to_broadcast([128, 4, 128])
                    if j == 0:
                        S.op("dve", lambda e, wj=wj: e.tensor_tensor(out=cacc.t[:], in0=pre.t[:, :, 0:128], in1=wj, op=ALU.mult),
                             reads=[pre.b, convw.b], writes=[cacc.b])
                    else:
                        S.op("dve", lambda e, wj=wj, j=j: e.tensor_tensor(out=ctmp.t[:], in0=pre.t[:, :, j:j + 128], in1=wj,
                                                                           op=ALU.mult),
                             reads=[pre.b, convw.b], writes=[ctmp.b])
                        S.op("dve", lambda e: e.tensor_tensor(out=cacc.t[:], in0=cacc.t[:], in1=ctmp.t[:], op=ALU.add),
                             reads=[cacc.b, ctmp.b], writes=[cacc.b])
                S.op("dve", lambda e: e.tensor_tensor(out=cacc.t[:], in0=cacc.t[:],
                                                      in1=convb.t[:, :].unsqueeze(2).to_broadcast([128, 4, 128]), op=ALU.add),
                     reads=[cacc.b, convb.b], writes=[cacc.b])
                S.op("act", lambda e: e.activation(out=qkT.t[:], in_=cacc.t[:], func=AF.Silu), reads=[cacc.b], writes=[qkT.b])
                S.op("dve", lambda e: e.tensor_copy(out=pre.t[:, :, 0:3], in_=pre.t[:, :, 128:131]), reads=[pre.b], writes=[pre.b])

            if part == "ALL":
                if not light:
                    S.op("dve", lambda e: e.tensor_copy(out=LFB.t[:], in_=gs.t[:, 4:8].unsqueeze(2).to_broadcast([128, 4, 128])),
                         reads=[gs.b], writes=[LFB.b])
                    ps_o, pb_o = bank()
                    for dc in range(8):
                        S.op("pe", lambda e, dc=dc: e.matmul(ps_o[:, :], lhsT=xT.t[:, dc, :], rhs=Win.t[:, dc, 1024:1536],
                                                             start=(dc == 0), stop=(dc == 7)), reads=[Win.b, xT.b], writes=[pb_o])
                    S.op("act", lambda e: e.activation(out=gsig.t[:], in_=ps_o[:, :], func=AF.Sigmoid), reads=[pb_o], writes=[gsig.b])
                    S.op("dve", lambda e: e.tensor_tensor(out=gsig.t[:], in0=gsig.t[:], in1=gmh.t[:], op=ALU.mult),
                         reads=[gsig.b, gmh.b], writes=[gsig.b])
                    pd_q, pbs_q = dbank()
                    for hq in range(8):
                        for dc in range(8):
                            S.op("pe", lambda e, hq=hq, dc=dc: e.matmul(
                                pd_q[0:64, hq * 128:(hq + 1) * 128], lhsT=Win.t[:, dc, 1544 + hq * 64:1544 + (hq + 1) * 64],
                                rhs=xT.t[:, dc, :], start=(dc == 0), stop=(dc == 7)),
                                 reads=[Win.b, xT.b], writes=[pbs_q[hq // 4]])
                    S.op("dve", lambda e: e.tensor_copy(out=aqT.t[0:64, 0:4, :], in_=pd_q[0:64, 0:512].rearrange("p (h t) -> p h t", h=4)),
                         reads=[pbs_q[0]], writes=[aqT.b])
                    S.op("dve", lambda e: e.tensor_copy(out=aqT.t[0:64, 4:8, :],
                                                        in_=pd_q[0:64, 512:1024].rearrange("p (h t) -> p h t", h=4)),
                         reads=[pbs_q[1]], writes=[aqT.b])
                    mixb = S1
                    for h in range(4):
                        c, hh = h // 2, h % 2
                        p0, p1 = hh * 64, hh * 64 + 64
                        ps_d, pb_d = bank()
                        S.op("pe", lambda e, h=h, ps_d=ps_d: e.matmul(ps_d[:, 0:128], lhsT=LFB.t[:, h, :], rhs=Umat.t[:],
                                                                      start=True, stop=False),
                             reads=[LFB.b, Umat.b], writes=[pb_d])
                        S.op("pe", lambda e, ps_d=ps_d: e.matmul(ps_d[:, 0:128], lhsT=ident.t[:], rhs=mcur.t[:], start=False, stop=True),
                             reads=[ident.b, mcur.b], writes=[pb_d])
                        S.op("pe", lambda e, ps_d=ps_d, c=c, p0=p0, p1=p1: e.matmul(
                            ps_d[:, 128:256], lhsT=qkT.t[p0:p1, 2 + c, :], rhs=qkT.t[p0:p1, c, :], start=True, stop=True),
                             reads=[qkT.b], writes=[pb_d])
                        S.op("act", lambda e, ps_d=ps_d, h=h: e.activation(out=DT.t[:], in_=ps_d[:, 0:128], func=AF.Exp,
                                                                           bias=gs.t[:, 12 + h:13 + h]),
                             reads=[pb_d, gs.b], writes=[DT.b])
                        S.op("dve", lambda e, ps_d=ps_d: e.tensor_tensor(out=ST.t[:], in0=ps_d[:, 128:256], in1=DT.t[:], op=ALU.mult),
                             reads=[pb_d, DT.b], writes=[ST.b])
                        ps_n, pb_n = bank()
                        S.op("pe", lambda e, ps_n=ps_n, h=h: e.matmul(ps_n[:, 0:129], lhsT=ST.t[:], rhs=vext.t[:, h, :],
                                                                      start=True, stop=True), reads=[ST.b, vext.b], writes=[pb_n])
                        S.op("pe", lambda e, ps_n=ps_n, c=c, p0=p0, p1=p1: e.matmul(
                            ps_n[:, 256:385], lhsT=qkT.t[p0:p1, c, :], rhs=CT.t[p0:p1, c, :], start=True, stop=True),
                             reads=[qkT.b, CT.b], writes=[pb_n])
                        S.op("act", lambda e, ps_n=ps_n, h=h: e.activation(out=its.t[:], in_=ps_n[:, 256:385], func=AF.Copy,
                                                                           scale=gs.t[:, 8 + h:9 + h]),
                             reads=[pb_n, gs.b], writes=[its.b])
                        S.op("dve", lambda e, ps_n=ps_n: e.tensor_tensor(out=num.t[:], in0=ps_n[:, 0:129], in1=its.t[:], op=ALU.add),
                             reads=[pb_n, its.b], writes=[num.b])
                        S.op("act", lambda e: e.activation(out=sm.t[:, 0:1], in_=num.t[:, 128:129], func=AF.Abs),
                             reads=[num.b], writes=[sm.b])
                        S.op("dve", lambda e: e.tensor_scalar(out=sm.t[:, 0:1], in0=sm.t[:, 0:1], scalar1=1.0, scalar2=None, op0=ALU.max),
                             reads=[sm.b], writes=[sm.b])
                        S.op("dve", lambda e: e.reciprocal(out=sm.t[:, 1:2], in_=sm.t[:, 0:1]), reads=[sm.b], writes=[sm.b])
                        zero(sm, sm.t[:, 2:3])
                        S.op("act", lambda e: e.activation(out=its.t[:, 0:128], in_=num.t[:, 0:128], func=AF.Square,
                                                           scale=sm.t[:, 1:2], accum_out=sm.t[:, 2:3]),
                             reads=[num.b, sm.b], writes=[its.b, sm.b])
                        S.op("act", lambda e: e.activation(out=sm.t[:, 3:4], in_=sm.t[:, 2:3], func=AF.Sqrt, bias=EPS, scale=1.0 / 128),
                             reads=[sm.b], writes=[sm.b])
                        S.op("dve", lambda e: e.reciprocal(out=sm.t[:, 4:5], in_=sm.t[:, 3:4]), reads=[sm.b], writes=[sm.b])
                        S.op("dve", lambda e: e.tensor_tensor(out=sm.t[:, 5:6], in0=sm.t[:, 4:5], in1=sm.t[:, 1:2], op=ALU.mult),
                             reads=[sm.b], writes=[sm.b])
                        S.op("dve", lambda e, h=h: e.scalar_tensor_tensor(
                            out=mixb.t[:, h * 128:(h + 1) * 128], in0=num.t[:, 0:128], scalar=sm.t[:, 5:6],
                            in1=gsig.t[:, h * 128:(h + 1) * 128], op0=ALU.mult, op1=ALU.mult),
                             reads=[num.b, sm.b, gsig.b], writes=[mixb.b])
            if part in ("ALL", "U"):
                for c in range(2):
                    ps_k, pb_k = bank()
                    S.op("pe", lambda e, ps_k=ps_k, c=c: e.transpose(out=ps_k[:, 0:128], in_=qkT.t[:, 2 + c, :], identity=ident.t[:]),
                         reads=[qkT.b, ident.b], writes=[pb_k])
                    S.op("dve", lambda e, ps_k=ps_k, c=c: e.tensor_tensor(
                        out=kw.t[:, :].rearrange("p (a k) -> p a k", a=2), in0=ps_k[:, 0:128].rearrange("p (a k) -> p a k", a=2),
                        in1=gs.t[:, 16 + 2 * c:18 + 2 * c].unsqueeze(2).to_broadcast([128, 2, 64]), op=ALU.mult),
                         reads=[pb_k, gs.b], writes=[kw.b])
                    S.op("pe", lambda e, ps_k=ps_k, c=c: e.matmul(
                        ps_k[:, 128:386], lhsT=kw.t[:], rhs=vext.t[:, 2 * c:2 * c + 2, :].rearrange("p a b -> p (a b)"),
                        start=True, stop=True), reads=[kw.b, vext.b], writes=[pb_k])
                    for hh in range(2):
                        p0, p1 = hh * 64, hh * 64 + 64
                        h = 2 * c + hh
                        S.op("dve", lambda e, ps_k=ps_k, c=c, hh=hh, p0=p0, p1=p1, h=h: e.scalar_tensor_tensor(
                            out=CT.t[p0:p1, c, :], in0=CT.t[p0:p1, c, :], scalar=gs.t[p0:p1, 20 + h:21 + h],
                            in1=ps_k[p0:p1, 128 + hh * 129:128 + (hh + 1) * 129], op0=ALU.mult, op1=ALU.add),
                             reads=[pb_k, gs.b, CT.b], writes=[CT.b])
            if part != "ALL":
                return
            if light:
                return
            for kv in range(2):
                ps_o, pb_o = bank()
                pts = (PT, PT2)
                srcs = ((1 - slot, mprev0 if n == 0 else mprev), (slot, mcur))
                for wi, (sl, mk) in enumerate(srcs):
                    ps_s, pb_s = bank()
                    S.op("pe", lambda e, ps_s=ps_s, sl=sl, kv=kv: e.matmul(
                        ps_s[:, :], lhsT=kTs.t[0:64, sl, kv, :], rhs=aqT.t[0:64, kv * 4:kv * 4 + 4, :].rearrange("p h t -> p (h t)"),
                        start=True, stop=False), reads=[kTs.b, aqT.b], writes=[pb_s])
                    for g in range(4):
                        S.op("pe", lambda e, ps_s=ps_s, g=g, mk=mk: e.matmul(
                            ps_s[:, g * 128:(g + 1) * 128], lhsT=ident.t[:], rhs=mk.t[:], start=False, stop=(g == 3)),
                             reads=[ident.b, mk.b], writes=[pb_s])
                    S.op("act", lambda e, ps_s=ps_s, wi=wi: e.activation(out=pts[wi].t[:], in_=ps_s[:, :], func=AF.Exp, scale=0.125),
                         reads=[pb_s], writes=[pts[wi].b])
                for g in range(4):
                    for wi, (sl, mk) in enumerate(srcs):
                        S.op("pe", lambda e, ps_o=ps_o, g=g, sl=sl, kv=kv, wi=wi: e.matmul(
                            ps_o[:, g * 65:(g + 1) * 65], lhsT=pts[wi].t[:, g * 128:(g + 1) * 128], rhs=vsw.t[:, sl, kv, :],
                            start=(wi == 0), stop=(wi == 1)), reads=[pts[wi].b, vsw.b], writes=[pb_o])
                o3 = ps_o[:, 0:260].rearrange("p (g d) -> p g d", g=4)
                S.op("dve", lambda e, o3=o3, kv=kv: e.tensor_tensor(out=sm.t[:, 8:12], in0=o3[:, :, 64],
                                                                    in1=esink.t[:, kv * 4:kv * 4 + 4], op=ALU.add),
                     reads=[pb_o, esink.b], writes=[sm.b])
                S.op("dve", lambda e: e.reciprocal(out=sm.t[:, 12:16], in_=sm.t[:, 8:12]), reads=[sm.b], writes=[sm.b])
                S.op("dve", lambda e, o3=o3, kv=kv: e.tensor_tensor(
                    out=mixb.t[:, 512 + kv * 256:512 + (kv + 1) * 256].rearrange("p (g d) -> p g d", g=4),
                    in0=o3[:, :, 0:64], in1=sm.t[:, 12:16].unsqueeze(2).to_broadcast([128, 4, 64]), op=ALU.mult),
                     reads=[pb_o, sm.b], writes=[mixb.b])
            transpose8(mixb, xT)
            out_proj_add(xT, Wout, n)

        for i in range(NPRE):
            mixer_block(xp_d, i * 128, i, True, i, i == NPRE - 1, part="F")
            if i > 0:
                mixer_block(xp_d, (i - 1) * 128, i - 1, True, i - 1, i - 1 == NPRE - 1, part="U")
            mixer_block(xp_d, i * 128, i, True, i, i == NPRE - 1, part="G")
        mixer_block(xp_d, (NPRE - 1) * 128, NPRE - 1, True, NPRE - 1, True, part="U")
        for n in range(NB):
            mixer_block(xs_d, n * 128, n, False, None, True)
        S.barrier()
        S.emit()

    if stop_after >= 2:
        with ExitStack() as es:
            Wq = alloc(es, "Wq", [128, 8, D])
            W2 = alloc(es, "W2", [128, 8, D])
            load_w(Wq, w_xq_d, D)
            load_w(W2, w_xk_d, D)
            gcT = load_colvec(es, "gcT", g_cross_d, 8)
            gmT = load_colvec(es, "gmT", g_mem_d, 8)
            S1 = alloc(es, "S1b", [128, D])
            xT = alloc(es, "xTb", [128, 8, 128])
            memT = alloc(es, "memT", [128, 8, 256])
            KT = alloc(es, "KT", [128, 8, 256])
            Vx = alloc(es, "Vx", [128, 2, 4, 257])
            qT = alloc(es, "qT", [128, 8, 128])
            PTb = alloc(es, "PTb", [128, 8, 128])
            oc = alloc(es, "oc", [128, D])
            MX = alloc(es, "MXb", [128, D])
            S.op("pool", lambda e: e.memset(Vx.t[:], 1.0), writes=[Vx.b])
            for mc in range(2):
                S.dma(dmaq(), lambda e, mc=mc: e.dma_start(out=MX.t[:], in_=mem_d[mc * 128:(mc + 1) * 128, :]), writes=[MX.b])
                norm_T(MX.t[:], MX.b, gmT, S1, xT)
                S.op("dve", lambda e, mc=mc: e.tensor_copy(out=memT.t[:, :, mc * 128:(mc + 1) * 128], in_=xT.t[:]),
                     reads=[xT.b], writes=[memT.b])
            for j in range(8):
                ps, pb = bank()
                for dc in range(8):
                    S.op("pe", lambda e, ps=ps, j=j, dc=dc: e.matmul(ps[:, 0:256], lhsT=W2.t[:, dc, j * 128:(j + 1) * 128],
                                                                     rhs=memT.t[:, dc, :], start=(dc == 0), stop=(dc == 7)),
                         reads=[W2.b, memT.b], writes=[pb])
                S.op("act", lambda e, ps=ps, j=j: e.activation(func=AF.Copy, out=KT.t[:, j, :], in_=ps[:, 0:256]), reads=[pb], writes=[KT.b])
            load_w(W2, w_xv_d, D)
            for mc in range(2):
                for half in range(2):
                    ps, pb = bank()
                    for dc in range(8):
                        S.op("pe", lambda e, ps=ps, mc=mc, half=half, dc=dc: e.matmul(
                            ps[:, :], lhsT=memT.t[:, dc, mc * 128:(mc + 1) * 128], rhs=W2.t[:, dc, half * 512:(half + 1) * 512],
                            start=(dc == 0), stop=(dc == 7)), reads=[W2.b, memT.b], writes=[pb])
                    S.op("dve", lambda e, ps=ps, mc=mc, half=half: e.tensor_copy(
                        out=Vx.t[:, mc, 2 * half:2 * half + 2, 0:256], in_=ps[:, :].rearrange("p (h v) -> p h v", h=2)),
                         reads=[pb], writes=[Vx.b])
            load_w(W2, w_xo_d, D)
            for n in range(NB):
                norm_T(H[:, n, :], Hb[n], gcT, S1, xT)
                pd, pbs = dbank()
                for j in range(8):
                    for dc in range(8):
                        S.op("pe", lambda e, j=j, dc=dc: e.matmul(pd[:, j * 128:(j + 1) * 128], lhsT=Wq.t[:, dc, j * 128:(j + 1) * 128],
                                                                  rhs=xT.t[:, dc, :], start=(dc == 0), stop=(dc == 7)),
                             reads=[Wq.b, xT.b], writes=[pbs[j // 4]])
                S.op("act", lambda e: e.activation(func=AF.Copy, out=qT.t[:, 0:4, :], in_=pd[:, 0:512].rearrange("p (j t) -> p j t", j=4)),
                     reads=[pbs[0]], writes=[qT.b])
                S.op("dve", lambda e: e.tensor_copy(out=qT.t[:, 4:8, :], in_=pd[:, 512:1024].rearrange("p (j t) -> p j t", j=4)),
                     reads=[pbs[1]], writes=[qT.b])
                pd2, pbs2 = dbank()
                for h in range(4):
                    for mc in range(2):
                        col = (h * 2 + mc) * 128
                        for cc in range(2):
                            S.op("pe", lambda e, h=h, mc=mc, cc=cc, col=col: e.matmul(
                                pd2[:, col:col + 128], lhsT=KT.t[:, h * 2 + cc, mc * 128:(mc + 1) * 128], rhs=qT.t[:, h * 2 + cc, :],
                                start=(cc == 0), stop=(cc == 1)), reads=[KT.b, qT.b], writes=[pbs2[h // 2]])
                for k in range(2):
                    S.op("act", lambda e, k=k: e.activation(out=PTb.t[:, 4 * k:4 * k + 4, :],
                                                            in_=pd2[:, k * 512:(k + 1) * 512].rearrange("p (j t) -> p j t", j=4),
                                                            func=AF.Exp, scale=1.0 / 16), reads=[pbs2[k]], writes=[PTb.b])
                for h in range(4):
                    ps, pb = bank()
                    for mc in range(2):
                        S.op("pe", lambda e, ps=ps, h=h, mc=mc: e.matmul(ps[:, 0:257], lhsT=PTb.t[:, h * 2 + mc, :], rhs=Vx.t[:, mc, h, :],
                                                                         start=(mc == 0), stop=(mc == 1)),
                             reads=[PTb.b, Vx.b], writes=[pb])
                    S.op("dve", lambda e, ps=ps: e.reciprocal(out=ss.t[:, 3:4], in_=ps[:, 256:257]), reads=[pb], writes=[ss.b])
                    S.op("act", lambda e, ps=ps, h=h: e.activation(out=oc.t[:, h * 256:(h + 1) * 256], in_=ps[:, 0:256], func=AF.Copy,
                                                                   scale=ss.t[:, 3:4]), reads=[pb, ss.b], writes=[oc.b])
                transpose8(oc, xT)
                out_proj_add(xT, W2, n)
            S.barrier()
            S.emit()

    with ExitStack() as esC:
        S1 = alloc(esC, "S1c", [128, D])
        if stop_after >= 3:
            idxTa = alloc(esC, "idxTa", [128, NB, 128], U32)
            gTa = alloc(esC, "gTa", [128, NB, 128])
        if stop_after >= 3:
          with ExitStack() as es:
              Wpq = alloc(es, "Wpq", [128, 8, 2048])
              load_w(Wpq, w_pq_d, 2048)
              gfT = load_colvec(es, "gfT", g_ffn_d, 8)
              skT = alloc(es, "skT", [128, 2, 128])
              skr = alloc(es, "skr", [128, 128])
              for i, skd in enumerate((sk1_d, sk2_d)):
                  S.dma(dmaq(), lambda e, skd=skd: e.dma_start(out=skr.t[:], in_=skd), writes=[skr.b])
                  ps, pb = bank()
                  S.op("pe", lambda e, ps=ps: e.transpose(out=ps[:, 0:128], in_=skr.t[:], identity=ident.t[:]),
                       reads=[skr.b, ident.b], writes=[pb])
                  S.op("dve", lambda e, ps=ps, i=i: e.tensor_copy(out=skT.t[:, i, :], in_=ps[:, 0:128]), reads=[pb], writes=[skT.b])
              xT = alloc(es, "xTc", [128, 8, 128])
              T8a = alloc(es, "T8a", [128, 2048])
              T8bs = [alloc(es, "T8b%d" % i, [128, 2048]) for i in range(2)]
              Q8 = alloc(es, "Q8", [128, 2048])
              T8c = alloc(es, "T8c", [128, 2048])
              v16 = alloc(es, "v16", [128, 16, 16])
              i16u = alloc(es, "i16u", [128, 16, 16], U32)
              i16 = alloc(es, "i16", [128, 16, 16])
              tsv = alloc(es, "tsv", [128, 8, 16])
              posu = alloc(es, "posu", [128, 8, 16], U32)
              pau = alloc(es, "pau", [128, 8, 16], U32)
              pbu = alloc(es, "pbu", [128, 8, 16], U32)
              pa = alloc(es, "pa", [128, 8, 16])
              pbq = alloc(es, "pbq", [128, 8, 16])
              eid = alloc(es, "eid", [128, 8, 16])
              eid2 = alloc(es, "eid2", [128, 8, 16])
              gg = alloc(es, "gg", [128, 8, 16])
              gsum = alloc(es, "gsum", [128, 8])
              iota16 = alloc(es, "iota16", [128, 16])
              Aacc = alloc(es, "Aacc", [128, 128])
              S.op("pool", lambda e: e.iota(iota16.t[:], pattern=[[1, 16]], base=0, channel_multiplier=0,
                                            allow_small_or_imprecise_dtypes=True), writes=[iota16.b])

              def top16_multi(groups, src_b, work_b, out_bs):
                  gbs = [Buf("g%d" % i) for i in range(len(groups))]
                  for st in range(5):
                      for gi, (src_ap, work, vout, iout) in enumerate(groups):
                          gb = gbs[gi]
                          extra = ([work_b] + out_bs) if (st == 0 and gi == 0) else []
                          if st == 0:
                              S.op("dve", lambda e: e.max(out=vout[:, 0:8], in_=src_ap), reads=[src_b], writes=[gb] + extra)
                          elif st == 1:
                              S.op("dve", lambda e: e.max_index(out=iout[:, 0:8], in_max=vout[:, 0:8], in_values=src_ap),
                                   reads=[src_b, gb], writes=[gb])
                          elif st == 2:
                              S.op("dve", lambda e: e.match_replace(out=work, in_to_replace=vout[:, 0:8], in_values=src_ap,
                                                                    imm_value=-1e30), reads=[src_b, gb], writes=[gb])
                          elif st == 3:
                              S.op("dve", lambda e: e.max(out=vout[:, 8:16], in_=work), reads=[gb], writes=[gb])
                          else:
                              S.op("dve", lambda e: e.max_index(out=iout[:, 8:16], in_max=vout[:, 8:16], in_values=work),
                                   reads=[gb], writes=[gb])
                  return gbs

              def c1_front(n):
                  T8b = T8bs[n % 2]
                  norm_T(H[:, n, :], Hb[n], gfT, S1, xT)
                  for q4 in range(4):
                      ps, pb = bank()
                      for jj in range(4):
                          j = q4 * 4 + jj
                          for dc in range(8):
                              S.op("pe", lambda e, ps=ps, jj=jj, j=j, dc=dc: e.matmul(
                                  ps[:, jj * 128:(jj + 1) * 128], lhsT=Wpq.t[:, dc, j * 128:(j + 1) * 128], rhs=xT.t[:, dc, :],
                                  start=(dc == 0), stop=(dc == 7)), reads=[Wpq.b, xT.b], writes=[pb])
                      S.op("act", lambda e, ps=ps, q4=q4: e.activation(func=AF.Copy, out=Q8.t[:, q4 * 512:(q4 + 1) * 512], in_=ps[:, :]),
                           reads=[pb], writes=[Q8.b])
                  for q4 in range(4):
                      ps, pb = bank()
                      for jj in range(4):
                          j = q4 * 4 + jj
                          S.op("pe", lambda e, ps=ps, jj=jj, j=j: e.matmul(
                              ps[:, jj * 128:(jj + 1) * 128], lhsT=Q8.t[:, j * 128:(j + 1) * 128], rhs=skT.t[:, j % 2, :],
                              start=True, stop=True), reads=[Q8.b, skT.b], writes=[pb])
                      S.op("act", lambda e, ps=ps, q4=q4: e.activation(func=AF.Copy, out=T8b.t[:, q4 * 512:(q4 + 1) * 512], in_=ps[:, :]),
                           reads=[pb], writes=[T8b.b])
              def c1_back(n):
                  T8b = T8bs[n % 2]
                  g1 = top16_multi([(T8b.t[:, j * 128:(j + 1) * 128], T8c.t[:, j * 128:(j + 1) * 128], v16.t[:, j, :], i16u.t[:, j, :])
                                    for j in range(16)], T8b.b, T8c.b, [v16.b, i16u.b])
                  S.op("dve", lambda e: e.tensor_copy(out=i16.t[:], in_=i16u.t[:]), reads=[i16u.b] + g1, writes=[i16.b])
                  v4 = v16.t[:, :, :].rearrange("p (h two) k -> p h two k", two=2)
                  cs4 = T8c.t[:, :].rearrange("p (h a b) -> p h a b", h=8, a=16)
                  S.op("dve", lambda e: e.tensor_tensor(out=cs4, in0=v4[:, :, 0, :].unsqueeze(3).to_broadcast([128, 8, 16, 16]),
                                                        in1=v4[:, :, 1, :].unsqueeze(2).to_broadcast([128, 8, 16, 16]), op=ALU.add),
                       reads=[v16.b] + g1, writes=[T8c.b])
                  g2 = top16_multi([(T8c.t[:, h * 256:(h + 1) * 256], T8a.t[:, h * 256:(h + 1) * 256], tsv.t[:, h, :], posu.t[:, h, :])
                                    for h in range(8)], T8c.b, T8a.b, [tsv.b, posu.b])
                  S.op("dve", lambda e: e.tensor_single_scalar(out=pau.t[:], in_=posu.t[:], scalar=4, op=ALU.logical_shift_right),
                       reads=[posu.b] + g2, writes=[pau.b])
                  S.op("dve", lambda e: e.tensor_single_scalar(out=pbu.t[:], in_=posu.t[:], scalar=15, op=ALU.bitwise_and),
                       reads=[posu.b] + g2, writes=[pbu.b])
                  S.op("dve", lambda e: e.tensor_copy(out=pa.t[:], in_=pau.t[:]), reads=[pau.b], writes=[pa.b])
                  S.op("dve", lambda e: e.tensor_copy(out=pbq.t[:], in_=pbu.t[:]), reads=[pbu.b], writes=[pbq.b])
                  i4 = i16.t[:, :, :].rearrange("p (h two) k -> p h two k", two=2)
                  oh = T8a.t[:, :].rearrange("p (h j a) -> p h j a", h=8, j=16)
                  io4 = iota16.t[:, :].unsqueeze(1).unsqueeze(1).to_broadcast([128, 8, 16, 16])
                  for which, (sel_ap, dst) in enumerate(((pa, eid), (pbq, eid2))):
                      S.op("dve", lambda e, sel_ap=sel_ap: e.tensor_tensor(
                          out=oh, in0=sel_ap.t[:, :, :].unsqueeze(3).to_broadcast([128, 8, 16, 16]), in1=io4, op=ALU.is_equal),
                           reads=[sel_ap.b, iota16.b], writes=[T8a.b])
                      S.op("dve", lambda e, which=which: e.tensor_tensor(
                          out=oh, in0=oh, in1=i4[:, :, which, :].unsqueeze(2).to_broadcast([128, 8, 16, 16]), op=ALU.mult),
                           reads=[T8a.b, i16.b], writes=[T8a.b])
                      S.op("dve", lambda e, dst=dst: e.tensor_reduce(out=dst.t[:], in_=oh, axis=AX.X, op=ALU.add),
                           reads=[T8a.b], writes=[dst.b])
                  S.op("dve", lambda e: e.scalar_tensor_tensor(out=eid.t[:], in0=eid.t[:], scalar=128.0, in1=eid2.t[:],
                                                               op0=ALU.mult, op1=ALU.add), reads=[eid.b, eid2.b], writes=[eid.b])
                  S.op("dve", lambda e: e.tensor_tensor(out=gg.t[:], in0=tsv.t[:], in1=tsv.t[:, :, 0:1].to_broadcast([128, 8, 16]),
                                                        op=ALU.subtract), reads=[tsv.b] + g2, writes=[gg.b])
                  S.op("act", lambda e: e.activation(out=gg.t[:], in_=gg.t[:], func=AF.Exp), reads=[gg.b], writes=[gg.b])
                  S.op("dve", lambda e: e.tensor_reduce(out=gsum.t[:], in_=gg.t[:], axis=AX.X, op=ALU.add),
                       reads=[gg.b], writes=[gsum.b])
                  S.op("dve", lambda e: e.reciprocal(out=gsum.t[:], in_=gsum.t[:]), reads=[gsum.b], writes=[gsum.b])
                  S.op("dve", lambda e: e.tensor_tensor(out=gg.t[:], in0=gg.t[:], in1=gsum.t[:, :].unsqueeze(2).to_broadcast([128, 8, 16]),
                                                        op=ALU.mult), reads=[gg.b, gsum.b], writes=[gg.b])
                  ps, pb = bank()
                  S.op("pe", lambda e, ps=ps: e.transpose(out=ps[:, 0:128], in_=eid.t[:, :, :].rearrange("p h k -> p (h k)"),
                                                          identity=ident.t[:]), reads=[eid.b, ident.b], writes=[pb])
                  S.op("pe", lambda e, ps=ps: e.transpose(out=ps[:, 128:256], in_=gg.t[:, :, :].rearrange("p h k -> p (h k)"),
                                                          identity=ident.t[:]), reads=[gg.b, ident.b], writes=[pb])
                  S.op("dve", lambda e, ps=ps: e.tensor_scalar(out=Aacc.t[:], in0=ps[:, 0:128], scalar1=8388608.0, scalar2=None,
                                                               op0=ALU.add), reads=[pb], writes=[Aacc.b])
                  S.op("dve", lambda e: e.tensor_single_scalar(out=idxTa.t[:, n, :], in_=Aacc.t[:].bitcast(U32), scalar=0x7FFFFF,
                                                               op=ALU.bitwise_and), reads=[Aacc.b], writes=[idxTa.b])
                  S.op("act", lambda e, ps=ps: e.activation(func=AF.Copy, out=gTa.t[:, n, :], in_=ps[:, 128:256]), reads=[pb], writes=[gTa.b])

              c1_front(0)
              for n in range(NB):
                  if n + 1 < NB:
                      c1_front(n + 1)
                  c1_back(n)
              S.barrier()
              S.emit()
        if stop_after >= 3:
          with ExitStack() as es:
            gfrow = load_bcast(es, "gfrow", g_ffn_row_d, D)
            xfs = [alloc(es, "xf%d" % i, [128, D]) for i in range(2)]
            Aacc = alloc(es, "Aacc2", [128, 128])
            AA = alloc(es, "AA", [128, 2, 128])
            Lms = [alloc(es, "Lm%d" % i, [128, 128]) for i in range(3)]
            zcol = alloc(es, "zcol", [128, 1])
            sel = [alloc(es, "sel%d" % i, [128, 128], BF16) for i in range(4)]
            xps = [[alloc(es, "xp%d_%d" % (i, k), [128, D], BF16) for k in range(XB_PASSES)] for i in range(2)]
            NG = 8
            Gd = [alloc(es, "Gd%d" % i, [128, D]) for i in range(NG)]
            Gu = [alloc(es, "Gu%d" % i, [128, D]) for i in range(NG)]
            for lm_ in Lms:
                S.op("pool", lambda e: e.memset(lm_.t[:], 0.0), writes=[lm_.b])
            S.op("pool", lambda e: e.memset(zcol.t[:], 0.0), writes=[zcol.b])
            tokc = [0]
            for n in range(NB):
                xfn = xfs[n % 2]
                zero(ss, ss.t[:, 0:1])
                S.op("act", lambda e: e.activation(out=S1.t[:], in_=H[:, n, :], func=AF.Square, accum_out=ss.t[:, 0:1]),
                     reads=[Hb[n], ss.b], writes=[S1.b, ss.b])
                S.op("act", lambda e: e.activation(out=ss.t[:, 1:2], in_=ss.t[:, 0:1], func=AF.Sqrt, bias=EPS, scale=1.0 / D),
                     reads=[ss.b], writes=[ss.b])
                S.op("dve", lambda e: e.reciprocal(out=ss.t[:, 2:3], in_=ss.t[:, 1:2]), reads=[ss.b], writes=[ss.b])
                S.op("dve", lambda e: e.scalar_tensor_tensor(out=xfn.t[:], in0=H[:, n, :], scalar=ss.t[:, 2:3], in1=gfrow.t[:],
                                                             op0=ALU.mult, op1=ALU.mult),
                     reads=[Hb[n], ss.b, gfrow.b], writes=[xfn.b])
                S.op("pool", lambda e: e.memset(AA.t[:], 0.0), writes=[AA.b])
                xpn = xps[n % 2]
                for k in range(XB_PASSES):
                    src = xfn if k == 0 else S1
                    S.op("dve", lambda e: e.tensor_copy(out=xpn[k].t[:], in_=src.t[:]), reads=[src.b], writes=[xpn[k].b])
                    if k + 1 < XB_PASSES:
                        S.op("dve", lambda e: e.tensor_tensor(out=S1.t[:], in0=src.t[:], in1=xpn[k].t[:], op=ALU.subtract),
                             reads=[src.b, xpn[k].b], writes=[S1.b])
                pd_o, pbs_o = dbank()
                reserved[0] = last_db[0]
                ntok = n_tok_peer
                LA = 2
                fr = {}

                def front(t):
                    k = tokc[0] % NG
                    tokc[0] += 1
                    gd, gu, sl = Gd[k], Gu[k], sel[k % len(sel)]
                    S.dma("pool", lambda e: e.indirect_dma_start(
                        out=gd.t[:, :], out_offset=None, in_=ed_d[:, :],
                        in_offset=bass.IndirectOffsetOnAxis(ap=idxTa.t[:, n, t:t + 1].bitcast(I32), axis=0)), reads=[idxTa.b], writes=[gd.b])
                    S.dma("pool", lambda e: e.indirect_dma_start(
                        out=gu.t[:, :], out_offset=None, in_=eu_d[:, :],
                        in_offset=bass.IndirectOffsetOnAxis(ap=idxTa.t[:, n, t:t + 1].bitcast(I32), axis=0)), reads=[idxTa.b], writes=[gu.b])
                    S.op("act", lambda e: e.activation(func=AF.Copy, out=sl.t[:], in_=ident.t[:, t:t + 1].to_broadcast([128, 128])),
                         reads=[ident.b], writes=[sl.b])
                    pd_x, pbs_x = dbank()
                    for half in range(2):
                        for k in range(XB_PASSES):
                            S.op("pe", lambda e: e.matmul(
                                pd_x[:, half * 512:(half + 1) * 512], lhsT=sl.t[:], rhs=xpn[k].t[:, half * 512:(half + 1) * 512],
                                start=(k == 0), stop=(k == XB_PASSES - 1)), reads=[sl.b, xpn[k].b], writes=[pbs_x[half]])
                    fr[t] = (gd, gu, pd_x, pbs_x)

                def back(t):
                    gd, gu, pd_x, pbs_x = fr.pop(t)
                    lm = Lms[t % 3]
                    for half in range(2):
                        S.op("dve", lambda e: e.scalar_tensor_tensor(
                            out=S1.t[:, half * 512:(half + 1) * 512], in0=gd.t[:, half * 512:(half + 1) * 512], scalar=1.0,
                            in1=pd_x[:, half * 512:(half + 1) * 512], op0=ALU.mult, op1=ALU.mult,
                            accum_out=AA.t[:, half, t:t + 1]),
                             reads=[gd.b, pbs_x[half], AA.b], writes=[S1.b, AA.b])
                    S.op("dve", lambda e: e.tensor_tensor(out=Aacc.t[:, t:t + 1], in0=AA.t[:, 0, t:t + 1], in1=AA.t[:, 1, t:t + 1],
                                                          op=ALU.add), reads=[AA.b], writes=[Aacc.b])
                    S.op("act", lambda e: e.activation(out=Aacc.t[:, t:t + 1], in_=Aacc.t[:, t:t + 1], func=AF.Gelu),
                         reads=[Aacc.b], writes=[Aacc.b])
                    S.op("dve", lambda e: e.tensor_tensor(out=lm.t[:, t:t + 1], in0=Aacc.t[:, t:t + 1], in1=gTa.t[:, n, t:t + 1],
                                                          op=ALU.mult), reads=[Aacc.b, gTa.b], writes=[lm.b])
                    for half in range(2):
                        S.op("pe", lambda e: e.matmul(
                            pd_o[:, half * 512:(half + 1) * 512], lhsT=lm.t[:], rhs=gu.t[:, half * 512:(half + 1) * 512],
                            start=(t == 0), stop=(t == ntok - 1)), reads=[lm.b, gu.b], writes=[pbs_o[half]])
                    if t >= 2:
                        lmo = Lms[(t - 2) % 3]
                        S.op("act", lambda e: e.activation(func=AF.Copy, out=lmo.t[:, t - 2:t - 1], in_=zcol.t[:, 0:1]),
                             reads=[zcol.b], writes=[lmo.b])

                for t in range(min(LA, ntok)):
                    front(t)
                for t in range(ntok):
                    if t + LA < ntok:
                        front(t + LA)
                    back(t)
                for t in range(max(0, ntok - 2), ntok):
                    lmo = Lms[t % 3]
                    S.op("act", lambda e: e.activation(func=AF.Copy, out=lmo.t[:, t:t + 1], in_=zcol.t[:, 0:1]),
                         reads=[zcol.b], writes=[lmo.b])
                reserved[0] = None
                for half in range(2):
                    S.op("dve", lambda e, half=half: e.tensor_tensor(
                        out=H[:, n, half * 512:(half + 1) * 512], in0=H[:, n, half * 512:(half + 1) * 512],
                        in1=pd_o[:, half * 512:(half + 1) * 512], op=ALU.add), reads=[pbs_o[half], Hb[n]], writes=[Hb[n]])
            S.barrier()
            S.emit()
        gfin = load_bcast(esC, "gfin", g_final_d, D)
        for n in range(NB):
            zero(ss, ss.t[:, 0:1])
            S.op("act", lambda e, n=n: e.activation(out=S1.t[:], in_=H[:, n, :], func=AF.Square, accum_out=ss.t[:, 0:1]),
                 reads=[Hb[n], ss.b], writes=[S1.b, ss.b])
            S.op("act", lambda e: e.activation(out=ss.t[:, 1:2], in_=ss.t[:, 0:1], func=AF.Sqrt, bias=EPS, scale=1.0 / D),
                 reads=[ss.b], writes=[ss.b])
            S.op("dve", lambda e: e.reciprocal(out=ss.t[:, 2:3], in_=ss.t[:, 1:2]), reads=[ss.b], writes=[ss.b])
            S.op("dve", lambda e, n=n: e.scalar_tensor_tensor(out=S1.t[:], in0=H[:, n, :], scalar=ss.t[:, 2:3], in1=gfin.t[:],
                                                              op0=ALU.mult, op1=ALU.mult),
                 reads=[Hb[n], ss.b, gfin.b], writes=[S1.b])
            S.dma("sp", lambda e, n=n: e.dma_start(out=out_d[n * 128:(n + 1) * 128, :], in_=S1.t[:]), reads=[S1.b],
                  writes=[Buf("o")])
        S.wait_all_dma("sp")
        S.barrier()
        S.emit()
    S.close()
    es_all.close()
    return nc


def make_in_maps(inp, NB):
    x = np.ascontiguousarray(inp["x"], dtype=np.float32)
    B, SEQ, _ = x.shape
    seg = NB * 128
    assert SEQ == NSEG * seg
    NPRE = (NSEG - 1) * NB

    def f(a):
        return np.ascontiguousarray(np.asarray(a, dtype=np.float32))

    shared = {
        "g_mix": f(inp["g_mix"][0]).reshape(8, 128),
        "w_in": f(inp["w_in"][0]),
        "conv_w": f(inp["conv_w"][0]),
        "conv_b": f(inp["conv_b"][0]).reshape(4, 128),
        "gateb": np.concatenate([f(inp["b_igate"][0]), f(inp["b_fgate"][0])]).reshape(1, 8),
        "g_mhead": f(inp["g_mhead"][0]).reshape(1, 512),
        "sinks": f(inp["sinks"][0]).reshape(1, 8),
        "w_out": f(inp["w_out"][0]),
        "g_cross": f(inp["g_cross"][0]).reshape(8, 128),
        "g_mem": f(inp["g_mem"][0]).reshape(8, 128),
        "w_xq": f(inp["w_xq"][0]), "w_xk": f(inp["w_xk"][0]), "w_xv": f(inp["w_xv"][0]), "w_xo": f(inp["w_xo"][0]),
        "g_ffn": f(inp["g_ffn"][0]).reshape(8, 128),
        "g_ffn_row": f(inp["g_ffn"][0]).reshape(1, D),
        "w_pq": f(inp["w_pq"][0]),
        "sub_keys1": f(inp["sub_keys1"][0]), "sub_keys2": f(inp["sub_keys2"][0]),
        "expert_down": f(inp["expert_down"][0]), "expert_up": f(inp["expert_up"][0]),
        "g_final": f(inp["g_final"]).reshape(1, D),
    }
    mem = f(inp["mem"])
    maps = []
    for c in range(B * NSEG):
        b, j = divmod(c, NSEG)
        xp = np.zeros((NPRE * 128, D), np.float32)
        pm = np.zeros((1, NPRE), np.float32)
        if j > 0:
            xp[NPRE * 128 - j * seg:] = x[b, :j * seg]
            pm[0, NPRE - j * NB:] = 1.0
        m = dict(shared)
        m["xs"] = np.ascontiguousarray(x[b, j * seg:(j + 1) * seg])
        m["xp"] = xp
        m["pm"] = pm
        m["mem"] = np.ascontiguousarray(mem[b])
        maps.append(m)
    return maps


_NC_CACHE = {}


def kernel(**inputs):
    x = np.asarray(inputs["x"])
    B, SEQ, _ = x.shape
    NB = SEQ // (NSEG * 128)
    key = (NB,)
    if key not in _NC_CACHE:
        _NC_CACHE[key] = build(NB)
    nc = _NC_CACHE[key]
    maps = make_in_maps(inputs, NB)
    res = run_bass_kernel_spmd(nc, maps, core_ids=list(range(B * NSEG)))
    out = np.zeros((B, SEQ, D), np.float32)
    seg = NB * 128
    for c in range(B * NSEG):
        b, j = divmod(c, NSEG)
        out[b, j * seg:(j + 1) * seg] = res.results[c]["out"]
    return out
```

```python
import numpy as np
from contextlib import ExitStack
import concourse.bass as bass
import concourse.mybir as mybir
from concourse.bass_utils import run_bass_kernel_spmd

F32 = mybir.dt.float32
BF16 = mybir.dt.bfloat16
XB_PASSES = 3
I32 = mybir.dt.int32
U32 = mybir.dt.uint32
AF = mybir.ActivationFunctionType
ALU = mybir.AluOpType
AX = mybir.AxisListType

D = 1024
P_IN = 2312
EPS = 1e-6
NEG = -30000.0
NSEG = 4
SAME_ENGINE_SYNC = True
DBG = 0


class Buf:
    __slots__ = ("name", "last_write", "readers", "exclusive")

    def __init__(self, name, exclusive=False):
        self.name = name
        self.last_write = None
        self.readers = {}
        self.exclusive = exclusive


class TB:
    __slots__ = ("t", "b")

    def __init__(self, t, name):
        self.t = t
        self.b = Buf(name)


class _Rec:
    def __getattr__(self, name):
        def call(*a, **k):
            return (name, a, k)
        return call


_REC = _Rec()


class Sync:
    ENGS = ("pe", "act", "dve", "pool", "sp")

    def __init__(self, nc, n_dma_sems=32):
        self.nc = nc
        self.sems = {}
        self.counts = {}
        self.seen = {e: {} for e in self.ENGS}
        self.prog = {e: [] for e in self.ENGS}
        self._cms = []
        for name in self.ENGS:
            self._new_sem(name)
        self.dma_keys = []
        for i in range(n_dma_sems):
            k = "dma%d" % i
            self._new_sem(k)
            self.dma_keys.append(k)
        self.pdma_keys = []
        for i in range(8):
            k = "pdma%d" % i
            self._new_sem(k)
            self.pdma_keys.append(k)
        self.dma_rr = 0
        self.pdma_rr = 0
        self.dma_inflight = {k: None for k in self.dma_keys + self.pdma_keys}

    def _new_sem(self, key):
        cm = self.nc.semaphore("s_" + key)
        h = cm.__enter__()
        self._cms.append(cm)
        self.sems[key] = h
        self.counts[key] = 0

    def close(self):
        for cm in reversed(self._cms):
            cm.__exit__(None, None, None)

    def _need(self, ename, ev):
        if ev is None:
            return
        key, val = ev
        if key == ename and (ename == "pe" or not SAME_ENGINE_SYNC):
            return
        if self.seen[ename].get(key, 0) >= val:
            return
        self.prog[ename].append(("wait", key, val))
        self.seen[ename][key] = val

    def _deps(self, ename, reads, writes):
        for b in reads:
            self._need(ename, b.last_write)
        for b in writes:
            self._need(ename, b.last_write)
            for k, v in list(b.readers.items()):
                self._need(ename, (k, v))

    def op(self, ename, fn, reads=(), writes=()):
        writes = list(writes) + [b for b in reads if b.exclusive and b not in writes]
        self._deps(ename, reads, writes)
        self.counts[ename] += 1
        self.prog[ename].append(("ins", fn(_REC), ename, 1))
        ev = (ename, self.counts[ename])
        for b in reads:
            b.readers[ename] = ev[1]
        for b in writes:
            b.last_write = ev
            b.readers = {}
        return ev

    def dma(self, qname, fn, reads=(), writes=()):
        self._deps(qname, reads, writes)
        if qname == "pool":
            k = self.pdma_keys[self.pdma_rr]
            self.pdma_rr = (self.pdma_rr + 1) % len(self.pdma_keys)
        else:
            k = self.dma_keys[self.dma_rr]
            self.dma_rr = (self.dma_rr + 1) % len(self.dma_keys)
        prev = self.dma_inflight[k]
        if prev is not None:
            self._need(qname, prev)
        self.counts[k] += 16
        self.prog[qname].append(("ins", fn(_REC), k, 16))
        ev = (k, self.counts[k])
        self.dma_inflight[k] = ev
        for b in reads:
            b.readers[k] = ev[1]
        for b in writes:
            b.last_write = ev
            b.readers = {}
        return ev

    def wait_all_dma(self, ename):
        for k in self.dma_keys + self.pdma_keys:
            if self.dma_inflight[k] is not None:
                self._need(ename, self.dma_inflight[k])

    def barrier(self):
        for e in self.ENGS:
            for k in self.sems:
                if self.counts[k] > 0:
                    self._need(e, (k, self.counts[k]))

    def emit(self):
        nc = self.nc
        sems = self.sems
        prog = self.prog

        def replay(eng, items):
            for it in items:
                if it[0] == "wait":
                    eng.wait_ge(sems[it[1]], it[2])
                else:
                    name, a, k = it[1]
                    ins = getattr(eng, name)(*a, **k)
                    ins.then_inc(sems[it[2]], it[3])

        with nc.allow_low_precision("exact multi-term bf16 split of an fp32 operand (fp32-emulating)"), nc.Block() as block:
            @block.tensor
            def _(e):
                replay(e, prog["pe"])

            @block.scalar
            def _(e):
                replay(e, prog["act"])

            @block.vector
            def _(e):
                replay(e, prog["dve"])

            @block.gpsimd
            def _(e):
                replay(e, prog["pool"])

            @block.sync
            def _(e):
                replay(e, prog["sp"])
        self.prog = {e: [] for e in self.ENGS}


def build(NB, stop_after=3, n_tok_peer=128):
    NPRE = (NSEG - 1) * NB
    nc = bass.Bass("TRN2", target_bir_lowering=False)

    def din(name, shape, dt=F32):
        return nc.dram_tensor(name, list(shape), dt, kind="ExternalInput").ap()

    xs_d = din("xs", [NB * 128, D])
    xp_d = din("xp", [NPRE * 128, D])
    pm_d = din("pm", [1, NPRE])
    mem_d = din("mem", [256, D])
    g_mix_d = din("g_mix", [8, 128])
    w_in_d = din("w_in", [D, P_IN])
    conv_w_d = din("conv_w", [4, 512])
    conv_b_d = din("conv_b", [4, 128])
    gb_d = din("gateb", [1, 8])
    g_mhead_d = din("g_mhead", [1, 512])
    sinks_d = din("sinks", [1, 8])
    w_out_d = din("w_out", [D, D])
    g_cross_d = din("g_cross", [8, 128])
    g_mem_d = din("g_mem", [8, 128])
    w_xq_d = din("w_xq", [D, D])
    w_xk_d = din("w_xk", [D, D])
    w_xv_d = din("w_xv", [D, D])
    w_xo_d = din("w_xo", [D, D])
    g_ffn_d = din("g_ffn", [8, 128])
    g_ffn_row_d = din("g_ffn_row", [1, D])
    w_pq_d = din("w_pq", [D, 2048])
    sk1_d = din("sub_keys1", [128, 128])
    sk2_d = din("sub_keys2", [128, 128])
    ed_d = din("expert_down", [16384, D])
    eu_d = din("expert_up", [16384, D])
    g_final_d = din("g_final", [1, D])
    out_d = nc.dram_tensor("out", [NB * 128, D], F32, kind="ExternalOutput").ap()

    S = Sync(nc)
    es_all = ExitStack()

    def alloc(es, name, shape, dt=F32):
        return TB(es.enter_context(nc.sbuf_tensor("sb_" + name, list(shape), dt)), name)

    H = es_all.enter_context(nc.sbuf_tensor("sb_H", [128, NB, D], F32))
    Hb = [Buf("H%d" % i) for i in range(NB)]
    ident = alloc(es_all, "ident", [128, 128])
    Umat = alloc(es_all, "Umat", [128, 128])
    ones = alloc(es_all, "ones", [128, 128])
    mcur = alloc(es_all, "mcur", [128, 128])
    mprev = alloc(es_all, "mprev", [128, 128])
    mprev0 = alloc(es_all, "mprev0", [128, 128])
    ss = alloc(es_all, "ss", [128, 4])
    PSD = [es_all.enter_context(nc.psum_tensor("psd%d" % i, [128, 1024], F32)) for i in range(4)]
    PSb = [Buf("ps%d" % i, exclusive=True) for i in range(8)]
    bank_ptr = [0]
    last_db = [0]
    reserved = [None]

    def bank():
        k = bank_ptr[0] % 8
        if reserved[0] is not None and k // 2 == reserved[0]:
            bank_ptr[0] += 2 - (k % 2)
            k = bank_ptr[0] % 8
        bank_ptr[0] += 1
        return PSD[k // 2][:, (k % 2) * 512:(k % 2) * 512 + 512], PSb[k]

    def dbank():
        if bank_ptr[0] % 2:
            bank_ptr[0] += 1
        k = bank_ptr[0] % 8
        if reserved[0] is not None and k // 2 == reserved[0]:
            bank_ptr[0] += 2
            k = bank_ptr[0] % 8
        bank_ptr[0] += 2
        last_db[0] = k // 2
        return PSD[k // 2], [PSb[k], PSb[k + 1]]

    def dmaq():
        return "sp"

    S.op("pool", lambda e: e.memset(ident.t[:], 0.0), writes=[ident.b])
    S.op("pool", lambda e: e.affine_select(out=ident.t[:], in_=ident.t[:], pattern=[[-1, 128]],
                                           compare_op=ALU.not_equal, fill=1.0, base=0, channel_multiplier=1),
         reads=[ident.b], writes=[ident.b])
    S.op("pool", lambda e: e.memset(ones.t[:], 1.0), writes=[ones.b])
    S.op("pool", lambda e: e.memset(Umat.t[:], 1.0), writes=[Umat.b])
    S.op("pool", lambda e: e.affine_select(out=Umat.t[:], in_=Umat.t[:], pattern=[[1, 128]],
                                           compare_op=ALU.is_ge, fill=0.0, base=0, channel_multiplier=-1),
         reads=[Umat.b], writes=[Umat.b])
    S.op("pool", lambda e: e.memset(mcur.t[:], 0.0), writes=[mcur.b])
    S.op("pool", lambda e: e.affine_select(out=mcur.t[:], in_=mcur.t[:], pattern=[[1, 128]],
                                           compare_op=ALU.is_ge, fill=NEG, base=0, channel_multiplier=-1),
         reads=[mcur.b], writes=[mcur.b])
    S.op("pool", lambda e: e.memset(mprev.t[:], 0.0), writes=[mprev.b])
    S.op("pool", lambda e: e.affine_select(out=mprev.t[:], in_=mprev.t[:], pattern=[[-1, 128]],
                                           compare_op=ALU.is_gt, fill=NEG, base=0, channel_multiplier=1),
         reads=[mprev.b], writes=[mprev.b])

    def zero(tb, ap):
        S.op("pool", lambda e: e.memset(ap, 0.0), writes=[tb.b])

    def load_colvec(es, name, src_d, ncol):
        raw = alloc(es, name + "_raw", [ncol, 128])
        dst = alloc(es, name, [128, ncol])
        S.dma(dmaq(), lambda e: e.dma_start(out=raw.t[:], in_=src_d), writes=[raw.b])
        ps, pb = bank()
        S.op("pe", lambda e: e.transpose(out=ps[:, 0:ncol], in_=raw.t[:], identity=ident.t[0:ncol, 0:ncol]),
             reads=[raw.b, ident.b], writes=[pb])
        S.op("dve", lambda e: e.tensor_copy(out=dst.t[:], in_=ps[:, 0:ncol]), reads=[pb], writes=[dst.b])
        return dst

    def load_bcast(es, name, src_d, n):
        t = alloc(es, name, [128, n])
        S.dma(dmaq(), lambda e: e.dma_start(out=t.t[:], in_=src_d.partition_broadcast(128)), writes=[t.b])
        return t

    def norm_T(src_ap, src_b, gT, S1, xT, tok_out=None, g_row=None):
        zero(ss, ss.t[:, 0:1])
        S.op("act", lambda e: e.activation(out=S1.t[:], in_=src_ap, func=AF.Square, accum_out=ss.t[:, 0:1]),
             reads=[src_b, ss.b], writes=[S1.b, ss.b])
        S.op("act", lambda e: e.activation(out=ss.t[:, 1:2], in_=ss.t[:, 0:1], func=AF.Sqrt, bias=EPS, scale=1.0 / D),
             reads=[ss.b], writes=[ss.b])
        S.op("dve", lambda e: e.reciprocal(out=ss.t[:, 2:3], in_=ss.t[:, 1:2]), reads=[ss.b], writes=[ss.b])
        S.op("act", lambda e: e.activation(out=S1.t[:], in_=src_ap, func=AF.Copy, scale=ss.t[:, 2:3]),
             reads=[src_b, ss.b], writes=[S1.b])
        if tok_out is not None:
            S.op("dve", lambda e: e.tensor_tensor(out=tok_out.t[:], in0=S1.t[:], in1=g_row.t[:], op=ALU.mult),
                 reads=[S1.b, g_row.b], writes=[tok_out.b])
        pd, pbs = dbank()
        for c in range(8):
            S.op("pe", lambda e, c=c: e.transpose(out=pd[:, c * 128:(c + 1) * 128], in_=S1.t[:, c * 128:(c + 1) * 128],
                                                  identity=ident.t[:]),
                 reads=[S1.b, ident.b], writes=[pbs[c // 4]])
        for k in range(2):
            S.op("dve", lambda e, k=k: e.tensor_tensor(
                out=xT.t[:, 4 * k:4 * k + 4, :],
                in0=pd[:, k * 512:(k + 1) * 512].rearrange("p (c t) -> p c t", c=4),
                in1=gT.t[:, 4 * k:4 * k + 4].unsqueeze(2).to_broadcast([128, 4, 128]), op=ALU.mult),
                 reads=[pbs[k], gT.b], writes=[xT.b])

    def transpose8(src, dstT, eng="dve"):
        pd, pbs = dbank()
        for c in range(8):
            S.op("pe", lambda e, c=c: e.transpose(out=pd[:, c * 128:(c + 1) * 128], in_=src.t[:, c * 128:(c + 1) * 128],
                                                  identity=ident.t[:]),
                 reads=[src.b, ident.b], writes=[pbs[c // 4]])
        S.op("dve", lambda e: e.tensor_copy(out=dstT.t[:, 0:4, :], in_=pd[:, 0:512].rearrange("p (c t) -> p c t", c=4)),
             reads=[pbs[0]], writes=[dstT.b])
        S.op("act", lambda e: e.activation(func=AF.Copy, out=dstT.t[:, 4:8, :], in_=pd[:, 512:1024].rearrange("p (c t) -> p c t", c=4)),
             reads=[pbs[1]], writes=[dstT.b])

    def out_proj_add(srcT, W, n):
        for half in range(2):
            ps, pb = bank()
            for cc in range(8):
                S.op("pe", lambda e, cc=cc, half=half, ps=ps: e.matmul(
                    ps[:, :], lhsT=srcT.t[:, cc, :], rhs=W.t[:, cc, half * 512:(half + 1) * 512],
                    start=(cc == 0), stop=(cc == 7)), reads=[srcT.b, W.b], writes=[pb])
            S.op("dve", lambda e, half=half, ps=ps: e.tensor_tensor(
                out=H[:, n, half * 512:(half + 1) * 512], in0=H[:, n, half * 512:(half + 1) * 512], in1=ps[:, :],
                op=ALU.add), reads=[pb, Hb[n]], writes=[Hb[n]])

    joinT = alloc(es_all, "joinT", [128, 1])

    def load_w(W, src_d, ncols, col0=0):
        S._deps("sp", [], [W.b])
        cbs = [Buf("wchunk") for _ in range(8)]
        for dc in range(8):
            S.dma(dmaq(), lambda e, dc=dc: e.dma_start(out=W.t[:, dc, col0:col0 + ncols],
                                                        in_=src_d[dc * 128:(dc + 1) * 128, 0:ncols]),
                  writes=[cbs[dc]])
        S.op("pool", lambda e: e.memset(joinT.t[:], 0.0), reads=cbs, writes=[W.b, joinT.b])

    with ExitStack() as es:
        Win = alloc(es, "Win", [128, 8, P_IN])
        Wout = alloc(es, "Wout", [128, 8, D])
        load_w(Win, w_in_d, P_IN)
        load_w(Wout, w_out_d, D)
        gmixT = load_colvec(es, "gmixT", g_mix_d, 8)
        cwraw = alloc(es, "cwraw", [4, 512])
        S.dma(dmaq(), lambda e: e.dma_start(out=cwraw.t[:], in_=conv_w_d), writes=[cwraw.b])
        convw = alloc(es, "convw", [128, 4, 4])
        ps, pb = bank()
        for ch in range(4):
            S.op("pe", lambda e, ch=ch: e.transpose(out=ps[:, ch * 4:ch * 4 + 4], in_=cwraw.t[:, ch * 128:(ch + 1) * 128],
                                                    identity=ident.t[0:4, 0:4]), reads=[cwraw.b, ident.b], writes=[pb])
        S.op("dve", lambda e: e.tensor_copy(out=convw.t[:], in_=ps[:, 0:16].rearrange("p (c j) -> p c j", c=4)),
             reads=[pb], writes=[convw.b])
        convb = load_colvec(es, "convb", conv_b_d, 4)
        gateb = load_bcast(es, "gateb", gb_d, 8)
        gmh = load_bcast(es, "gmh", g_mhead_d, 512)
        esink = load_bcast(es, "esink", sinks_d, 8)
        S.op("act", lambda e: e.activation(out=esink.t[:], in_=esink.t[:], func=AF.Exp), reads=[esink.b], writes=[esink.b])
        pmask = load_bcast(es, "pmask", pm_d, NPRE)
        pmm = alloc(es, "pmm", [128, 1])
        S.op("dve", lambda e: e.tensor_scalar(out=pmm.t[:], in0=pmask.t[:, NPRE - 1:NPRE], scalar1=-1.0, scalar2=-NEG,
                                              op0=ALU.add, op1=ALU.mult), reads=[pmask.b], writes=[pmm.b])
        S.op("dve", lambda e: e.tensor_scalar(out=mprev0.t[:], in0=mprev.t[:], scalar1=pmm.t[:, 0:1], scalar2=None,
                                              op0=ALU.add), reads=[mprev.b, pmm.b], writes=[mprev0.b])

        S1 = alloc(es, "S1", [128, D])
        xT = alloc(es, "xT", [128, 8, 128])
        pre = alloc(es, "pre", [128, 4, 131])
        qkT = alloc(es, "qkT", [128, 4, 128])
        vext0 = alloc(es, "vext", [128, 4, 129])
        aqT = alloc(es, "aqT", [128, 8, 128])
        kTs = alloc(es, "kTs", [128, 2, 2, 128])
        vsw = alloc(es, "vsw", [128, 2, 2, 65])
        PT = alloc(es, "PT", [128, 512])
        PT2 = alloc(es, "PT2", [128, 512])
        cacc = TB(PT.t[:, :].rearrange("p (c t) -> p c t", c=4), "cacc_alias")
        cacc.b = PT.b
        ctmp = TB(PT2.t[:, :].rearrange("p (c t) -> p c t", c=4), "ctmp_alias")
        ctmp.b = PT2.b
        gsig = PT2
        LFB = cacc
        DT = alloc(es, "DT", [128, 128])
        ST = alloc(es, "ST", [128, 128])
        num = alloc(es, "num", [128, 129])
        its = alloc(es, "its", [128, 129])
        kw = alloc(es, "kw", [128, 128])
        CT = alloc(es, "CT", [128, 2, 129])
        gs0 = alloc(es, "gs", [128, 48])
        gs1 = alloc(es, "gs1", [128, 48])
        gss = [gs0, gs1]
        sm = alloc(es, "sm", [128, 16])
        S.op("pool", lambda e: e.memset(pre.t[:], 0.0), writes=[pre.b])
        S.op("pool", lambda e: e.memset(CT.t[:], 0.0), writes=[CT.b])
        vext1 = TB(aqT.t[:, :, :].rearrange("p a b -> p (a b)")[:, 0:516].rearrange("p (h v) -> p h v", h=4), "vext1_alias")
        vext1.b = aqT.b
        vexts = [vext0, vext1]
        for vx in vexts:
            S.op("pool", lambda e: e.memset(vx.t[:], 1.0), writes=[vx.b])
        S.op("pool", lambda e: e.memset(vsw.t[:], 1.0), writes=[vsw.b])
        LN8 = float(np.log(0.125))

        def mixer_block(src_d, row0, n, light, slot_mask_col, with_swa_kv, part="ALL"):
            slot = n % 2
            vext = vexts[n % 2] if light else vexts[0]
            gs = gss[n % 2] if light else gss[0]
            if part in ("ALL", "F"):
                xap, xb = H[:, n % NB, :], Hb[n % NB]
                S.dma(dmaq(), lambda e: e.dma_start(out=xap, in_=src_d[row0:row0 + 128, :]), writes=[xb])
                norm_T(xap, xb, gmixT, S1, xT)
                ps_g, pb_g = bank()
                for dc in range(8):
                    S.op("pe", lambda e, dc=dc: e.matmul(ps_g[:, 0:8], lhsT=xT.t[:, dc, :], rhs=Win.t[:, dc, 1536:1544],
                                                         start=(dc == 0), stop=(dc == 7)), reads=[Win.b, xT.b], writes=[pb_g])
                if with_swa_kv:
                    for dc in range(8):
                        S.op("pe", lambda e, dc=dc: e.matmul(ps_g[:, 128:256], lhsT=xT.t[:, dc, :], rhs=Win.t[:, dc, 2184:2312],
                                                             start=(dc == 0), stop=(dc == 7)), reads=[Win.b, xT.b], writes=[pb_g])
                    for kv in range(2):
                        for dc in range(8):
                            S.op("pe", lambda e, dc=dc, kv=kv: e.matmul(
                                ps_g[0:64, 256 + kv * 128:256 + (kv + 1) * 128],
                                lhsT=Win.t[:, dc, 2056 + kv * 64:2056 + (kv + 1) * 64], rhs=xT.t[:, dc, :],
                                start=(dc == 0), stop=(dc == 7)), reads=[Win.b, xT.b], writes=[pb_g])
                    S.op("dve", lambda e: e.tensor_copy(out=vsw.t[:, slot, :, 0:64],
                                                        in_=ps_g[:, 128:256].rearrange("p (k d) -> p k d", k=2)),
                         reads=[pb_g], writes=[vsw.b])
                    S.op("dve", lambda e: e.tensor_copy(out=kTs.t[0:64, slot, :, :],
                                                        in_=ps_g[0:64, 256:512].rearrange("p (k t) -> p k t", k=2)),
                         reads=[pb_g], writes=[kTs.b])
                S.op("dve", lambda e: e.tensor_tensor(out=gs.t[:, 0:8], in0=ps_g[:, 0:8], in1=gateb.t[:, 0:8], op=ALU.add),
                     reads=[pb_g, gateb.b], writes=[gs.b])
                ps_qk, pb_qk = bank()
                for ch in (range(4) if (not light or with_swa_kv) else (2, 3)):
                    for dc in range(8):
                        S.op("pe", lambda e, ch=ch, dc=dc: e.matmul(
                            ps_qk[:, ch * 128:(ch + 1) * 128], lhsT=Win.t[:, dc, ch * 128:(ch + 1) * 128], rhs=xT.t[:, dc, :],
                            start=(dc == 0), stop=(dc == 7)), reads=[Win.b, xT.b], writes=[pb_qk])
                c0 = 0 if (not light or with_swa_kv) else 2
                S.op("act", lambda e: e.activation(func=AF.Copy, out=pre.t[:, c0:4, 3:131],
                                                   in_=ps_qk[:, c0 * 128:512].rearrange("p (c t) -> p c t", c=4 - c0)),
                     reads=[pb_qk], writes=[pre.b])
                ps_v, pb_v = bank()
                for dc in range(8):
                    S.op("pe", lambda e, dc=dc: e.matmul(ps_v[:, :], lhsT=xT.t[:, dc, :], rhs=Win.t[:, dc, 512:1024],
                                                         start=(dc == 0), stop=(dc == 7)), reads=[Win.b, xT.b], writes=[pb_v])
                S.op("dve", lambda e: e.tensor_copy(out=vext.t[:, :, 0:128], in_=ps_v[:, :].rearrange("p (h v) -> p h v", h=4)),
                     reads=[pb_v], writes=[vext.b])
            if part in ("ALL", "G"):
                S.op("act", lambda e: e.activation(out=gs.t[:, 24:28], in_=gs.t[:, 4:8], func=AF.Exp, scale=-1.0),
                     reads=[gs.b], writes=[gs.b])
                S.op("act", lambda e: e.activation(out=gs.t[:, 24:28], in_=gs.t[:, 24:28], func=AF.Ln, bias=1.0),
                     reads=[gs.b], writes=[gs.b])
                S.op("dve", lambda e: e.tensor_scalar(out=gs.t[:, 4:8], in0=gs.t[:, 24:28], scalar1=-1.0, scalar2=None,
                                                      op0=ALU.mult), reads=[gs.b], writes=[gs.b])
                ps_b, pb_b = bank()
                S.op("pe", lambda e: e.matmul(ps_b[:, 0:4], lhsT=Umat.t[:], rhs=gs.t[:, 4:8], start=True, stop=True),
                     reads=[Umat.b, gs.b], writes=[pb_b])
                S.op("pe", lambda e: e.matmul(ps_b[:, 4:8], lhsT=ones.t[:], rhs=gs.t[:, 4:8], start=True, stop=True),
                     reads=[ones.b, gs.b], writes=[pb_b])
                S.op("dve", lambda e: e.tensor_tensor(out=gs.t[:, 12:16], in0=gs.t[:, 0:4], in1=ps_b[:, 0:4], op=ALU.subtract),
                     reads=[gs.b, pb_b], writes=[gs.b])
                S.op("dve", lambda e: e.tensor_tensor(out=gs.t[:, 24:28], in0=gs.t[:, 12:16], in1=ps_b[:, 4:8], op=ALU.add),
                     reads=[gs.b, pb_b], writes=[gs.b])
                S.op("act", lambda e: e.activation(out=gs.t[:, 16:20], in_=gs.t[:, 24:28], func=AF.Exp),
                     reads=[gs.b], writes=[gs.b])
                if slot_mask_col is not None:
                    S.op("dve", lambda e: e.tensor_scalar(out=gs.t[:, 16:20], in0=gs.t[:, 16:20],
                                                          scalar1=pmask.t[:, slot_mask_col:slot_mask_col + 1], scalar2=None,
                                                          op0=ALU.mult), reads=[gs.b, pmask.b], writes=[gs.b])
                S.op("act", lambda e: e.activation(out=gs.t[:, 20:24], in_=ps_b[:, 4:8], func=AF.Exp),
                     reads=[pb_b], writes=[gs.b])
                if not light:
                    S.op("act", lambda e: e.activation(out=gs.t[:, 8:12], in_=ps_b[:, 0:4], func=AF.Exp, bias=LN8),
                         reads=[pb_b], writes=[gs.b])
                    S.op("dve", lambda e: e.tensor_scalar(out=gs.t[:, 12:16], in0=gs.t[:, 12:16], scalar1=LN8, scalar2=None,
                                                          op0=ALU.add), reads=[gs.b], writes=[gs.b])
                for j in range(4):
                    wj = convw.t[:, :, j:j + 1].to_broadcast([128, 4, 128])
                    if j == 0:
                        S.op("dve", lambda e, wj=wj: e.tensor_tensor(out=cacc.t[:], in0=pre.t[:, :, 0:128], in1=wj, op=ALU.mult),
                             reads=[pre.b, convw.b], writes=[cacc.b])
                    else:
                        S.op("dve", lambda e, wj=wj, j=j: e.tensor_tensor(out=ctmp.t[:], in0=pre.t[:, :, j:j + 128], in1=wj,
                                                                           op=ALU.mult),
                             reads=[pre.b, convw.b], writes=[ctmp.b])
                        S.op("dve", lambda e: e.tensor_tensor(out=cacc.t[:], in0=cacc.t[:], in1=ctmp.t[:], op=ALU.add),
                             reads=[cacc.b, ctmp.b], writes=[cacc.b])
                S.op("dve", lambda e: e.tensor_tensor(out=cacc.t[:], in0=cacc.t[:],
                                                      in1=convb.t[:, :].unsqueeze(2).to_broadcast([128, 4, 128]), op=ALU.add),
                     reads=[cacc.b, convb.b], writes=[cacc.b])
                S.op("act", lambda e: e.activation(out=qkT.t[:], in_=cacc.t[:], func=AF.Silu), reads=[cacc.b], writes=[qkT.b])
                S.op("dve", lambda e: e.tensor_copy(out=pre.t[:, :, 0:3], in_=pre.t[:, :, 128:131]), reads=[pre.b], writes=[pre.b])

            if part == "ALL":
                if not light:
                    S.op("dve", lambda e: e.tensor_copy(out=LFB.t[:], in_=gs.t[:, 4:8].unsqueeze(2).to_broadcast([128, 4, 128])),
                         reads=[gs.b], writes=[LFB.b])
                    ps_o, pb_o = bank()
                    for dc in range(8):
                        S.op("pe", lambda e, dc=dc: e.matmul(ps_o[:, :], lhsT=xT.t[:, dc, :], rhs=Win.t[:, dc, 1024:1536],
                                                             start=(dc == 0), stop=(dc == 7)), reads=[Win.b, xT.b], writes=[pb_o])
                    S.op("act", lambda e: e.activation(out=gsig.t[:], in_=ps_o[:, :], func=AF.Sigmoid), reads=[pb_o], writes=[gsig.b])
                    S.op("dve", lambda e: e.tensor_tensor(out=gsig.t[:], in0=gsig.t[:], in1=gmh.t[:], op=ALU.mult),
                         reads=[gsig.b, gmh.b], writes=[gsig.b])
                    pd_q, pbs_q = dbank()
                    for hq in range(8):
                        for dc in range(8):
                            S.op("pe", lambda e, hq=hq, dc=dc: e.matmul(
                                pd_q[0:64, hq * 128:(hq + 1) * 128], lhsT=Win.t[:, dc, 1544 + hq * 64:1544 + (hq + 1) * 64],
                                rhs=xT.t[:, dc, :], start=(dc == 0), stop=(dc == 7)),
                                 reads=[Win.b, xT.b], writes=[pbs_q[hq // 4]])
                    S.op("dve", lambda e: e.tensor_copy(out=aqT.t[0:64, 0:4, :], in_=pd_q[0:64, 0:512].rearrange("p (h t) -> p h t", h=4)),
                         reads=[pbs_q[0]], writes=[aqT.b])
                    S.op("dve", lambda e: e.tensor_copy(out=aqT.t[0:64, 4:8, :],
                                                        in_=pd_q[0:64, 512:1024].rearrange("p (h t) -> p h t", h=4)),
                         reads=[pbs_q[1]], writes=[aqT.b])
                    mixb = S1
                    for h in range(4):
                        c, hh = h // 2, h % 2
                        p0, p1 = hh * 64, hh * 64 + 64
                        ps_d, pb_d = bank()
                        S.op("pe", lambda e, h=h, ps_d=ps_d: e.matmul(ps_d[:, 0:128], lhsT=LFB.t[:, h, :], rhs=Umat.t[:],
                                                                      start=True, stop=False),
                             reads=[LFB.b, Umat.b], writes=[pb_d])
                        S.op("pe", lambda e, ps_d=ps_d: e.matmul(ps_d[:, 0:128], lhsT=ident.t[:], rhs=mcur.t[:], start=False, stop=True),
                             reads=[ident.b, mcur.b], writes=[pb_d])
                        S.op("pe", lambda e, ps_d=ps_d, c=c, p0=p0, p1=p1: e.matmul(
                            ps_d[:, 128:256], lhsT=qkT.t[p0:p1, 2 + c, :], rhs=qkT.t[p0:p1, c, :], start=True, stop=True),
                             reads=[qkT.b], writes=[pb_d])
                        S.op("act", lambda e, ps_d=ps_d, h=h: e.activation(out=DT.t[:], in_=ps_d[:, 0:128], func=AF.Exp,
                                                                           bias=gs.t[:, 12 + h:13 + h]),
                             reads=[pb_d, gs.b], writes=[DT.b])
                        S.op("dve", lambda e, ps_d=ps_d: e.tensor_tensor(out=ST.t[:], in0=ps_d[:, 128:256], in1=DT.t[:], op=ALU.mult),
                             reads=[pb_d, DT.b], writes=[ST.b])
                        ps_n, pb_n = bank()
                        S.op("pe", lambda e, ps_n=ps_n, h=h: e.matmul(ps_n[:, 0:129], lhsT=ST.t[:], rhs=vext.t[:, h, :],
                                                                      start=True, stop=True), reads=[ST.b, vext.b], writes=[pb_n])
                        S.op("pe", lambda e, ps_n=ps_n, c=c, p0=p0, p1=p1: e.matmul(
                            ps_n[:, 256:385], lhsT=qkT.t[p0:p1, c, :], rhs=CT.t[p0:p1, c, :], start=True, stop=True),
                             reads=[qkT.b, CT.b], writes=[pb_n])
                        S.op("act", lambda e, ps_n=ps_n, h=h: e.activation(out=its.t[:], in_=ps_n[:, 256:385], func=AF.Copy,
                                                                           scale=gs.t[:, 8 + h:9 + h]),
                             reads=[pb_n, gs.b], writes=[its.b])
                        S.op("dve", lambda e, ps_n=ps_n: e.tensor_tensor(out=num.t[:], in0=ps_n[:, 0:129], in1=its.t[:], op=ALU.add),
                             reads=[pb_n, its.b], writes=[num.b])
                        S.op("act", lambda e: e.activation(out=sm.t[:, 0:1], in_=num.t[:, 128:129], func=AF.Abs),
                             reads=[num.b], writes=[sm.b])
                        S.op("dve", lambda e: e.tensor_scalar(out=sm.t[:, 0:1], in0=sm.t[:, 0:1], scalar1=1.0, scalar2=None, op0=ALU.max),
                             reads=[sm.b], writes=[sm.b])
                        S.op("dve", lambda e: e.reciprocal(out=sm.t[:, 1:2], in_=sm.t[:, 0:1]), reads=[sm.b], writes=[sm.b])
                        zero(sm, sm.t[:, 2:3])
                        S.op("act", lambda e: e.activation(out=its.t[:, 0:128], in_=num.t[:, 0:128], func=AF.Square,
                                                           scale=sm.t[:, 1:2], accum_out=sm.t[:, 2:3]),
                             reads=[num.b, sm.b], writes=[its.b, sm.b])
                        S.op("act", lambda e: e.activation(out=sm.t[:, 3:4], in_=sm.t[:, 2:3], func=AF.Sqrt, bias=EPS, scale=1.0 / 128),
                             reads=[sm.b], writes=[sm.b])
                        S.op("dve", lambda e: e.reciprocal(out=sm.t[:, 4:5], in_=sm.t[:, 3:4]), reads=[sm.b], writes=[sm.b])
                        S.op("dve", lambda e: e.tensor_tensor(out=sm.t[:, 5:6], in0=sm.t[:, 4:5], in1=sm.t[:, 1:2], op=ALU.mult),
                             reads=[sm.b], writes=[sm.b])
                        S.op("dve", lambda e, h=h: e.scalar_tensor_tensor(
                            out=mixb.t[:, h * 128:(h + 1) * 128], in0=num.t[:, 0:128], scalar=sm.t[:, 5:6],
                            in1=gsig.t[:, h * 128:(h + 1) * 128], op0=ALU.mult, op1=ALU.mult),
                             reads=[num.b, sm.b, gsig.b], writes=[mixb.b])
            if part in ("ALL", "U"):
                for c in range(2):
                    ps_k, pb_k = bank()
                    S.op("pe", lambda e, ps_k=ps_k, c=c: e.transpose(out=ps_k[:, 0:128], in_=qkT.t[:, 2 + c, :], identity=ident.t[:]),
                         reads=[qkT.b, ident.b], writes=[pb_k])
                    S.op("dve", lambda e, ps_k=ps_k, c=c: e.tensor_tensor(
                        out=kw.t[:, :].rearrange("p (a k) -> p a k", a=2), in0=ps_k[:, 0:128].rearrange("p (a k) -> p a k", a=2),
                        in1=gs.t[:, 16 + 2 * c:18 + 2 * c].unsqueeze(2).to_broadcast([128, 2, 64]), op=ALU.mult),
                         reads=[pb_k, gs.b], writes=[kw.b])
                    S.op("pe", lambda e, ps_k=ps_k, c=c: e.matmul(
                        ps_k[:, 128:386], lhsT=kw.t[:], rhs=vext.t[:, 2 * c:2 * c + 2, :].rearrange("p a b -> p (a b)"),
                        start=True, stop=True), reads=[kw.b, vext.b], writes=[pb_k])
                    for hh in range(2):
                        p0, p1 = hh * 64, hh * 64 + 64
                        h = 2 * c + hh
                        S.op("dve", lambda e, ps_k=ps_k, c=c, hh=hh, p0=p0, p1=p1, h=h: e.scalar_tensor_tensor(
                            out=CT.t[p0:p1, c, :], in0=CT.t[p0:p1, c, :], scalar=gs.t[p0:p1, 20 + h:21 + h],
                            in1=ps_k[p0:p1, 128 + hh * 129:128 + (hh + 1) * 129], op0=ALU.mult, op1=ALU.add),
                             reads=[pb_k, gs.b, CT.b], writes=[CT.b])
            if part != "ALL":
                return
            if light:
                return
            for kv in range(2):
                ps_o, pb_o = bank()
                pts = (PT, PT2)
                srcs = ((1 - slot, mprev0 if n == 0 else mprev), (slot, mcur))
                for wi, (sl, mk) in enumerate(srcs):
                    ps_s, pb_s = bank()
                    S.op("pe", lambda e, ps_s=ps_s, sl=sl, kv=kv: e.matmul(
                        ps_s[:, :], lhsT=kTs.t[0:64, sl, kv, :], rhs=aqT.t[0:64, kv * 4:kv * 4 + 4, :].rearrange("p h t -> p (h t)"),
                        start=True, stop=False), reads=[kTs.b, aqT.b], writes=[pb_s])
                    for g in range(4):
                        S.op("pe", lambda e, ps_s=ps_s, g=g, mk=mk: e.matmul(
                            ps_s[:, g * 128:(g + 1) * 128], lhsT=ident.t[:], rhs=mk.t[:], start=False, stop=(g == 3)),
                             reads=[ident.b, mk.b], writes=[pb_s])
                    S.op("act", lambda e, ps_s=ps_s, wi=wi: e.activation(out=pts[wi].t[:], in_=ps_s[:, :], func=AF.Exp, scale=0.125),
                         reads=[pb_s], writes=[pts[wi].b])
                for g in range(4):
                    for wi, (sl, mk) in enumerate(srcs):
                        S.op("pe", lambda e, ps_o=ps_o, g=g, sl=sl, kv=kv, wi=wi: e.matmul(
                            ps_o[:, g * 65:(g + 1) * 65], lhsT=pts[wi].t[:, g * 128:(g + 1) * 128], rhs=vsw.t[:, sl, kv, :],
                            start=(wi == 0), stop=(wi == 1)), reads=[pts[wi].b, vsw.b], writes=[pb_o])
                o3 = ps_o[:, 0:260].rearrange("p (g d) -> p g d", g=4)
                S.op("dve", lambda e, o3=o3, kv=kv: e.tensor_tensor(out=sm.t[:, 8:12], in0=o3[:, :, 64],
                                                                    in1=esink.t[:, kv * 4:kv * 4 + 4], op=ALU.add),
                     reads=[pb_o, esink.b], writes=[sm.b])
                S.op("dve", lambda e: e.reciprocal(out=sm.t[:, 12:16], in_=sm.t[:, 8:12]), reads=[sm.b], writes=[sm.b])
                S.op("dve", lambda e, o3=o3, kv=kv: e.tensor_tensor(
                    out=mixb.t[:, 512 + kv * 256:512 + (kv + 1) * 256].rearrange("p (g d) -> p g d", g=4),
                    in0=o3[:, :, 0:64], in1=sm.t[:, 12:16].unsqueeze(2).to_broadcast([128, 4, 64]), op=ALU.mult),
                     reads=[pb_o, sm.b], writes=[mixb.b])
            transpose8(mixb, xT)
            out_proj_add(xT, Wout, n)

        for i in range(NPRE):
            mixer_block(xp_d, i * 128, i, True, i, i == NPRE - 1, part="F")
            if i > 0:
                mixer_block(xp_d, (i - 1) * 128, i - 1, True, i - 1, i - 1 == NPRE - 1, part="U")
            mixer_block(xp_d, i * 128, i, True, i, i == NPRE - 1, part="G")
        mixer_block(xp_d, (NPRE - 1) * 128, NPRE - 1, True, NPRE - 1, True, part="U")
        for n in range(NB):
            mixer_block(xs_d, n * 128, n, False, None, True)
        S.barrier()
        S.emit()

    if stop_after >= 2:
        with ExitStack() as es:
            Wq = alloc(es, "Wq", [128, 8, D])
            W2 = alloc(es, "W2", [128, 8, D])
            load_w(Wq, w_xq_d, D)
            load_w(W2, w_xk_d, D)
            gcT = load_colvec(es, "gcT", g_cross_d, 8)
            gmT = load_colvec(es, "gmT", g_mem_d, 8)
            S1 = alloc(es, "S1b", [128, D])
            xT = alloc(es, "xTb", [128, 8, 128])
            memT = alloc(es, "memT", [128, 8, 256])
            KT = alloc(es, "KT", [128, 8, 256])
            Vx = alloc(es, "Vx", [128, 2, 4, 257])
            qT = alloc(es, "qT", [128, 8, 128])
            PTb = alloc(es, "PTb", [128, 8, 128])
            oc = alloc(es, "oc", [128, D])
            MX = alloc(es, "MXb", [128, D])
            S.op("pool", lambda e: e.memset(Vx.t[:], 1.0), writes=[Vx.b])
            for mc in range(2):
                S.dma(dmaq(), lambda e, mc=mc: e.dma_start(out=MX.t[:], in_=mem_d[mc * 128:(mc + 1) * 128, :]), writes=[MX.b])
                norm_T(MX.t[:], MX.b, gmT, S1, xT)
                S.op("dve", lambda e, mc=mc: e.tensor_copy(out=memT.t[:, :, mc * 128:(mc + 1) * 128], in_=xT.t[:]),
                     reads=[xT.b], writes=[memT.b])
            for j in range(8):
                ps, pb = bank()
                for dc in range(8):
                    S.op("pe", lambda e, ps=ps, j=j, dc=dc: e.matmul(ps[:, 0:256], lhsT=W2.t[:, dc, j * 128:(j + 1) * 128],
                                                                     rhs=memT.t[:, dc, :], start=(dc == 0), stop=(dc == 7)),
                         reads=[W2.b, memT.b], writes=[pb])
                S.op("act", lambda e, ps=ps, j=j: e.activation(func=AF.Copy, out=KT.t[:, j, :], in_=ps[:, 0:256]), reads=[pb], writes=[KT.b])
            load_w(W2, w_xv_d, D)
            for mc in range(2):
                for half in range(2):
                    ps, pb = bank()
                    for dc in range(8):
                        S.op("pe", lambda e, ps=ps, mc=mc, half=half, dc=dc: e.matmul(
                            ps[:, :], lhsT=memT.t[:, dc, mc * 128:(mc + 1) * 128], rhs=W2.t[:, dc, half * 512:(half + 1) * 512],
                            start=(dc == 0), stop=(dc == 7)), reads=[W2.b, memT.b], writes=[pb])
                    S.op("dve", lambda e, ps=ps, mc=mc, half=half: e.tensor_copy(
                        out=Vx.t[:, mc, 2 * half:2 * half + 2, 0:256], in_=ps[:, :].rearrange("p (h v) -> p h v", h=2)),
                         reads=[pb], writes=[Vx.b])
            load_w(W2, w_xo_d, D)
            for n in range(NB):
                norm_T(H[:, n, :], Hb[n], gcT, S1, xT)
                pd, pbs = dbank()
                for j in range(8):
                    for dc in range(8):
                        S.op("pe", lambda e, j=j, dc=dc: e.matmul(pd[:, j * 128:(j + 1) * 128], lhsT=Wq.t[:, dc, j * 128:(j + 1) * 128],
                                                                  rhs=xT.t[:, dc, :], start=(dc == 0), stop=(dc == 7)),
                             reads=[Wq.b, xT.b], writes=[pbs[j // 4]])
                S.op("act", lambda e: e.activation(func=AF.Copy, out=qT.t[:, 0:4, :], in_=pd[:, 0:512].rearrange("p (j t) -> p j t", j=4)),
                     reads=[pbs[0]], writes=[qT.b])
                S.op("dve", lambda e: e.tensor_copy(out=qT.t[:, 4:8, :], in_=pd[:, 512:1024].rearrange("p (j t) -> p j t", j=4)),
                     reads=[pbs[1]], writes=[qT.b])
                pd2, pbs2 = dbank()
                for h in range(4):
                    for mc in range(2):
                        col = (h * 2 + mc) * 128
                        for cc in range(2):
                            S.op("pe", lambda e, h=h, mc=mc, cc=cc, col=col: e.matmul(
                                pd2[:, col:col + 128], lhsT=KT.t[:, h * 2 + cc, mc * 128:(mc + 1) * 128], rhs=qT.t[:, h * 2 + cc, :],
                                start=(cc == 0), stop=(cc == 1)), reads=[KT.b, qT.b], writes=[pbs2[h // 2]])
                for k in range(2):
                    S.op("act", lambda e, k=k: e.activation(out=PTb.t[:, 4 * k:4 * k + 4, :],
                                                            in_=pd2[:, k * 512:(k + 1) * 512].rearrange("p (j t) -> p j t", j=4),
                                                            func=AF.Exp, scale=1.0 / 16), reads=[pbs2[k]], writes=[PTb.b])
                for h in range(4):
                    ps, pb = bank()
                    for mc in range(2):
                        S.op("pe", lambda e, ps=ps, h=h, mc=mc: e.matmul(ps[:, 0:257], lhsT=PTb.t[:, h * 2 + mc, :], rhs=Vx.t[:, mc, h, :],
                                                                         start=(mc == 0), stop=(mc == 1)),
                             reads=[PTb.b, Vx.b], writes=[pb])
                    S.op("dve", lambda e, ps=ps: e.reciprocal(out=ss.t[:, 3:4], in_=ps[:, 256:257]), reads=[pb], writes=[ss.b])
                    S.op("act", lambda e, ps=ps, h=h: e.activation(out=oc.t[:, h * 256:(h + 1) * 256], in_=ps[:, 0:256], func=AF.Copy,
                                                                   scale=ss.t[:, 3:4]), reads=[pb, ss.b], writes=[oc.b])
                transpose8(oc, xT)
                out_proj_add(xT, W2, n)
            S.barrier()
            S.emit()

    with ExitStack() as esC:
        S1 = alloc(esC, "S1c", [128, D])
        if stop_after >= 3:
            idxTa = alloc(esC, "idxTa", [128, NB, 128], U32)
            gTa = alloc(esC, "gTa", [128, NB, 128])
        if stop_after >= 3:
          with ExitStack() as es:
              Wpq = alloc(es, "Wpq", [128, 8, 2048])
              load_w(Wpq, w_pq_d, 2048)
              gfT = load_colvec(es, "gfT", g_ffn_d, 8)
              skT = alloc(es, "skT", [128, 2, 128])
              skr = alloc(es, "skr", [128, 128])
              for i, skd in enumerate((sk1_d, sk2_d)):
                  S.dma(dmaq(), lambda e, skd=skd: e.dma_start(out=skr.t[:], in_=skd), writes=[skr.b])
                  ps, pb = bank()
                  S.op("pe", lambda e, ps=ps: e.transpose(out=ps[:, 0:128], in_=skr.t[:], identity=ident.t[:]),
                       reads=[skr.b, ident.b], writes=[pb])
                  S.op("dve", lambda e, ps=ps, i=i: e.tensor_copy(out=skT.t[:, i, :], in_=ps[:, 0:128]), reads=[pb], writes=[skT.b])
              xT = alloc(es, "xTc", [128, 8, 128])
              T8a = alloc(es, "T8a", [128, 2048])
              T8bs = [alloc(es, "T8b%d" % i, [128, 2048]) for i in range(2)]
              Q8 = alloc(es, "Q8", [128, 2048])
              T8c = alloc(es, "T8c", [128, 2048])
              v16 = alloc(es, "v16", [128, 16, 16])
              i16u = alloc(es, "i16u", [128, 16, 16], U32)
              i16 = alloc(es, "i16", [128, 16, 16])
              tsv = alloc(es, "tsv", [128, 8, 16])
              posu = alloc(es, "posu", [128, 8, 16], U32)
              pau = alloc(es, "pau", [128, 8, 16], U32)
              pbu = alloc(es, "pbu", [128, 8, 16], U32)
              pa = alloc(es, "pa", [128, 8, 16])
              pbq = alloc(es, "pbq", [128, 8, 16])
              eid = alloc(es, "eid", [128, 8, 16])
              eid2 = alloc(es, "eid2", [128, 8, 16])
              gg = alloc(es, "gg", [128, 8, 16])
              gsum = alloc(es, "gsum", [128, 8])
              iota16 = alloc(es, "iota16", [128, 16])
              Aacc = alloc(es, "Aacc", [128, 128])
              S.op("pool", lambda e: e.iota(iota16.t[:], pattern=[[1, 16]], base=0, channel_multiplier=0,
                                            allow_small_or_imprecise_dtypes=True), writes=[iota16.b])

              def top16_multi(groups, src_b, work_b, out_bs):
                  gbs = [Buf("g%d" % i) for i in range(len(groups))]
                  for st in range(5):
                      for gi, (src_ap, work, vout, iout) in enumerate(groups):
                          gb = gbs[gi]
                          extra = ([work_b] + out_bs) if (st == 0 and gi == 0) else []
                          if st == 0:
                              S.op("dve", lambda e: e.max(out=vout[:, 0:8], in_=src_ap), reads=[src_b], writes=[gb] + extra)
                          elif st == 1:
                              S.op("dve", lambda e: e.max_index(out=iout[:, 0:8], in_max=vout[:, 0:8], in_values=src_ap),
                                   reads=[src_b, gb], writes=[gb])
                          elif st == 2:
                              S.op("dve", lambda e: e.match_replace(out=work, in_to_replace=vout[:, 0:8], in_values=src_ap,
                                                                    imm_value=-1e30), reads=[src_b, gb], writes=[gb])
                          elif st == 3:
                              S.op("dve", lambda e: e.max(out=vout[:, 8:16], in_=work), reads=[gb], writes=[gb])
                          else:
                              S.op("dve", lambda e: e.max_index(out=iout[:, 8:16], in_max=vout[:, 8:16], in_values=work),
                                   reads=[gb], writes=[gb])
                  return gbs

              def c1_front(n):
                  T8b = T8bs[n % 2]
                  norm_T(H[:, n, :], Hb[n], gfT, S1, xT)
                  for q4 in range(4):
                      ps, pb = bank()
                      for jj in range(4):
                          j = q4 * 4 + jj
                          for dc in range(8):
                              S.op("pe", lambda e, ps=ps, jj=jj, j=j, dc=dc: e.matmul(
                                  ps[:, jj * 128:(jj + 1) * 128], lhsT=Wpq.t[:, dc, j * 128:(j + 1) * 128], rhs=xT.t[:, dc, :],
                                  start=(dc == 0), stop=(dc == 7)), reads=[Wpq.b, xT.b], writes=[pb])
                      S.op("act", lambda e, ps=ps, q4=q4: e.activation(func=AF.Copy, out=Q8.t[:, q4 * 512:(q4 + 1) * 512], in_=ps[:, :]),
                           reads=[pb], writes=[Q8.b])
                  for q4 in range(4):
                      ps, pb = bank()
                      for jj in range(4):
                          j = q4 * 4 + jj
                          S.op("pe", lambda e, ps=ps, jj=jj, j=j: e.matmul(
                              ps[:, jj * 128:(jj + 1) * 128], lhsT=Q8.t[:, j * 128:(j + 1) * 128], rhs=skT.t[:, j % 2, :],
                              start=True, stop=True), reads=[Q8.b, skT.b], writes=[pb])
                      S.op("act", lambda e, ps=ps, q4=q4: e.activation(func=AF.Copy, out=T8b.t[:, q4 * 512:(q4 + 1) * 512], in_=ps[:, :]),
                           reads=[pb], writes=[T8b.b])
              def c1_back(n):
                  T8b = T8bs[n % 2]
                  g1 = top16_multi([(T8b.t[:, j * 128:(j + 1) * 128], T8c.t[:, j * 128:(j + 1) * 128], v16.t[:, j, :], i16u.t[:, j, :])
                                    for j in range(16)], T8b.b, T8c.b, [v16.b, i16u.b])
                  S.op("dve", lambda e: e.tensor_copy(out=i16.t[:], in_=i16u.t[:]), reads=[i16u.b] + g1, writes=[i16.b])
                  v4 = v16.t[:, :, :].rearrange("p (h two) k -> p h two k", two=2)
                  cs4 = T8c.t[:, :].rearrange("p (h a b) -> p h a b", h=8, a=16)
                  S.op("dve", lambda e: e.tensor_tensor(out=cs4, in0=v4[:, :, 0, :].unsqueeze(3).to_broadcast([128, 8, 16, 16]),
                                                        in1=v4[:, :, 1, :].unsqueeze(2).to_broadcast([128, 8, 16, 16]), op=ALU.add),
                       reads=[v16.b] + g1, writes=[T8c.b])
                  g2 = top16_multi([(T8c.t[:, h * 256:(h + 1) * 256], T8a.t[:, h * 256:(h + 1) * 256], tsv.t[:, h, :], posu.t[:, h, :])
                                    for h in range(8)], T8c.b, T8a.b, [tsv.b, posu.b])
                  S.op("dve", lambda e: e.tensor_single_scalar(out=pau.t[:], in_=posu.t[:], scalar=4, op=ALU.logical_shift_right),
                       reads=[posu.b] + g2, writes=[pau.b])
                  S.op("dve", lambda e: e.tensor_single_scalar(out=pbu.t[:], in_=posu.t[:], scalar=15, op=ALU.bitwise_and),
                       reads=[posu.b] + g2, writes=[pbu.b])
                  S.op("dve", lambda e: e.tensor_copy(out=pa.t[:], in_=pau.t[:]), reads=[pau.b], writes=[pa.b])
                  S.op("dve", lambda e: e.tensor_copy(out=pbq.t[:], in_=pbu.t[:]), reads=[pbu.b], writes=[pbq.b])
                  i4 = i16.t[:, :, :].rearrange("p (h two) k -> p h two k", two=2)
                  oh = T8a.t[:, :].rearrange("p (h j a) -> p h j a", h=8, j=16)
                  io4 = iota16.t[:, :].unsqueeze(1).unsqueeze(1).to_broadcast([128, 8, 16, 16])
                  for which, (sel_ap, dst) in enumerate(((pa, eid), (pbq, eid2))):
                      S.op("dve", lambda e, sel_ap=sel_ap: e.tensor_tensor(
                          out=oh, in0=sel_ap.t[:, :, :].unsqueeze(3).to_broadcast([128, 8, 16, 16]), in1=io4, op=ALU.is_equal),
                           reads=[sel_ap.b, iota16.b], writes=[T8a.b])
                      S.op("dve", lambda e, which=which: e.tensor_tensor(
                          out=oh, in0=oh, in1=i4[:, :, which, :].unsqueeze(2).to_broadcast([128, 8, 16, 16]), op=ALU.mult),
                           reads=[T8a.b, i16.b], writes=[T8a.b])
                      S.op("dve", lambda e, dst=dst: e.tensor_reduce(out=dst.t[:], in_=oh, axis=AX.X, op=ALU.add),
                           reads=[T8a.b], writes=[dst.b])
                  S.op("dve", lambda e: e.scalar_tensor_tensor(out=eid.t[:], in0=eid.t[:], scalar=128.0, in1=eid2.t[:],
                                                               op0=ALU.mult, op1=ALU.add), reads=[eid.b, eid2.b], writes=[eid.b])
                  S.op("dve", lambda e: e.tensor_tensor(out=gg.t[:], in0=tsv.t[:], in1=tsv.t[:, :, 0:1].to_broadcast([128, 8, 16]),
                                                        op=ALU.subtract), reads=[tsv.b] + g2, writes=[gg.b])
                  S.op("act", lambda e: e.activation(out=gg.t[:], in_=gg.t[:], func=AF.Exp), reads=[gg.b], writes=[gg.b])
                  S.op("dve", lambda e: e.tensor_reduce(out=gsum.t[:], in_=gg.t[:], axis=AX.X, op=ALU.add),
                       reads=[gg.b], writes=[gsum.b])
                  S.op("dve", lambda e: e.reciprocal(out=gsum.t[:], in_=gsum.t[:]), reads=[gsum.b], writes=[gsum.b])
                  S.op("dve", lambda e: e.tensor_tensor(out=gg.t[:], in0=gg.t[:], in1=gsum.t[:, :].unsqueeze(2).to_broadcast([128, 8, 16]),
                                                        op=ALU.mult), reads=[gg.b, gsum.b], writes=[gg.b])
                  ps, pb = bank()
                  S.op("pe", lambda e, ps=ps: e.transpose(out=ps[:, 0:128], in_=eid.t[:, :, :].rearrange("p h k -> p (h k)"),
                                                          identity=ident.t[:]), reads=[eid.b, ident.b], writes=[pb])
                  S.op("pe", lambda e, ps=ps: e.transpose(out=ps[:, 128:256], in_=gg.t[:, :, :].rearrange("p h k -> p (h k)"),
                                                          identity=ident.t[:]), reads=[gg.b, ident.b], writes=[pb])
                  S.op("dve", lambda e, ps=ps: e.tensor_scalar(out=Aacc.t[:], in0=ps[:, 0:128], scalar1=8388608.0, scalar2=None,
                                                               op0=ALU.add), reads=[pb], writes=[Aacc.b])
                  S.op("dve", lambda e: e.tensor_single_scalar(out=idxTa.t[:, n, :], in_=Aacc.t[:].bitcast(U32), scalar=0x7FFFFF,
                                                               op=ALU.bitwise_and), reads=[Aacc.b], writes=[idxTa.b])
                  S.op("act", lambda e, ps=ps: e.activation(func=AF.Copy, out=gTa.t[:, n, :], in_=ps[:, 128:256]), reads=[pb], writes=[gTa.b])

              c1_front(0)
              for n in range(NB):
                  if n + 1 < NB:
                      c1_front(n + 1)
                  c1_back(n)
              S.barrier()
              S.emit()
        if stop_after >= 3:
          with ExitStack() as es:
            gfrow = load_bcast(es, "gfrow", g_ffn_row_d, D)
            xfs = [alloc(es, "xf%d" % i, [128, D]) for i in range(2)]
            Aacc = alloc(es, "Aacc2", [128, 128])
            AA = alloc(es, "AA", [128, 2, 128])
            Lms = [alloc(es, "Lm%d" % i, [128, 128]) for i in range(3)]
            zcol = alloc(es, "zcol", [128, 1])
            sel = [alloc(es, "sel%d" % i, [128, 128], BF16) for i in range(4)]
            xps = [[alloc(es, "xp%d_%d" % (i, k), [128, D], BF16) for k in range(XB_PASSES)] for i in range(2)]
            NG = 8
            Gd = [alloc(es, "Gd%d" % i, [128, D]) for i in range(NG)]
            Gu = [alloc(es, "Gu%d" % i, [128, D]) for i in range(NG)]
            for lm_ in Lms:
                S.op("pool", lambda e: e.memset(lm_.t[:], 0.0), writes=[lm_.b])
            S.op("pool", lambda e: e.memset(zcol.t[:], 0.0), writes=[zcol.b])
            tokc = [0]
            for n in range(NB):
                xfn = xfs[n % 2]
                zero(ss, ss.t[:, 0:1])
                S.op("act", lambda e: e.activation(out=S1.t[:], in_=H[:, n, :], func=AF.Square, accum_out=ss.t[:, 0:1]),
                     reads=[Hb[n], ss.b], writes=[S1.b, ss.b])
                S.op("act", lambda e: e.activation(out=ss.t[:, 1:2], in_=ss.t[:, 0:1], func=AF.Sqrt, bias=EPS, scale=1.0 / D),
                     reads=[ss.b], writes=[ss.b])
                S.op("dve", lambda e: e.reciprocal(out=ss.t[:, 2:3], in_=ss.t[:, 1:2]), reads=[ss.b], writes=[ss.b])
                S.op("dve", lambda e: e.scalar_tensor_tensor(out=xfn.t[:], in0=H[:, n, :], scalar=ss.t[:, 2:3], in1=gfrow.t[:],
                                                             op0=ALU.mult, op1=ALU.mult),
                     reads=[Hb[n], ss.b, gfrow.b], writes=[xfn.b])
                S.op("pool", lambda e: e.memset(AA.t[:], 0.0), writes=[AA.b])
                xpn = xps[n % 2]
                for k in range(XB_PASSES):
                    src = xfn if k == 0 else S1
                    S.op("dve", lambda e: e.tensor_copy(out=xpn[k].t[:], in_=src.t[:]), reads=[src.b], writes=[xpn[k].b])
                    if k + 1 < XB_PASSES:
                        S.op("dve", lambda e: e.tensor_tensor(out=S1.t[:], in0=src.t[:], in1=xpn[k].t[:], op=ALU.subtract),
                             reads=[src.b, xpn[k].b], writes=[S1.b])
                pd_o, pbs_o = dbank()
                reserved[0] = last_db[0]
                ntok = n_tok_peer
                LA = 2
                fr = {}

                def front(t):
                    k = tokc[0] % NG
                    tokc[0] += 1
                    gd, gu, sl = Gd[k], Gu[k], sel[k % len(sel)]
                    S.dma("pool", lambda e: e.indirect_dma_start(
                        out=gd.t[:, :], out_offset=None, in_=ed_d[:, :],
                        in_offset=bass.IndirectOffsetOnAxis(ap=idxTa.t[:, n, t:t + 1].bitcast(I32), axis=0)), reads=[idxTa.b], writes=[gd.b])
                    S.dma("pool", lambda e: e.indirect_dma_start(
                        out=gu.t[:, :], out_offset=None, in_=eu_d[:, :],
                        in_offset=bass.IndirectOffsetOnAxis(ap=idxTa.t[:, n, t:t + 1].bitcast(I32), axis=0)), reads=[idxTa.b], writes=[gu.b])
                    S.op("act", lambda e: e.activation(func=AF.Copy, out=sl.t[:], in_=ident.t[:, t:t + 1].to_broadcast([128, 128])),
                         reads=[ident.b], writes=[sl.b])
                    pd_x, pbs_x = dbank()
                    for half in range(2):
                        for k in range(XB_PASSES):
                            S.op("pe", lambda e: e.matmul(
                                pd_x[:, half * 512:(half + 1) * 512], lhsT=sl.t[:], rhs=xpn[k].t[:, half * 512:(half + 1) * 512],
                                start=(k == 0), stop=(k == XB_PASSES - 1)), reads=[sl.b, xpn[k].b], writes=[pbs_x[half]])
                    fr[t] = (gd, gu, pd_x, pbs_x)

                def back(t):
                    gd, gu, pd_x, pbs_x = fr.pop(t)
                    lm = Lms[t % 3]
                    for half in range(2):
                        S.op("dve", lambda e: e.scalar_tensor_tensor(
                            out=S1.t[:, half * 512:(half + 1) * 512], in0=gd.t[:, half * 512:(half + 1) * 512], scalar=1.0,
                            in1=pd_x[:, half * 512:(half + 1) * 512], op0=ALU.mult, op1=ALU.mult,
                            accum_out=AA.t[:, half, t:t + 1]),
                             reads=[gd.b, pbs_x[half], AA.b], writes=[S1.b, AA.b])
                    S.op("dve", lambda e: e.tensor_tensor(out=Aacc.t[:, t:t + 1], in0=AA.t[:, 0, t:t + 1], in1=AA.t[:, 1, t:t + 1],
                                                          op=ALU.add), reads=[AA.b], writes=[Aacc.b])
                    S.op("act", lambda e: e.activation(out=Aacc.t[:, t:t + 1], in_=Aacc.t[:, t:t + 1], func=AF.Gelu),
                         reads=[Aacc.b], writes=[Aacc.b])
                    S.op("dve", lambda e: e.tensor_tensor(out=lm.t[:, t:t + 1], in0=Aacc.t[:, t:t + 1], in1=gTa.t[:, n, t:t + 1],
                                                          op=ALU.mult), reads=[Aacc.b, gTa.b], writes=[lm.b])
                    for half in range(2):
                        S.op("pe", lambda e: e.matmul(
                            pd_o[:, half * 512:(half + 1) * 512], lhsT=lm.t[:], rhs=gu.t[:, half * 512:(half + 1) * 512],
                            start=(t == 0), stop=(t == ntok - 1)), reads=[lm.b, gu.b], writes=[pbs_o[half]])
                    if t >= 2:
                        lmo = Lms[(t - 2) % 3]
                        S.op("act", lambda e: e.activation(func=AF.Copy, out=lmo.t[:, t - 2:t - 1], in_=zcol.t[:, 0:1]),
                             reads=[zcol.b], writes=[lmo.b])

                for t in range(min(LA, ntok)):
                    front(t)
                for t in range(ntok):
                    if t + LA < ntok:
                        front(t + LA)
                    back(t)
                for t in range(max(0, ntok - 2), ntok):
                    lmo = Lms[t % 3]
                    S.op("act", lambda e: e.activation(func=AF.Copy, out=lmo.t[:, t:t + 1], in_=zcol.t[:, 0:1]),
                         reads=[zcol.b], writes=[lmo.b])
                reserved[0] = None
                for half in range(2):
                    S.op("dve", lambda e, half=half: e.tensor_tensor(
                        out=H[:, n, half * 512:(half + 1) * 512], in0=H[:, n, half * 512:(half + 1) * 512],
                        in1=pd_o[:, half * 512:(half + 1) * 512], op=ALU.add), reads=[pbs_o[half], Hb[n]], writes=[Hb[n]])
            S.barrier()
            S.emit()
        gfin = load_bcast(esC, "gfin", g_final_d, D)
        for n in range(NB):
            zero(ss, ss.t[:, 0:1])
            S.op("act", lambda e, n=n: e.activation(out=S1.t[:], in_=H[:, n, :], func=AF.Square, accum_out=ss.t[:, 0:1]),
                 reads=[Hb[n], ss.b], writes=[S1.b, ss.b])
            S.op("act", lambda e: e.activation(out=ss.t[:, 1:2], in_=ss.t[:, 0:1], func=AF.Sqrt, bias=EPS, scale=1.0 / D),
                 reads=[ss.b], writes=[ss.b])
            S.op("dve", lambda e: e.reciprocal(out=ss.t[:, 2:3], in_=ss.t[:, 1:2]), reads=[ss.b], writes=[ss.b])
            S.op("dve", lambda e, n=n: e.scalar_tensor_tensor(out=S1.t[:], in0=H[:, n, :], scalar=ss.t[:, 2:3], in1=gfin.t[:],
                                                              op0=ALU.mult, op1=ALU.mult),
                 reads=[Hb[n], ss.b, gfin.b], writes=[S1.b])
            S.dma("sp", lambda e, n=n: e.dma_start(out=out_d[n * 128:(n + 1) * 128, :], in_=S1.t[:]), reads=[S1.b],
                  writes=[Buf("o")])
        S.wait_all_dma("sp")
        S.barrier()
        S.emit()
    S.close()
    es_all.close()
    return nc


def make_in_maps(inp, NB):
    x = np.ascontiguousarray(inp["x"], dtype=np.float32)
    B, SEQ, _ = x.shape
    seg = NB * 128
    assert SEQ == NSEG * seg
    NPRE = (NSEG - 1) * NB

    def f(a):
        return np.ascontiguousarray(np.asarray(a, dtype=np.float32))

    shared = {
        "g_mix": f(inp["g_mix"][0]).reshape(8, 128),
        "w_in": f(inp["w_in"][0]),
        "conv_w": f(inp["conv_w"][0]),
        "conv_b": f(inp["conv_b"][0]).reshape(4, 128),
        "gateb": np.concatenate([f(inp["b_igate"][0]), f(inp["b_fgate"][0])]).reshape(1, 8),
        "g_mhead": f(inp["g_mhead"][0]).reshape(1, 512),
        "sinks": f(inp["sinks"][0]).reshape(1, 8),
        "w_out": f(inp["w_out"][0]),
        "g_cross": f(inp["g_cross"][0]).reshape(8, 128),
        "g_mem": f(inp["g_mem"][0]).reshape(8, 128),
        "w_xq": f(inp["w_xq"][0]), "w_xk": f(inp["w_xk"][0]), "w_xv": f(inp["w_xv"][0]), "w_xo": f(inp["w_xo"][0]),
        "g_ffn": f(inp["g_ffn"][0]).reshape(8, 128),
        "g_ffn_row": f(inp["g_ffn"][0]).reshape(1, D),
        "w_pq": f(inp["w_pq"][0]),
        "sub_keys1": f(inp["sub_keys1"][0]), "sub_keys2": f(inp["sub_keys2"][0]),
        "expert_down": f(inp["expert_down"][0]), "expert_up": f(inp["expert_up"][0]),
        "g_final": f(inp["g_final"]).reshape(1, D),
    }
    mem = f(inp["mem"])
    maps = []
    for c in range(B * NSEG):
        b, j = divmod(c, NSEG)
        xp = np.zeros((NPRE * 128, D), np.float32)
        pm = np.zeros((1, NPRE), np.float32)
        if j > 0:
            xp[NPRE * 128 - j * seg:] = x[b, :j * seg]
            pm[0, NPRE - j * NB:] = 1.0
        m = dict(shared)
        m["xs"] = np.ascontiguousarray(x[b, j * seg:(j + 1) * seg])
        m["xp"] = xp
        m["pm"] = pm
        m["mem"] = np.ascontiguousarray(mem[b])
        maps.append(m)
    return maps


_NC_CACHE = {}


def kernel(**inputs):
    x = np.asarray(inputs["x"])
    B, SEQ, _ = x.shape
    NB = SEQ // (NSEG * 128)
    key = (NB,)
    if key not in _NC_CACHE:
        _NC_CACHE[key] = build(NB)
    nc = _NC_CACHE[key]
    maps = make_in_maps(inputs, NB)
    res = run_bass_kernel_spmd(nc, maps, core_ids=list(range(B * NSEG)))
    out = np.zeros((B, SEQ, D), np.float32)
    seg = NB * 128
    for c in range(B * NSEG):
        b, j = divmod(c, NSEG)
        out[b, j * seg:(j + 1) * seg] = res.results[c]["out"]
    return out
```

```python
import numpy as np
from contextlib import ExitStack
import concourse.bass as bass
import concourse.mybir as mybir
from concourse.bass_utils import run_bass_kernel_spmd

F32 = mybir.dt.float32
BF16 = mybir.dt.bfloat16
XB_PASSES = 3
I32 = mybir.dt.int32
U32 = mybir.dt.uint32
AF = mybir.ActivationFunctionType
ALU = mybir.AluOpType
AX = mybir.AxisListType

D = 1024
P_IN = 2312
EPS = 1e-6
NEG = -30000.0
NSEG = 4
SAME_ENGINE_SYNC = True
DBG = 0


class Buf:
    __slots__ = ("name", "last_write", "readers", "exclusive")

    def __init__(self, name, exclusive=False):
        self.name = name
        self.last_write = None
        self.readers = {}
        self.exclusive = exclusive


class TB:
    __slots__ = ("t", "b")

    def __init__(self, t, name):
        self.t = t
        self.b = Buf(name)


class _Rec:
    def __getattr__(self, name):
        def call(*a, **k):
            return (name, a, k)
        return call


_REC = _Rec()


class Sync:
    ENGS = ("pe", "act", "dve", "pool", "sp")

    def __init__(self, nc, n_dma_sems=32):
        self.nc = nc
        self.sems = {}
        self.counts = {}
        self.seen = {e: {} for e in self.ENGS}
        self.prog = {e: [] for e in self.ENGS}
        self._cms = []
        for name in self.ENGS:
            self._new_sem(name)
        self.dma_keys = []
        for i in range(n_dma_sems):
            k = "dma%d" % i
            self._new_sem(k)
            self.dma_keys.append(k)
        self.pdma_keys = []
        for i in range(8):
            k = "pdma%d" % i
            self._new_sem(k)
            self.pdma_keys.append(k)
        self.dma_rr = 0
        self.pdma_rr = 0
        self.dma_inflight = {k: None for k in self.dma_keys + self.pdma_keys}

    def _new_sem(self, key):
        cm = self.nc.semaphore("s_" + key)
        h = cm.__enter__()
        self._cms.append(cm)
        self.sems[key] = h
        self.counts[key] = 0

    def close(self):
        for cm in reversed(self._cms):
            cm.__exit__(None, None, None)

    def _need(self, ename, ev):
        if ev is None:
            return
        key, val = ev
        if key == ename and (ename == "pe" or not SAME_ENGINE_SYNC):
            return
        if self.seen[ename].get(key, 0) >= val:
            return
        self.prog[ename].append(("wait", key, val))
        self.seen[ename][key] = val

    def _deps(self, ename, reads, writes):
        for b in reads:
            self._need(ename, b.last_write)
        for b in writes:
            self._need(ename, b.last_write)
            for k, v in list(b.readers.items()):
                self._need(ename, (k, v))

    def op(self, ename, fn, reads=(), writes=()):
        writes = list(writes) + [b for b in reads if b.exclusive and b not in writes]
        self._deps(ename, reads, writes)
        self.counts[ename] += 1
        self.prog[ename].append(("ins", fn(_REC), ename, 1))
        ev = (ename, self.counts[ename])
        for b in reads:
            b.readers[ename] = ev[1]
        for b in writes:
            b.last_write = ev
            b.readers = {}
        return ev

    def dma(self, qname, fn, reads=(), writes=()):
        self._deps(qname, reads, writes)
        if qname == "pool":
            k = self.pdma_keys[self.pdma_rr]
            self.pdma_rr = (self.pdma_rr + 1) % len(self.pdma_keys)
        else:
            k = self.dma_keys[self.dma_rr]
            self.dma_rr = (self.dma_rr + 1) % len(self.dma_keys)
        prev = self.dma_inflight[k]
        if prev is not None:
            self._need(qname, prev)
        self.counts[k] += 16
        self.prog[qname].append(("ins", fn(_REC), k, 16))
        ev = (k, self.counts[k])
        self.dma_inflight[k] = ev
        for b in reads:
            b.readers[k] = ev[1]
        for b in writes:
            b.last_write = ev
            b.readers = {}
        return ev

    def wait_all_dma(self, ename):
        for k in self.dma_keys + self.pdma_keys:
            if self.dma_inflight[k] is not None:
                self._need(ename, self.dma_inflight[k])

    def barrier(self):
        for e in self.ENGS:
            for k in self.sems:
                if self.counts[k] > 0:
                    self._need(e, (k, self.counts[k]))

    def emit(self):
        nc = self.nc
        sems = self.sems
        prog = self.prog

        def replay(eng, items):
            for it in items:
                if it[0] == "wait":
                    eng.wait_ge(sems[it[1]], it[2])
                else:
                    name, a, k = it[1]
                    ins = getattr(eng, name)(*a, **k)
                    ins.then_inc(sems[it[2]], it[3])

        with nc.allow_low_precision("exact multi-term bf16 split of an fp32 operand (fp32-emulating)"), nc.Block() as block:
            @block.tensor
            def _(e):
                replay(e, prog["pe"])

            @block.scalar
            def _(e):
                replay(e, prog["act"])

            @block.vector
            def _(e):
                replay(e, prog["dve"])

            @block.gpsimd
            def _(e):
                replay(e, prog["pool"])

            @block.sync
            def _(e):
                replay(e, prog["sp"])
        self.prog = {e: [] for e in self.ENGS}


def build(NB, stop_after=3, n_tok_peer=128):
    NPRE = (NSEG - 1) * NB
    nc = bass.Bass("TRN2", target_bir_lowering=False)

    def din(name, shape, dt=F32):
        return nc.dram_tensor(name, list(shape), dt, kind="ExternalInput").ap()

    xs_d = din("xs", [NB * 128, D])
    xp_d = din("xp", [NPRE * 128, D])
    pm_d = din("pm", [1, NPRE])
    mem_d = din("mem", [256, D])
    g_mix_d = din("g_mix", [8, 128])
    w_in_d = din("w_in", [D, P_IN])
    conv_w_d = din("conv_w", [4, 512])
    conv_b_d = din("conv_b", [4, 128])
    gb_d = din("gateb", [1, 8])
    g_mhead_d = din("g_mhead", [1, 512])
    sinks_d = din("sinks", [1, 8])
    w_out_d = din("w_out", [D, D])
    g_cross_d = din("g_cross", [8, 128])
    g_mem_d = din("g_mem", [8, 128])
    w_xq_d = din("w_xq", [D, D])
    w_xk_d = din("w_xk", [D, D])
    w_xv_d = din("w_xv", [D, D])
    w_xo_d = din("w_xo", [D, D])
    g_ffn_d = din("g_ffn", [8, 128])
    g_ffn_row_d = din("g_ffn_row", [1, D])
    w_pq_d = din("w_pq", [D, 2048])
    sk1_d = din("sub_keys1", [128, 128])
    sk2_d = din("sub_keys2", [128, 128])
    ed_d = din("expert_down", [16384, D])
    eu_d = din("expert_up", [16384, D])
    g_final_d = din("g_final", [1, D])
    out_d = nc.dram_tensor("out", [NB * 128, D], F32, kind="ExternalOutput").ap()

    S = Sync(nc)
    es_all = ExitStack()

    def alloc(es, name, shape, dt=F32):
        return TB(es.enter_context(nc.sbuf_tensor("sb_" + name, list(shape), dt)), name)

    H = es_all.enter_context(nc.sbuf_tensor("sb_H", [128, NB, D], F32))
    Hb = [Buf("H%d" % i) for i in range(NB)]
    ident = alloc(es_all, "ident", [128, 128])
    Umat = alloc(es_all, "Umat", [128, 128])
    ones = alloc(es_all, "ones", [128, 128])
    mcur = alloc(es_all, "mcur", [128, 128])
    mprev = alloc(es_all, "mprev", [128, 128])
    mprev0 = alloc(es_all, "mprev0", [128, 128])
    ss = alloc(es_all, "ss", [128, 4])
    PSD = [es_all.enter_context(nc.psum_tensor("psd%d" % i, [128, 1024], F32)) for i in range(4)]
    PSb = [Buf("ps%d" % i, exclusive=True) for i in range(8)]
    bank_ptr = [0]
    last_db = [0]
    reserved = [None]

    def bank():
        k = bank_ptr[0] % 8
        if reserved[0] is not None and k // 2 == reserved[0]:
            bank_ptr[0] += 2 - (k % 2)
            k = bank_ptr[0] % 8
        bank_ptr[0] += 1
        return PSD[k // 2][:, (k % 2) * 512:(k % 2) * 512 + 512], PSb[k]

    def dbank():
        if bank_ptr[0] % 2:
            bank_ptr[0] += 1
        k = bank_ptr[0] % 8
        if reserved[0] is not None and k // 2 == reserved[0]:
            bank_ptr[0] += 2
            k = bank_ptr[0] % 8
        bank_ptr[0] += 2
        last_db[0] = k // 2
        return PSD[k // 2], [PSb[k], PSb[k + 1]]

    def dmaq():
        return "sp"

    S.op("pool", lambda e: e.memset(ident.t[:], 0.0), writes=[ident.b])
    S.op("pool", lambda e: e.affine_select(out=ident.t[:], in_=ident.t[:], pattern=[[-1, 128]],
                                           compare_op=ALU.not_equal, fill=1.0, base=0, channel_multiplier=1),
         reads=[ident.b], writes=[ident.b])
    S.op("pool", lambda e: e.memset(ones.t[:], 1.0), writes=[ones.b])
    S.op("pool", lambda e: e.memset(Umat.t[:], 1.0), writes=[Umat.b])
    S.op("pool", lambda e: e.affine_select(out=Umat.t[:], in_=Umat.t[:], pattern=[[1, 128]],
                                           compare_op=ALU.is_ge, fill=0.0, base=0, channel_multiplier=-1),
         reads=[Umat.b], writes=[Umat.b])
    S.op("pool", lambda e: e.memset(mcur.t[:], 0.0), writes=[mcur.b])
    S.op("pool", lambda e: e.affine_select(out=mcur.t[:], in_=mcur.t[:], pattern=[[1, 128]],
                                           compare_op=ALU.is_ge, fill=NEG, base=0, channel_multiplier=-1),
         reads=[mcur.b], writes=[mcur.b])
    S.op("pool", lambda e: e.memset(mprev.t[:], 0.0), writes=[mprev.b])
    S.op("pool", lambda e: e.affine_select(out=mprev.t[:], in_=mprev.t[:], pattern=[[-1, 128]],
                                           compare_op=ALU.is_gt, fill=NEG, base=0, channel_multiplier=1),
         reads=[mprev.b], writes=[mprev.b])

    def zero(tb, ap):
        S.op("pool", lambda e: e.memset(ap, 0.0), writes=[tb.b])

    def load_colvec(es, name, src_d, ncol):
        raw = alloc(es, name + "_raw", [ncol, 128])
        dst = alloc(es, name, [128, ncol])
        S.dma(dmaq(), lambda e: e.dma_start(out=raw.t[:], in_=src_d), writes=[raw.b])
        ps, pb = bank()
        S.op("pe", lambda e: e.transpose(out=ps[:, 0:ncol], in_=raw.t[:], identity=ident.t[0:ncol, 0:ncol]),
             reads=[raw.b, ident.b], writes=[pb])
        S.op("dve", lambda e: e.tensor_copy(out=dst.t[:], in_=ps[:, 0:ncol]), reads=[pb], writes=[dst.b])
        return dst

    def load_bcast(es, name, src_d, n):
        t = alloc(es, name, [128, n])
        S.dma(dmaq(), lambda e: e.dma_start(out=t.t[:], in_=src_d.partition_broadcast(128)), writes=[t.b])
        return t

    def norm_T(src_ap, src_b, gT, S1, xT, tok_out=None, g_row=None):
        zero(ss, ss.t[:, 0:1])
        S.op("act", lambda e: e.activation(out=S1.t[:], in_=src_ap, func=AF.Square, accum_out=ss.t[:, 0:1]),
             reads=[src_b, ss.b], writes=[S1.b, ss.b])
        S.op("act", lambda e: e.activation(out=ss.t[:, 1:2], in_=ss.t[:, 0:1], func=AF.Sqrt, bias=EPS, scale=1.0 / D),
             reads=[ss.b], writes=[ss.b])
        S.op("dve", lambda e: e.reciprocal(out=ss.t[:, 2:3], in_=ss.t[:, 1:2]), reads=[ss.b], writes=[ss.b])
        S.op("act", lambda e: e.activation(out=S1.t[:], in_=src_ap, func=AF.Copy, scale=ss.t[:, 2:3]),
             reads=[src_b, ss.b], writes=[S1.b])
        if tok_out is not None:
            S.op("dve", lambda e: e.tensor_tensor(out=tok_out.t[:], in0=S1.t[:], in1=g_row.t[:], op=ALU.mult),
                 reads=[S1.b, g_row.b], writes=[tok_out.b])
        pd, pbs = dbank()
        for c in range(8):
            S.op("pe", lambda e, c=c: e.transpose(out=pd[:, c * 128:(c + 1) * 128], in_=S1.t[:, c * 128:(c + 1) * 128],
                                                  identity=ident.t[:]),
                 reads=[S1.b, ident.b], writes=[pbs[c // 4]])
        for k in range(2):
            S.op("dve", lambda e, k=k: e.tensor_tensor(
                out=xT.t[:, 4 * k:4 * k + 4, :],
                in0=pd[:, k * 512:(k + 1) * 512].rearrange("p (c t) -> p c t", c=4),
                in1=gT.t[:, 4 * k:4 * k + 4].unsqueeze(2).to_broadcast([128, 4, 128]), op=ALU.mult),
                 reads=[pbs[k], gT.b], writes=[xT.b])

    def transpose8(src, dstT, eng="dve"):
        pd, pbs = dbank()
        for c in range(8):
            S.op("pe", lambda e, c=c: e.transpose(out=pd[:, c * 128:(c + 1) * 128], in_=src.t[:, c * 128:(c + 1) * 128],
                                                  identity=ident.t[:]),
                 reads=[src.b, ident.b], writes=[pbs[c // 4]])
        S.op("dve", lambda e: e.tensor_copy(out=dstT.t[:, 0:4, :], in_=pd[:, 0:512].rearrange("p (c t) -> p c t", c=4)),
             reads=[pbs[0]], writes=[dstT.b])
        S.op("act", lambda e: e.activation(func=AF.Copy, out=dstT.t[:, 4:8, :], in_=pd[:, 512:1024].rearrange("p (c t) -> p c t", c=4)),
             reads=[pbs[1]], writes=[dstT.b])

    def out_proj_add(srcT, W, n):
        for half in range(2):
            ps, pb = bank()
            for cc in range(8):
                S.op("pe", lambda e, cc=cc, half=half, ps=ps: e.matmul(
                    ps[:, :], lhsT=srcT.t[:, cc, :], rhs=W.t[:, cc, half * 512:(half + 1) * 512],
                    start=(cc == 0), stop=(cc == 7)), reads=[srcT.b, W.b], writes=[pb])
            S.op("dve", lambda e, half=half, ps=ps: e.tensor_tensor(
                out=H[:, n, half * 512:(half + 1) * 512], in0=H[:, n, half * 512:(half + 1) * 512], in1=ps[:, :],
                op=ALU.add), reads=[pb, Hb[n]], writes=[Hb[n]])

    joinT = alloc(es_all, "joinT", [128, 1])

    def load_w(W, src_d, ncols, col0=0):
        S._deps("sp", [], [W.b])
        cbs = [Buf("wchunk") for _ in range(8)]
        for dc in range(8):
            S.dma(dmaq(), lambda e, dc=dc: e.dma_start(out=W.t[:, dc, col0:col0 + ncols],
                                                        in_=src_d[dc * 128:(dc + 1) * 128, 0:ncols]),
                  writes=[cbs[dc]])
        S.op("pool", lambda e: e.memset(joinT.t[:], 0.0), reads=cbs, writes=[W.b, joinT.b])

    with ExitStack() as es:
        Win = alloc(es, "Win", [128, 8, P_IN])
        Wout = alloc(es, "Wout", [128, 8, D])
        load_w(Win, w_in_d, P_IN)
        load_w(Wout, w_out_d, D)
        gmixT = load_colvec(es, "gmixT", g_mix_d, 8)
        cwraw = alloc(es, "cwraw", [4, 512])
        S.dma(dmaq(), lambda e: e.dma_start(out=cwraw.t[:], in_=conv_w_d), writes=[cwraw.b])
        convw = alloc(es, "convw", [128, 4, 4])
        ps, pb = bank()
        for ch in range(4):
            S.op("pe", lambda e, ch=ch: e.transpose(out=ps[:, ch * 4:ch * 4 + 4], in_=cwraw.t[:, ch * 128:(ch + 1) * 128],
                                                    identity=ident.t[0:4, 0:4]), reads=[cwraw.b, ident.b], writes=[pb])
        S.op("dve", lambda e: e.tensor_copy(out=convw.t[:], in_=ps[:, 0:16].rearrange("p (c j) -> p c j", c=4)),
             reads=[pb], writes=[convw.b])
        convb = load_colvec(es, "convb", conv_b_d, 4)
        gateb = load_bcast(es, "gateb", gb_d, 8)
        gmh = load_bcast(es, "gmh", g_mhead_d, 512)
        esink = load_bcast(es, "esink", sinks_d, 8)
        S.op("act", lambda e: e.activation(out=esink.t[:], in_=esink.t[:], func=AF.Exp), reads=[esink.b], writes=[esink.b])
        pmask = load_bcast(es, "pmask", pm_d, NPRE)
        pmm = alloc(es, "pmm", [128, 1])
        S.op("dve", lambda e: e.tensor_scalar(out=pmm.t[:], in0=pmask.t[:, NPRE - 1:NPRE], scalar1=-1.0, scalar2=-NEG,
                                              op0=ALU.add, op1=ALU.mult), reads=[pmask.b], writes=[pmm.b])
        S.op("dve", lambda e: e.tensor_scalar(out=mprev0.t[:], in0=mprev.t[:], scalar1=pmm.t[:, 0:1], scalar2=None,
                                              op0=ALU.add), reads=[mprev.b, pmm.b], writes=[mprev0.b])

        S1 = alloc(es, "S1", [128, D])
        xT = alloc(es, "xT", [128, 8, 128])
        pre = alloc(es, "pre", [128, 4, 131])
        qkT = alloc(es, "qkT", [128, 4, 128])
        vext0 = alloc(es, "vext", [128, 4, 129])
        aqT = alloc(es, "aqT", [128, 8, 128])
        kTs = alloc(es, "kTs", [128, 2, 2, 128])
        vsw = alloc(es, "vsw", [128, 2, 2, 65])
        PT = alloc(es, "PT", [128, 512])
        PT2 = alloc(es, "PT2", [128, 512])
        cacc = TB(PT.t[:, :].rearrange("p (c t) -> p c t", c=4), "cacc_alias")
        cacc.b = PT.b
        ctmp = TB(PT2.t[:, :].rearrange("p (c t) -> p c t", c=4), "ctmp_alias")
        ctmp.b = PT2.b
        gsig = PT2
        LFB = cacc
        DT = alloc(es, "DT", [128, 128])
        ST = alloc(es, "ST", [128, 128])
        num = alloc(es, "num", [128, 129])
        its = alloc(es, "its", [128, 129])
        kw = alloc(es, "kw", [128, 128])
        CT = alloc(es, "CT", [128, 2, 129])
        gs0 = alloc(es, "gs", [128, 48])
        gs1 = alloc(es, "gs1", [128, 48])
        gss = [gs0, gs1]
        sm = alloc(es, "sm", [128, 16])
        S.op("pool", lambda e: e.memset(pre.t[:], 0.0), writes=[pre.b])
        S.op("pool", lambda e: e.memset(CT.t[:], 0.0), writes=[CT.b])
        vext1 = TB(aqT.t[:, :, :].rearrange("p a b -> p (a b)")[:, 0:516].rearrange("p (h v) -> p h v", h=4), "vext1_alias")
        vext1.b = aqT.b
        vexts = [vext0, vext1]
        for vx in vexts:
            S.op("pool", lambda e: e.memset(vx.t[:], 1.0), writes=[vx.b])
        S.op("pool", lambda e: e.memset(vsw.t[:], 1.0), writes=[vsw.b])
        LN8 = float(np.log(0.125))

        def mixer_block(src_d, row0, n, light, slot_mask_col, with_swa_kv, part="ALL"):
            slot = n % 2
            vext = vexts[n % 2] if light else vexts[0]
            gs = gss[n % 2] if light else gss[0]
            def c_load():
                xap, xb = H[:, n % NB, :], Hb[n % NB]
                S.dma(dmaq(), lambda e: e.dma_start(out=xap, in_=src_d[row0:row0 + 128, :]), writes=[xb])
                norm_T(xap, xb, gmixT, S1, xT)
            def c_gates():
                ps_g, pb_g = bank()
                for dc in range(8):
                    S.op("pe", lambda e, dc=dc: e.matmul(ps_g[:, 0:8], lhsT=xT.t[:, dc, :], rhs=Win.t[:, dc, 1536:1544],
                                                         start=(dc == 0), stop=(dc == 7)), reads=[Win.b, xT.b], writes=[pb_g])
                if with_swa_kv:
                    for dc in range(8):
                        S.op("pe", lambda e, dc=dc: e.matmul(ps_g[:, 128:256], lhsT=xT.t[:, dc, :], rhs=Win.t[:, dc, 2184:2312],
                                                             start=(dc == 0), stop=(dc == 7)), reads=[Win.b, xT.b], writes=[pb_g])
                    for kv in range(2):
                        for dc in range(8):
                            S.op("pe", lambda e, dc=dc, kv=kv: e.matmul(
                                ps_g[0:64, 256 + kv * 128:256 + (kv + 1) * 128],
                                lhsT=Win.t[:, dc, 2056 + kv * 64:2056 + (kv + 1) * 64], rhs=xT.t[:, dc, :],
                                start=(dc == 0), stop=(dc == 7)), reads=[Win.b, xT.b], writes=[pb_g])
                    S.op("dve", lambda e: e.tensor_copy(out=vsw.t[:, slot, :, 0:64],
                                                        in_=ps_g[:, 128:256].rearrange("p (k d) -> p k d", k=2)),
                         reads=[pb_g], writes=[vsw.b])
                    S.op("dve", lambda e: e.tensor_copy(out=kTs.t[0:64, slot, :, :],
                                                        in_=ps_g[0:64, 256:512].rearrange("p (k t) -> p k t", k=2)),
                         reads=[pb_g], writes=[kTs.b])
                S.op("dve", lambda e: e.tensor_tensor(out=gs.t[:, 0:8], in0=ps_g[:, 0:8], in1=gateb.t[:, 0:8], op=ALU.add),
                     reads=[pb_g, gateb.b], writes=[gs.b])
            def c_qk():
                ps_qk, pb_qk = bank()
                for ch in (range(4) if (not light or with_swa_kv) else (2, 3)):
                    for dc in range(8):
                        S.op("pe", lambda e, ch=ch, dc=dc: e.matmul(
                            ps_qk[:, ch * 128:(ch + 1) * 128], lhsT=Win.t[:, dc, ch * 128:(ch + 1) * 128], rhs=xT.t[:, dc, :],
                            start=(dc == 0), stop=(dc == 7)), reads=[Win.b, xT.b], writes=[pb_qk])
                c0 = 0 if (not light or with_swa_kv) else 2
                S.op("act", lambda e: e.activation(func=AF.Copy, out=pre.t[:, c0:4, 3:131],
                                                   in_=ps_qk[:, c0 * 128:512].rearrange("p (c t) -> p c t", c=4 - c0)),
                     reads=[pb_qk], writes=[pre.b])
            def c_v():
                ps_v, pb_v = bank()
                for dc in range(8):
                    S.op("pe", lambda e, dc=dc: e.matmul(ps_v[:, :], lhsT=xT.t[:, dc, :], rhs=Win.t[:, dc, 512:1024],
                                                         start=(dc == 0), stop=(dc == 7)), reads=[Win.b, xT.b], writes=[pb_v])
                S.op("dve", lambda e: e.tensor_copy(out=vext.t[:, :, 0:128], in_=ps_v[:, :].rearrange("p (h v) -> p h v", h=4)),
                     reads=[pb_v], writes=[vext.b])
            def c_lf():
                S.op("act", lambda e: e.activation(out=gs.t[:, 24:28], in_=gs.t[:, 4:8], func=AF.Exp, scale=-1.0),
                     reads=[gs.b], writes=[gs.b])
                S.op("act", lambda e: e.activation(out=gs.t[:, 24:28], in_=gs.t[:, 24:28], func=AF.Ln, bias=1.0),
                     reads=[gs.b], writes=[gs.b])
                S.op("dve", lambda e: e.tensor_scalar(out=gs.t[:, 4:8], in0=gs.t[:, 24:28], scalar1=-1.0, scalar2=None,
                                                      op0=ALU.mult), reads=[gs.b], writes=[gs.b])
            def c_b():
                ps_b, pb_b = bank()
                S.op("pe", lambda e: e.matmul(ps_b[:, 0:4], lhsT=Umat.t[:], rhs=gs.t[:, 4:8], start=True, stop=True),
                     reads=[Umat.b, gs.b], writes=[pb_b])
                S.op("pe", lambda e: e.matmul(ps_b[:, 4:8], lhsT=ones.t[:], rhs=gs.t[:, 4:8], start=True, stop=True),
                     reads=[ones.b, gs.b], writes=[pb_b])
                S.op("dve", lambda e: e.tensor_tensor(out=gs.t[:, 12:16], in0=gs.t[:, 0:4], in1=ps_b[:, 0:4], op=ALU.subtract),
                     reads=[gs.b, pb_b], writes=[gs.b])
                S.op("dve", lambda e: e.tensor_tensor(out=gs.t[:, 24:28], in0=gs.t[:, 12:16], in1=ps_b[:, 4:8], op=ALU.add),
                     reads=[gs.b, pb_b], writes=[gs.b])
                S.op("act", lambda e: e.activation(out=gs.t[:, 16:20], in_=gs.t[:, 24:28], func=AF.Exp),
                     reads=[gs.b], writes=[gs.b])
                if slot_mask_col is not None:
                    S.op("dve", lambda e: e.tensor_scalar(out=gs.t[:, 16:20], in0=gs.t[:, 16:20],
                                                          scalar1=pmask.t[:, slot_mask_col:slot_mask_col + 1], scalar2=None,
                                                          op0=ALU.mult), reads=[gs.b, pmask.b], writes=[gs.b])
                S.op("act", lambda e: e.activation(out=gs.t[:, 20:24], in_=ps_b[:, 4:8], func=AF.Exp),
                     reads=[pb_b], writes=[gs.b])
                if not light:
                    S.op("act", lambda e: e.activation(out=gs.t[:, 8:12], in_=ps_b[:, 0:4], func=AF.Exp, bias=LN8),
                         reads=[pb_b], writes=[gs.b])
                    S.op("dve", lambda e: e.tensor_scalar(out=gs.t[:, 12:16], in0=gs.t[:, 12:16], scalar1=LN8, scalar2=None,
                                                          op0=ALU.add), reads=[gs.b], writes=[gs.b])
            def c_conv():
                for j in range(4):
                    wj = convw.t[:, :, j:j + 1].to_broadcast([128, 4, 128])
                    if j == 0:
                        S.op("dve", lambda e, wj=wj: e.tensor_tensor(out=cacc.t[:], in0=pre.t[:, :, 0:128], in1=wj, op=ALU.mult),
                             reads=[pre.b, convw.b], writes=[cacc.b])
                    else:
                        S.op("dve", lambda e, wj=wj, j=j: e.tensor_tensor(out=ctmp.t[:], in0=pre.t[:, :, j:j + 128], in1=wj,
                                                                           op=ALU.mult),
                             reads=[pre.b, convw.b], writes=[ctmp.b])
                        S.op("dve", lambda e: e.tensor_tensor(out=cacc.t[:], in0=cacc.t[:], in1=ctmp.t[:], op=ALU.add),
                             reads=[cacc.b, ctmp.b], writes=[cacc.b])
                S.op("dve", lambda e: e.tensor_tensor(out=cacc.t[:], in0=cacc.t[:],
                                                      in1=convb.t[:, :].unsqueeze(2).to_broadcast([128, 4, 128]), op=ALU.add),
                     reads=[cacc.b, convb.b], writes=[cacc.b])
                S.op("act", lambda e: e.activation(out=qkT.t[:], in_=cacc.t[:], func=AF.Silu), reads=[cacc.b], writes=[qkT.b])
                S.op("dve", lambda e: e.tensor_copy(out=pre.t[:, :, 0:3], in_=pre.t[:, :, 128:131]), reads=[pre.b], writes=[pre.b])

            if part in ("ALL", "F"):
                c_load(); c_gates(); c_qk(); c_v()
            if part in ("ALL", "G"):
                c_lf(); c_b(); c_conv()
            if part == "L":
                c_load(); c_gates(); c_lf(); c_qk(); c_b(); c_conv(); c_v()
            if part == "ALL":
                if not light:
                    S.op("dve", lambda e: e.tensor_copy(out=LFB.t[:], in_=gs.t[:, 4:8].unsqueeze(2).to_broadcast([128, 4, 128])),
                         reads=[gs.b], writes=[LFB.b])
                    ps_o, pb_o = bank()
                    for dc in range(8):
                        S.op("pe", lambda e, dc=dc: e.matmul(ps_o[:, :], lhsT=xT.t[:, dc, :], rhs=Win.t[:, dc, 1024:1536],
                                                             start=(dc == 0), stop=(dc == 7)), reads=[Win.b, xT.b], writes=[pb_o])
                    S.op("act", lambda e: e.activation(out=gsig.t[:], in_=ps_o[:, :], func=AF.Sigmoid), reads=[pb_o], writes=[gsig.b])
                    S.op("dve", lambda e: e.tensor_tensor(out=gsig.t[:], in0=gsig.t[:], in1=gmh.t[:], op=ALU.mult),
                         reads=[gsig.b, gmh.b], writes=[gsig.b])
                    pd_q, pbs_q = dbank()
                    for hq in range(8):
                        for dc in range(8):
                            S.op("pe", lambda e, hq=hq, dc=dc: e.matmul(
                                pd_q[0:64, hq * 128:(hq + 1) * 128], lhsT=Win.t[:, dc, 1544 + hq * 64:1544 + (hq + 1) * 64],
                                rhs=xT.t[:, dc, :], start=(dc == 0), stop=(dc == 7)),
                                 reads=[Win.b, xT.b], writes=[pbs_q[hq // 4]])
                    S.op("dve", lambda e: e.tensor_copy(out=aqT.t[0:64, 0:4, :], in_=pd_q[0:64, 0:512].rearrange("p (h t) -> p h t", h=4)),
                         reads=[pbs_q[0]], writes=[aqT.b])
                    S.op("dve", lambda e: e.tensor_copy(out=aqT.t[0:64, 4:8, :],
                                                        in_=pd_q[0:64, 512:1024].rearrange("p (h t) -> p h t", h=4)),
                         reads=[pbs_q[1]], writes=[aqT.b])
                    mixb = S1
                    for h in range(4):
                        c, hh = h // 2, h % 2
                        p0, p1 = hh * 64, hh * 64 + 64
                        ps_d, pb_d = bank()
                        S.op("pe", lambda e, h=h, ps_d=ps_d: e.matmul(ps_d[:, 0:128], lhsT=LFB.t[:, h, :], rhs=Umat.t[:],
                                                                      start=True, stop=False),
                             reads=[LFB.b, Umat.b], writes=[pb_d])
                        S.op("pe", lambda e, ps_d=ps_d: e.matmul(ps_d[:, 0:128], lhsT=ident.t[:], rhs=mcur.t[:], start=False, stop=True),
                             reads=[ident.b, mcur.b], writes=[pb_d])
                        S.op("pe", lambda e, ps_d=ps_d, c=c, p0=p0, p1=p1: e.matmul(
                            ps_d[:, 128:256], lhsT=qkT.t[p0:p1, 2 + c, :], rhs=qkT.t[p0:p1, c, :], start=True, stop=True),
                             reads=[qkT.b], writes=[pb_d])
                        S.op("act", lambda e, ps_d=ps_d, h=h: e.activation(out=DT.t[:], in_=ps_d[:, 0:128], func=AF.Exp,
                                                                           bias=gs.t[:, 12 + h:13 + h]),
                             reads=[pb_d, gs.b], writes=[DT.b])
                        S.op("dve", lambda e, ps_d=ps_d: e.tensor_tensor(out=ST.t[:], in0=ps_d[:, 128:256], in1=DT.t[:], op=ALU.mult),
                             reads=[pb_d, DT.b], writes=[ST.b])
                        ps_n, pb_n = bank()
                        S.op("pe", lambda e, ps_n=ps_n, h=h: e.matmul(ps_n[:, 0:129], lhsT=ST.t[:], rhs=vext.t[:, h, :],
                                                                      start=True, stop=True), reads=[ST.b, vext.b], writes=[pb_n])
                        S.op("pe", lambda e, ps_n=ps_n, c=c, p0=p0, p1=p1: e.matmul(
                            ps_n[:, 256:385], lhsT=qkT.t[p0:p1, c, :], rhs=CT.t[p0:p1, c, :], start=True, stop=True),
                             reads=[qkT.b, CT.b], writes=[pb_n])
                        S.op("act", lambda e, ps_n=ps_n, h=h: e.activation(out=its.t[:], in_=ps_n[:, 256:385], func=AF.Copy,
                                                                           scale=gs.t[:, 8 + h:9 + h]),
                             reads=[pb_n, gs.b], writes=[its.b])
                        S.op("dve", lambda e, ps_n=ps_n: e.tensor_tensor(out=num.t[:], in0=ps_n[:, 0:129], in1=its.t[:], op=ALU.add),
                             reads=[pb_n, its.b], writes=[num.b])
                        S.op("act", lambda e: e.activation(out=sm.t[:, 0:1], in_=num.t[:, 128:129], func=AF.Abs),
                             reads=[num.b], writes=[sm.b])
                        S.op("dve", lambda e: e.tensor_scalar(out=sm.t[:, 0:1], in0=sm.t[:, 0:1], scalar1=1.0, scalar2=None, op0=ALU.max),
                             reads=[sm.b], writes=[sm.b])
                        S.op("dve", lambda e: e.reciprocal(out=sm.t[:, 1:2], in_=sm.t[:, 0:1]), reads=[sm.b], writes=[sm.b])
                        zero(sm, sm.t[:, 2:3])
                        S.op("act", lambda e: e.activation(out=its.t[:, 0:128], in_=num.t[:, 0:128], func=AF.Square,
                                                           scale=sm.t[:, 1:2], accum_out=sm.t[:, 2:3]),
                             reads=[num.b, sm.b], writes=[its.b, sm.b])
                        S.op("act", lambda e: e.activation(out=sm.t[:, 3:4], in_=sm.t[:, 2:3], func=AF.Sqrt, bias=EPS, scale=1.0 / 128),
                             reads=[sm.b], writes=[sm.b])
                        S.op("dve", lambda e: e.reciprocal(out=sm.t[:, 4:5], in_=sm.t[:, 3:4]), reads=[sm.b], writes=[sm.b])
                        S.op("dve", lambda e: e.tensor_tensor(out=sm.t[:, 5:6], in0=sm.t[:, 4:5], in1=sm.t[:, 1:2], op=ALU.mult),
                             reads=[sm.b], writes=[sm.b])
                        S.op("dve", lambda e, h=h: e.scalar_tensor_tensor(
                            out=mixb.t[:, h * 128:(h + 1) * 128], in0=num.t[:, 0:128], scalar=sm.t[:, 5:6],
                            in1=gsig.t[:, h * 128:(h + 1) * 128], op0=ALU.mult, op1=ALU.mult),
                             reads=[num.b, sm.b, gsig.b], writes=[mixb.b])
            if part in ("ALL", "U", "L"):
                for c in range(2):
                    ps_k, pb_k = bank()
                    S.op("pe", lambda e, ps_k=ps_k, c=c: e.transpose(out=ps_k[:, 0:128], in_=qkT.t[:, 2 + c, :], identity=ident.t[:]),
                         reads=[qkT.b, ident.b], writes=[pb_k])
                    S.op("dve", lambda e, ps_k=ps_k, c=c: e.tensor_tensor(
                        out=kw.t[:, :].rearrange("p (a k) -> p a k", a=2), in0=ps_k[:, 0:128].rearrange("p (a k) -> p a k", a=2),
                        in1=gs.t[:, 16 + 2 * c:18 + 2 * c].unsqueeze(2).to_broadcast([128, 2, 64]), op=ALU.mult),
                         reads=[pb_k, gs.b], writes=[kw.b])
                    S.op("pe", lambda e, ps_k=ps_k, c=c: e.matmul(
                        ps_k[:, 128:386], lhsT=kw.t[:], rhs=vext.t[:, 2 * c:2 * c + 2, :].rearrange("p a b -> p (a b)"),
                        start=True, stop=True), reads=[kw.b, vext.b], writes=[pb_k])
                    for hh in range(2):
                        p0, p1 = hh * 64, hh * 64 + 64
                        h = 2 * c + hh
                        S.op("dve", lambda e, ps_k=ps_k, c=c, hh=hh, p0=p0, p1=p1, h=h: e.scalar_tensor_tensor(
                            out=CT.t[p0:p1, c, :], in0=CT.t[p0:p1, c, :], scalar=gs.t[p0:p1, 20 + h:21 + h],
                            in1=ps_k[p0:p1, 128 + hh * 129:128 + (hh + 1) * 129], op0=ALU.mult, op1=ALU.add),
                             reads=[pb_k, gs.b, CT.b], writes=[CT.b])
            if part != "ALL":
                return
            if light:
                return
            for kv in range(2):
                ps_o, pb_o = bank()
                pts = (PT, PT2)
                srcs = ((1 - slot, mprev0 if n == 0 else mprev), (slot, mcur))
                for wi, (sl, mk) in enumerate(srcs):
                    ps_s, pb_s = bank()
                    S.op("pe", lambda e, ps_s=ps_s, sl=sl, kv=kv: e.matmul(
                        ps_s[:, :], lhsT=kTs.t[0:64, sl, kv, :], rhs=aqT.t[0:64, kv * 4:kv * 4 + 4, :].rearrange("p h t -> p (h t)"),
                        start=True, stop=False), reads=[kTs.b, aqT.b], writes=[pb_s])
                    for g in range(4):
                        S.op("pe", lambda e, ps_s=ps_s, g=g, mk=mk: e.matmul(
                            ps_s[:, g * 128:(g + 1) * 128], lhsT=ident.t[:], rhs=mk.t[:], start=False, stop=(g == 3)),
                             reads=[ident.b, mk.b], writes=[pb_s])
                    S.op("act", lambda e, ps_s=ps_s, wi=wi: e.activation(out=pts[wi].t[:], in_=ps_s[:, :], func=AF.Exp, scale=0.125),
                         reads=[pb_s], writes=[pts[wi].b])
                for g in range(4):
                    for wi, (sl, mk) in enumerate(srcs):
                        S.op("pe", lambda e, ps_o=ps_o, g=g, sl=sl, kv=kv, wi=wi: e.matmul(
                            ps_o[:, g * 65:(g + 1) * 65], lhsT=pts[wi].t[:, g * 128:(g + 1) * 128], rhs=vsw.t[:, sl, kv, :],
                            start=(wi == 0), stop=(wi == 1)), reads=[pts[wi].b, vsw.b], writes=[pb_o])
                o3 = ps_o[:, 0:260].rearrange("p (g d) -> p g d", g=4)
                S.op("dve", lambda e, o3=o3, kv=kv: e.tensor_tensor(out=sm.t[:, 8:12], in0=o3[:, :, 64],
                                                                    in1=esink.t[:, kv * 4:kv * 4 + 4], op=ALU.add),
                     reads=[pb_o, esink.b], writes=[sm.b])
                S.op("dve", lambda e: e.reciprocal(out=sm.t[:, 12:16], in_=sm.t[:, 8:12]), reads=[sm.b], writes=[sm.b])
                S.op("dve", lambda e, o3=o3, kv=kv: e.tensor_tensor(
                    out=mixb.t[:, 512 + kv * 256:512 + (kv + 1) * 256].rearrange("p (g d) -> p g d", g=4),
                    in0=o3[:, :, 0:64], in1=sm.t[:, 12:16].unsqueeze(2).to_broadcast([128, 4, 64]), op=ALU.mult),
                     reads=[pb_o, sm.b], writes=[mixb.b])
            transpose8(mixb, xT)
            out_proj_add(xT, Wout, n)

        for i in range(NPRE):
            mixer_block(xp_d, i * 128, i, True, i, i == NPRE - 1, part="L")
        for n in range(NB):
            mixer_block(xs_d, n * 128, n, False, None, True)
        S.barrier()
        S.emit()

    if stop_after >= 2:
        with ExitStack() as es:
            Wq = alloc(es, "Wq", [128, 8, D])
            W2 = alloc(es, "W2", [128, 8, D])
            load_w(Wq, w_xq_d, D)
            load_w(W2, w_xk_d, D)
            gcT = load_colvec(es, "gcT", g_cross_d, 8)
            gmT = load_colvec(es, "gmT", g_mem_d, 8)
            S1 = alloc(es, "S1b", [128, D])
            xT = alloc(es, "xTb", [128, 8, 128])
            memT = alloc(es, "memT", [128, 8, 256])
            KT = alloc(es, "KT", [128, 8, 256])
            Vx = alloc(es, "Vx", [128, 2, 4, 257])
            qT = alloc(es, "qT", [128, 8, 128])
            PTb = alloc(es, "PTb", [128, 8, 128])
            oc = alloc(es, "oc", [128, D])
            MX = alloc(es, "MXb", [128, D])
            S.op("pool", lambda e: e.memset(Vx.t[:], 1.0), writes=[Vx.b])
            for mc in range(2):
                S.dma(dmaq(), lambda e, mc=mc: e.dma_start(out=MX.t[:], in_=mem_d[mc * 128:(mc + 1) * 128, :]), writes=[MX.b])
                norm_T(MX.t[:], MX.b, gmT, S1, xT)
                S.op("dve", lambda e, mc=mc: e.tensor_copy(out=memT.t[:, :, mc * 128:(mc + 1) * 128], in_=xT.t[:]),
                     reads=[xT.b], writes=[memT.b])
            for j in range(8):
                ps, pb = bank()
                for dc in range(8):
                    S.op("pe", lambda e, ps=ps, j=j, dc=dc: e.matmul(ps[:, 0:256], lhsT=W2.t[:, dc, j * 128:(j + 1) * 128],
                                                                     rhs=memT.t[:, dc, :], start=(dc == 0), stop=(dc == 7)),
                         reads=[W2.b, memT.b], writes=[pb])
                S.op("act", lambda e, ps=ps, j=j: e.activation(func=AF.Copy, out=KT.t[:, j, :], in_=ps[:, 0:256]), reads=[pb], writes=[KT.b])
            load_w(W2, w_xv_d, D)
            for mc in range(2):
                for half in range(2):
                    ps, pb = bank()
                    for dc in range(8):
                        S.op("pe", lambda e, ps=ps, mc=mc, half=half, dc=dc: e.matmul(
                            ps[:, :], lhsT=memT.t[:, dc, mc * 128:(mc + 1) * 128], rhs=W2.t[:, dc, half * 512:(half + 1) * 512],
                            start=(dc == 0), stop=(dc == 7)), reads=[W2.b, memT.b], writes=[pb])
                    S.op("dve", lambda e, ps=ps, mc=mc, half=half: e.tensor_copy(
                        out=Vx.t[:, mc, 2 * half:2 * half + 2, 0:256], in_=ps[:, :].rearrange("p (h v) -> p h v", h=2)),
                         reads=[pb], writes=[Vx.b])
            load_w(W2, w_xo_d, D)
            for n in range(NB):
                norm_T(H[:, n, :], Hb[n], gcT, S1, xT)
                pd, pbs = dbank()
                for j in range(8):
                    for dc in range(8):
                        S.op("pe", lambda e, j=j, dc=dc: e.matmul(pd[:, j * 128:(j + 1) * 128], lhsT=Wq.t[:, dc, j * 128:(j + 1) * 128],
                                                                  rhs=xT.t[:, dc, :], start=(dc == 0), stop=(dc == 7)),
                             reads=[Wq.b, xT.b], writes=[pbs[j // 4]])
                S.op("act", lambda e: e.activation(func=AF.Copy, out=qT.t[:, 0:4, :], in_=pd[:, 0:512].rearrange("p (j t) -> p j t", j=4)),
                     reads=[pbs[0]], writes=[qT.b])
                S.op("dve", lambda e: e.tensor_copy(out=qT.t[:, 4:8, :], in_=pd[:, 512:1024].rearrange("p (j t) -> p j t", j=4)),
                     reads=[pbs[1]], writes=[qT.b])
                pd2, pbs2 = dbank()
                for h in range(4):
                    for mc in range(2):
                        col = (h * 2 + mc) * 128
                        for cc in range(2):
                            S.op("pe", lambda e, h=h, mc=mc, cc=cc, col=col: e.matmul(
                                pd2[:, col:col + 128], lhsT=KT.t[:, h * 2 + cc, mc * 128:(mc + 1) * 128], rhs=qT.t[:, h * 2 + cc, :],
                                start=(cc == 0), stop=(cc == 1)), reads=[KT.b, qT.b], writes=[pbs2[h // 2]])
                for k in range(2):
                    S.op("act", lambda e, k=k: e.activation(out=PTb.t[:, 4 * k:4 * k + 4, :],
                                                            in_=pd2[:, k * 512:(k + 1) * 512].rearrange("p (j t) -> p j t", j=4),
                                                            func=AF.Exp, scale=1.0 / 16), reads=[pbs2[k]], writes=[PTb.b])
                for h in range(4):
                    ps, pb = bank()
                    for mc in range(2):
                        S.op("pe", lambda e, ps=ps, h=h, mc=mc: e.matmul(ps[:, 0:257], lhsT=PTb.t[:, h * 2 + mc, :], rhs=Vx.t[:, mc, h, :],
                                                                         start=(mc == 0), stop=(mc == 1)),
                             reads=[PTb.b, Vx.b], writes=[pb])
                    S.op("dve", lambda e, ps=ps: e.reciprocal(out=ss.t[:, 3:4], in_=ps[:, 256:257]), reads=[pb], writes=[ss.b])
                    S.op("act", lambda e, ps=ps, h=h: e.activation(out=oc.t[:, h * 256:(h + 1) * 256], in_=ps[:, 0:256], func=AF.Copy,
                                                                   scale=ss.t[:, 3:4]), reads=[pb, ss.b], writes=[oc.b])
                transpose8(oc, xT)
                out_proj_add(xT, W2, n)
            S.barrier()
            S.emit()

    with ExitStack() as esC:
        S1 = alloc(esC, "S1c", [128, D])
        if stop_after >= 3:
            idxTa = alloc(esC, "idxTa", [128, NB, 128], U32)
            gTa = alloc(esC, "gTa", [128, NB, 128])
        if stop_after >= 3:
          with ExitStack() as es:
              Wpq = alloc(es, "Wpq", [128, 8, 2048])
              load_w(Wpq, w_pq_d, 2048)
              gfT = load_colvec(es, "gfT", g_ffn_d, 8)
              skT = alloc(es, "skT", [128, 2, 128])
              skr = alloc(es, "skr", [128, 128])
              for i, skd in enumerate((sk1_d, sk2_d)):
                  S.dma(dmaq(), lambda e, skd=skd: e.dma_start(out=skr.t[:], in_=skd), writes=[skr.b])
                  ps, pb = bank()
                  S.op("pe", lambda e, ps=ps: e.transpose(out=ps[:, 0:128], in_=skr.t[:], identity=ident.t[:]),
                       reads=[skr.b, ident.b], writes=[pb])
                  S.op("dve", lambda e, ps=ps, i=i: e.tensor_copy(out=skT.t[:, i, :], in_=ps[:, 0:128]), reads=[pb], writes=[skT.b])
              xT = alloc(es, "xTc", [128, 8, 128])
              T8a = alloc(es, "T8a", [128, 2048])
              T8bs = [alloc(es, "T8b%d" % i, [128, 2048]) for i in range(2)]
              Q8 = alloc(es, "Q8", [128, 2048])
              T8c = alloc(es, "T8c", [128, 2048])
              v16 = alloc(es, "v16", [128, 16, 16])
              i16u = alloc(es, "i16u", [128, 16, 16], U32)
              i16 = alloc(es, "i16", [128, 16, 16])
              tsv = alloc(es, "tsv", [128, 8, 16])
              posu = alloc(es, "posu", [128, 8, 16], U32)
              pau = alloc(es, "pau", [128, 8, 16], U32)
              pbu = alloc(es, "pbu", [128, 8, 16], U32)
              pa = alloc(es, "pa", [128, 8, 16])
              pbq = alloc(es, "pbq", [128, 8, 16])
              eid = alloc(es, "eid", [128, 8, 16])
              eid2 = alloc(es, "eid2", [128, 8, 16])
              gg = alloc(es, "gg", [128, 8, 16])
              gsum = alloc(es, "gsum", [128, 8])
              iota16 = alloc(es, "iota16", [128, 16])
              Aacc = alloc(es, "Aacc", [128, 128])
              S.op("pool", lambda e: e.iota(iota16.t[:], pattern=[[1, 16]], base=0, channel_multiplier=0,
                                            allow_small_or_imprecise_dtypes=True), writes=[iota16.b])

              def top16_multi(groups, src_b, work_b, out_bs):
                  gbs = [Buf("g%d" % i) for i in range(len(groups))]
                  for st in range(5):
                      for gi, (src_ap, work, vout, iout) in enumerate(groups):
                          gb = gbs[gi]
                          extra = ([work_b] + out_bs) if (st == 0 and gi == 0) else []
                          if st == 0:
                              S.op("dve", lambda e: e.max(out=vout[:, 0:8], in_=src_ap), reads=[src_b], writes=[gb] + extra)
                          elif st == 1:
                              S.op("dve", lambda e: e.max_index(out=iout[:, 0:8], in_max=vout[:, 0:8], in_values=src_ap),
                                   reads=[src_b, gb], writes=[gb])
                          elif st == 2:
                              S.op("dve", lambda e: e.match_replace(out=work, in_to_replace=vout[:, 0:8], in_values=src_ap,
                                                                    imm_value=-1e30), reads=[src_b, gb], writes=[gb])
                          elif st == 3:
                              S.op("dve", lambda e: e.max(out=vout[:, 8:16], in_=work), reads=[gb], writes=[gb])
                          else:
                              S.op("dve", lambda e: e.max_index(out=iout[:, 8:16], in_max=vout[:, 8:16], in_values=work),
                                   reads=[gb], writes=[gb])
                  return gbs

              def c1_front(n):
                  T8b = T8bs[n % 2]
                  norm_T(H[:, n, :], Hb[n], gfT, S1, xT)
                  for q4 in range(4):
                      ps, pb = bank()
                      for jj in range(4):
                          j = q4 * 4 + jj
                          for dc in range(8):
                              S.op("pe", lambda e, ps=ps, jj=jj, j=j, dc=dc: e.matmul(
                                  ps[:, jj * 128:(jj + 1) * 128], lhsT=Wpq.t[:, dc, j * 128:(j + 1) * 128], rhs=xT.t[:, dc, :],
                                  start=(dc == 0), stop=(dc == 7)), reads=[Wpq.b, xT.b], writes=[pb])
                      S.op("act", lambda e, ps=ps, q4=q4: e.activation(func=AF.Copy, out=Q8.t[:, q4 * 512:(q4 + 1) * 512], in_=ps[:, :]),
                           reads=[pb], writes=[Q8.b])
                  for q4 in range(4):
                      ps, pb = bank()
                      for jj in range(4):
                          j = q4 * 4 + jj
                          S.op("pe", lambda e, ps=ps, jj=jj, j=j: e.matmul(
                              ps[:, jj * 128:(jj + 1) * 128], lhsT=Q8.t[:, j * 128:(j + 1) * 128], rhs=skT.t[:, j % 2, :],
                              start=True, stop=True), reads=[Q8.b, skT.b], writes=[pb])
                      S.op("act", lambda e, ps=ps, q4=q4: e.activation(func=AF.Copy, out=T8b.t[:, q4 * 512:(q4 + 1) * 512], in_=ps[:, :]),
                           reads=[pb], writes=[T8b.b])
              def c1_back(n):
                  T8b = T8bs[n % 2]
                  g1 = top16_multi([(T8b.t[:, j * 128:(j + 1) * 128], T8c.t[:, j * 128:(j + 1) * 128], v16.t[:, j, :], i16u.t[:, j, :])
                                    for j in range(16)], T8b.b, T8c.b, [v16.b, i16u.b])
                  S.op("dve", lambda e: e.tensor_copy(out=i16.t[:], in_=i16u.t[:]), reads=[i16u.b] + g1, writes=[i16.b])
                  v4 = v16.t[:, :, :].rearrange("p (h two) k -> p h two k", two=2)
                  cs4 = T8c.t[:, :].rearrange("p (h a b) -> p h a b", h=8, a=16)
                  S.op("dve", lambda e: e.tensor_tensor(out=cs4, in0=v4[:, :, 0, :].unsqueeze(3).to_broadcast([128, 8, 16, 16]),
                                                        in1=v4[:, :, 1, :].unsqueeze(2).to_broadcast([128, 8, 16, 16]), op=ALU.add),
                       reads=[v16.b] + g1, writes=[T8c.b])
                  g2 = top16_multi([(T8c.t[:, h * 256:(h + 1) * 256], T8a.t[:, h * 256:(h + 1) * 256], tsv.t[:, h, :], posu.t[:, h, :])
                                    for h in range(8)], T8c.b, T8a.b, [tsv.b, posu.b])
                  S.op("dve", lambda e: e.tensor_single_scalar(out=pau.t[:], in_=posu.t[:], scalar=4, op=ALU.logical_shift_right),
                       reads=[posu.b] + g2, writes=[pau.b])
                  S.op("dve", lambda e: e.tensor_single_scalar(out=pbu.t[:], in_=posu.t[:], scalar=15, op=ALU.bitwise_and),
                       reads=[posu.b] + g2, writes=[pbu.b])
                  S.op("dve", lambda e: e.tensor_copy(out=pa.t[:], in_=pau.t[:]), reads=[pau.b], writes=[pa.b])
                  S.op("dve", lambda e: e.tensor_copy(out=pbq.t[:], in_=pbu.t[:]), reads=[pbu.b], writes=[pbq.b])
                  i4 = i16.t[:, :, :].rearrange("p (h two) k -> p h two k", two=2)
                  oh = T8a.t[:, :].rearrange("p (h j a) -> p h j a", h=8, j=16)
                  io4 = iota16.t[:, :].unsqueeze(1).unsqueeze(1).to_broadcast([128, 8, 16, 16])
                  for which, (sel_ap, dst) in enumerate(((pa, eid), (pbq, eid2))):
                      S.op("dve", lambda e, sel_ap=sel_ap: e.tensor_tensor(
                          out=oh, in0=sel_ap.t[:, :, :].unsqueeze(3).to_broadcast([128, 8, 16, 16]), in1=io4, op=ALU.is_equal),
                           reads=[sel_ap.b, iota16.b], writes=[T8a.b])
                      S.op("dve", lambda e, which=which: e.tensor_tensor(
                          out=oh, in0=oh, in1=i4[:, :, which, :].unsqueeze(2).to_broadcast([128, 8, 16, 16]), op=ALU.mult),
                           reads=[T8a.b, i16.b], writes=[T8a.b])
                      S.op("dve", lambda e, dst=dst: e.tensor_reduce(out=dst.t[:], in_=oh, axis=AX.X, op=ALU.add),
                           reads=[T8a.b], writes=[dst.b])
                  S.op("dve", lambda e: e.scalar_tensor_tensor(out=eid.t[:], in0=eid.t[:], scalar=128.0, in1=eid2.t[:],
                                                               op0=ALU.mult, op1=ALU.add), reads=[eid.b, eid2.b], writes=[eid.b])
                  S.op("dve", lambda e: e.tensor_tensor(out=gg.t[:], in0=tsv.t[:], in1=tsv.t[:, :, 0:1].to_broadcast([128, 8, 16]),
                                                        op=ALU.subtract), reads=[tsv.b] + g2, writes=[gg.b])
                  S.op("act", lambda e: e.activation(out=gg.t[:], in_=gg.t[:], func=AF.Exp), reads=[gg.b], writes=[gg.b])
                  S.op("dve", lambda e: e.tensor_reduce(out=gsum.t[:], in_=gg.t[:], axis=AX.X, op=ALU.add),
                       reads=[gg.b], writes=[gsum.b])
                  S.op("dve", lambda e: e.reciprocal(out=gsum.t[:], in_=gsum.t[:]), reads=[gsum.b], writes=[gsum.b])
                  S.op("dve", lambda e: e.tensor_tensor(out=gg.t[:], in0=gg.t[:], in1=gsum.t[:, :].unsqueeze(2).to_broadcast([128, 8, 16]),
                                                        op=ALU.mult), reads=[gg.b, gsum.b], writes=[gg.b])
                  ps, pb = bank()
                  S.op("pe", lambda e, ps=ps: e.transpose(out=ps[:, 0:128], in_=eid.t[:, :, :].rearrange("p h k -> p (h k)"),
                                                          identity=ident.t[:]), reads=[eid.b, ident.b], writes=[pb])
                  S.op("pe", lambda e, ps=ps: e.transpose(out=ps[:, 128:256], in_=gg.t[:, :, :].rearrange("p h k -> p (h k)"),
                                                          identity=ident.t[:]), reads=[gg.b, ident.b], writes=[pb])
                  S.op("dve", lambda e, ps=ps: e.tensor_scalar(out=Aacc.t[:], in0=ps[:, 0:128], scalar1=8388608.0, scalar2=None,
                                                               op0=ALU.add), reads=[pb], writes=[Aacc.b])
                  S.op("dve", lambda e: e.tensor_single_scalar(out=idxTa.t[:, n, :], in_=Aacc.t[:].bitcast(U32), scalar=0x7FFFFF,
                                                               op=ALU.bitwise_and), reads=[Aacc.b], writes=[idxTa.b])
                  S.op("act", lambda e, ps=ps: e.activation(func=AF.Copy, out=gTa.t[:, n, :], in_=ps[:, 128:256]), reads=[pb], writes=[gTa.b])

              c1_front(0)
              for n in range(NB):
                  if n + 1 < NB:
                      c1_front(n + 1)
                  c1_back(n)
              S.barrier()
              S.emit()
        if stop_after >= 3:
          with ExitStack() as es:
            gfrow = load_bcast(es, "gfrow", g_ffn_row_d, D)
            xfs = [alloc(es, "xf%d" % i, [128, D]) for i in range(2)]
            Aacc = alloc(es, "Aacc2", [128, 128])
            AA = alloc(es, "AA", [128, 2, 128])
            Lms = [alloc(es, "Lm%d" % i, [128, 128]) for i in range(3)]
            zcol = alloc(es, "zcol", [128, 1])
            sel = [alloc(es, "sel%d" % i, [128, 128], BF16) for i in range(4)]
            xps = [[alloc(es, "xp%d_%d" % (i, k), [128, D], BF16) for k in range(XB_PASSES)] for i in range(2)]
            NG = 8
            Gd = [alloc(es, "Gd%d" % i, [128, D]) for i in range(NG)]
            Gu = [alloc(es, "Gu%d" % i, [128, D]) for i in range(NG)]
            for lm_ in Lms:
                S.op("pool", lambda e: e.memset(lm_.t[:], 0.0), writes=[lm_.b])
            S.op("pool", lambda e: e.memset(zcol.t[:], 0.0), writes=[zcol.b])
            tokc = [0]
            for n in range(NB):
                xfn = xfs[n % 2]
                zero(ss, ss.t[:, 0:1])
                S.op("act", lambda e: e.activation(out=S1.t[:], in_=H[:, n, :], func=AF.Square, accum_out=ss.t[:, 0:1]),
                     reads=[Hb[n], ss.b], writes=[S1.b, ss.b])
                S.op("act", lambda e: e.activation(out=ss.t[:, 1:2], in_=ss.t[:, 0:1], func=AF.Sqrt, bias=EPS, scale=1.0 / D),
                     reads=[ss.b], writes=[ss.b])
                S.op("dve", lambda e: e.reciprocal(out=ss.t[:, 2:3], in_=ss.t[:, 1:2]), reads=[ss.b], writes=[ss.b])
                S.op("dve", lambda e: e.scalar_tensor_tensor(out=xfn.t[:], in0=H[:, n, :], scalar=ss.t[:, 2:3], in1=gfrow.t[:],
                                                             op0=ALU.mult, op1=ALU.mult),
                     reads=[Hb[n], ss.b, gfrow.b], writes=[xfn.b])
                S.op("pool", lambda e: e.memset(AA.t[:], 0.0), writes=[AA.b])
                xpn = xps[n % 2]
                for k in range(XB_PASSES):
                    src = xfn if k == 0 else S1
                    S.op("dve", lambda e: e.tensor_copy(out=xpn[k].t[:], in_=src.t[:]), reads=[src.b], writes=[xpn[k].b])
                    if k + 1 < XB_PASSES:
                        S.op("dve", lambda e: e.tensor_tensor(out=S1.t[:], in0=src.t[:], in1=xpn[k].t[:], op=ALU.subtract),
                             reads=[src.b, xpn[k].b], writes=[S1.b])
                pd_o, pbs_o = dbank()
                reserved[0] = last_db[0]
                ntok = n_tok_peer
                LA = 2
                fr = {}

                def front(t):
                    k = tokc[0] % NG
                    tokc[0] += 1
                    gd, gu, sl = Gd[k], Gu[k], sel[k % len(sel)]
                    S.dma("pool", lambda e: e.indirect_dma_start(
                        out=gd.t[:, :], out_offset=None, in_=ed_d[:, :],
                        in_offset=bass.IndirectOffsetOnAxis(ap=idxTa.t[:, n, t:t + 1].bitcast(I32), axis=0)), reads=[idxTa.b], writes=[gd.b])
                    S.dma("pool", lambda e: e.indirect_dma_start(
                        out=gu.t[:, :], out_offset=None, in_=eu_d[:, :],
                        in_offset=bass.IndirectOffsetOnAxis(ap=idxTa.t[:, n, t:t + 1].bitcast(I32), axis=0)), reads=[idxTa.b], writes=[gu.b])
                    S.op("act", lambda e: e.activation(func=AF.Copy, out=sl.t[:], in_=ident.t[:, t:t + 1].to_broadcast([128, 128])),
                         reads=[ident.b], writes=[sl.b])
                    pd_x, pbs_x = dbank()
                    for half in range(2):
                        for k in range(XB_PASSES):
                            S.op("pe", lambda e: e.matmul(
                                pd_x[:, half * 512:(half + 1) * 512], lhsT=sl.t[:], rhs=xpn[k].t[:, half * 512:(half + 1) * 512],
                                start=(k == 0), stop=(k == XB_PASSES - 1)), reads=[sl.b, xpn[k].b], writes=[pbs_x[half]])
                    fr[t] = (gd, gu, pd_x, pbs_x)

                def back(t):
                    gd, gu, pd_x, pbs_x = fr.pop(t)
                    lm = Lms[t % 3]
                    for half in range(2):
                        S.op("dve", lambda e: e.scalar_tensor_tensor(
                            out=S1.t[:, half * 512:(half + 1) * 512], in0=gd.t[:, half * 512:(half + 1) * 512], scalar=1.0,
                            in1=pd_x[:, half * 512:(half + 1) * 512], op0=ALU.mult, op1=ALU.mult,
                            accum_out=AA.t[:, half, t:t + 1]),
                             reads=[gd.b, pbs_x[half], AA.b], writes=[S1.b, AA.b])
                    S.op("dve", lambda e: e.tensor_tensor(out=Aacc.t[:, t:t + 1], in0=AA.t[:, 0, t:t + 1], in1=AA.t[:, 1, t:t + 1],
                                                          op=ALU.add), reads=[AA.b], writes=[Aacc.b])
                    S.op("act", lambda e: e.activation(out=Aacc.t[:, t:t + 1], in_=Aacc.t[:, t:t + 1], func=AF.Gelu),
                         reads=[Aacc.b], writes=[Aacc.b])
                    S.op("dve", lambda e: e.tensor_tensor(out=lm.t[:, t:t + 1], in0=Aacc.t[:, t:t + 1], in1=gTa.t[:, n, t:t + 1],
                                                          op=ALU.mult), reads=[Aacc.b, gTa.b], writes=[lm.b])
                    for half in range(2):
                        S.op("pe", lambda e: e.matmul(
                            pd_o[:, half * 512:(half + 1) * 512], lhsT=lm.t[:], rhs=gu.t[:, half * 512:(half + 1) * 512],
                            start=(t == 0), stop=(t == ntok - 1)), reads=[lm.b, gu.b], writes=[pbs_o[half]])
                    if t >= 2:
                        lmo = Lms[(t - 2) % 3]
                        S.op("act", lambda e: e.activation(func=AF.Copy, out=lmo.t[:, t - 2:t - 1], in_=zcol.t[:, 0:1]),
                             reads=[zcol.b], writes=[lmo.b])

                for t in range(min(LA, ntok)):
                    front(t)
                for t in range(ntok):
                    if t + LA < ntok:
                        front(t + LA)
                    back(t)
                for t in range(max(0, ntok - 2), ntok):
                    lmo = Lms[t % 3]
                    S.op("act", lambda e: e.activation(func=AF.Copy, out=lmo.t[:, t:t + 1], in_=zcol.t[:, 0:1]),
                         reads=[zcol.b], writes=[lmo.b])
                reserved[0] = None
                for half in range(2):
                    S.op("dve", lambda e, half=half: e.tensor_tensor(
                        out=H[:, n, half * 512:(half + 1) * 512], in0=H[:, n, half * 512:(half + 1) * 512],
                        in1=pd_o[:, half * 512:(half + 1) * 512], op=ALU.add), reads=[pbs_o[half], Hb[n]], writes=[Hb[n]])
            S.barrier()
            S.emit()
        gfin = load_bcast(esC, "gfin", g_final_d, D)
        for n in range(NB):
            zero(ss, ss.t[:, 0:1])
            S.op("act", lambda e, n=n: e.activation(out=S1.t[:], in_=H[:, n, :], func=AF.Square, accum_out=ss.t[:, 0:1]),
                 reads=[Hb[n], ss.b], writes=[S1.b, ss.b])
            S.op("act", lambda e: e.activation(out=ss.t[:, 1:2], in_=ss.t[:, 0:1], func=AF.Sqrt, bias=EPS, scale=1.0 / D),
                 reads=[ss.b], writes=[ss.b])
            S.op("dve", lambda e: e.reciprocal(out=ss.t[:, 2:3], in_=ss.t[:, 1:2]), reads=[ss.b], writes=[ss.b])
            S.op("dve", lambda e, n=n: e.scalar_tensor_tensor(out=S1.t[:], in0=H[:, n, :], scalar=ss.t[:, 2:3], in1=gfin.t[:],
                                                              op0=ALU.mult, op1=ALU.mult),
                 reads=[Hb[n], ss.b, gfin.b], writes=[S1.b])
            S.dma("sp", lambda e, n=n: e.dma_start(out=out_d[n * 128:(n + 1) * 128, :], in_=S1.t[:]), reads=[S1.b],
                  writes=[Buf("o")])
        S.wait_all_dma("sp")
        S.barrier()
        S.emit()
    S.close()
    es_all.close()
    return nc


def make_in_maps(inp, NB):
    x = np.ascontiguousarray(inp["x"], dtype=np.float32)
    B, SEQ, _ = x.shape
    seg = NB * 128
    assert SEQ == NSEG * seg
    NPRE = (NSEG - 1) * NB

    def f(a):
        return np.ascontiguousarray(np.asarray(a, dtype=np.float32))

    shared = {
        "g_mix": f(inp["g_mix"][0]).reshape(8, 128),
        "w_in": f(inp["w_in"][0]),
        "conv_w": f(inp["conv_w"][0]),
        "conv_b": f(inp["conv_b"][0]).reshape(4, 128),
        "gateb": np.concatenate([f(inp["b_igate"][0]), f(inp["b_fgate"][0])]).reshape(1, 8),
        "g_mhead": f(inp["g_mhead"][0]).reshape(1, 512),
        "sinks": f(inp["sinks"][0]).reshape(1, 8),
        "w_out": f(inp["w_out"][0]),
        "g_cross": f(inp["g_cross"][0]).reshape(8, 128),
        "g_mem": f(inp["g_mem"][0]).reshape(8, 128),
        "w_xq": f(inp["w_xq"][0]), "w_xk": f(inp["w_xk"][0]), "w_xv": f(inp["w_xv"][0]), "w_xo": f(inp["w_xo"][0]),
        "g_ffn": f(inp["g_ffn"][0]).reshape(8, 128),
        "g_ffn_row": f(inp["g_ffn"][0]).reshape(1, D),
        "w_pq": f(inp["w_pq"][0]),
        "sub_keys1": f(inp["sub_keys1"][0]), "sub_keys2": f(inp["sub_keys2"][0]),
        "expert_down": f(inp["expert_down"][0]), "expert_up": f(inp["expert_up"][0]),
        "g_final": f(inp["g_final"]).reshape(1, D),
    }
    mem = f(inp["mem"])
    maps = []
    for c in range(B * NSEG):
        b, j = divmod(c, NSEG)
        xp = np.zeros((NPRE * 128, D), np.float32)
        pm = np.zeros((1, NPRE), np.float32)
        if j > 0:
            xp[NPRE * 128 - j * seg:] = x[b, :j * seg]
            pm[0, NPRE - j * NB:] = 1.0
        m = dict(shared)
        m["xs"] = np.ascontiguousarray(x[b, j * seg:(j + 1) * seg])
        m["xp"] = xp
        m["pm"] = pm
        m["mem"] = np.ascontiguousarray(mem[b])
        maps.append(m)
    return maps


_NC_CACHE = {}


def kernel(**inputs):
    x = np.asarray(inputs["x"])
    B, SEQ, _ = x.shape
    NB = SEQ // (NSEG * 128)
    key = (NB,)
    if key not in _NC_CACHE:
        _NC_CACHE[key] = build(NB)
    nc = _NC_CACHE[key]
    maps = make_in_maps(inputs, NB)
    res = run_bass_kernel_spmd(nc, maps, core_ids=list(range(B * NSEG)))
    out = np.zeros((B, SEQ, D), np.float32)
    seg = NB * 128
    for c in range(B * NSEG):
        b, j = divmod(c, NSEG)
        out[b, j * seg:(j + 1) * seg] = res.results[c]["out"]
    return out
```
